# Optimizing a Trainium2 kernel written in Bass

```python
import math
import jax, jax.numpy as jnp
from jax import lax
import numpy as np

D_MODEL = 1024
BATCH = 4
SEQ = 8192
DEPTH = 4
DEC_BATCH = 8
DEC_SEQ = 32
PAST_LEN = 2048

CHUNK = 64
N_A = DEPTH // 2
N_B = DEPTH - N_A
ALPHA = (2.0 * DEPTH) ** 0.25
BETA = (8.0 * DEPTH) ** -0.25
LN_EPS = 1e-5
RMS_EPS = 1e-6

GM_CHUNK = 128
GM_HALF = 2 * D_MODEL
GM_GROUPS = 8
GM_GROUP_DIM = GM_HALF // GM_GROUPS

MLA_HEADS = 8
QK_NOPE = 128
QK_ROPE = 64
V_HEAD = 128
KV_LORA = D_MODEL // 4
Q_LORA = 3 * D_MODEL // 8
ROPE_BASE = 10000.0
Q_BLOCK = 128
ATTN_SCALE = (QK_NOPE + QK_ROPE) ** -0.5

PEER_HEADS = 8
PEER_NKEYS = 128
PEER_EXPERTS = PEER_NKEYS * PEER_NKEYS
PEER_DK = 256
PEER_TOPK = 16
PEER_BLOCK = 256

kernel_name = "yoco_gmlp_mla_peer_stream_step"


def layer_norm(x, g, b):
    xf = x.astype(jnp.float32)
    mu = jnp.mean(xf, axis=-1, keepdims=True)
    var = jnp.mean(jnp.square(xf - mu), axis=-1, keepdims=True)
    return ((xf - mu) * lax.rsqrt(var + LN_EPS)).astype(x.dtype) * g + b


def rms_norm(x, g):
    xf = x.astype(jnp.float32)
    return (xf * lax.rsqrt(jnp.mean(xf * xf, axis=-1, keepdims=True) + RMS_EPS)).astype(x.dtype) * g


def rope_tables(pos, dtype):
    inv = 1.0 / (ROPE_BASE ** (jnp.arange(0, QK_ROPE, 2, dtype=jnp.float32) / QK_ROPE))
    ang = pos.astype(jnp.float32)[:, None] * inv[None, :]
    return jnp.cos(ang).astype(dtype), jnp.sin(ang).astype(dtype)


def apply_rope(x, cos, sin):
    x1, x2 = jnp.split(x, 2, axis=-1)
    return jnp.concatenate([x1 * cos - x2 * sin, x1 * sin + x2 * cos], axis=-1)


def gm_mask():
    i = jnp.arange(GM_CHUNK)
    return (i[None, :] // CHUNK) <= (i[:, None] // CHUNK)


def gmlp_mixer(x, w_in, b_in, ln_g, ln_b, w_s, b_s, w_out):
    B, L, _ = x.shape
    z = jax.nn.gelu(x @ w_in + b_in, approximate=False)
    u, v = z[..., :GM_HALF], z[..., GM_HALF:]
    v = layer_norm(v, ln_g, ln_b)
    n = min(L, GM_CHUNK)
    w = jnp.where(gm_mask()[None], w_s, jnp.zeros_like(w_s))[:, :n, :n]
    vr = v.reshape(B, L // n, n, GM_GROUPS, GM_GROUP_DIM)
    sv = jnp.einsum('gij,bnjgc->bnigc', w, vr) + jnp.transpose(b_s[:, :n])[None, None, :, :, None]
    s = u * sv.reshape(B, L, GM_HALF)
    return s @ w_out, v


def mla_shared_kv(h, pos, w_dkv, kv_g):
    kv = h @ w_dkv
    c = rms_norm(kv[..., :KV_LORA], kv_g)
    cos, sin = rope_tables(pos, h.dtype)
    kpe = apply_rope(kv[..., KV_LORA:], cos[None], sin[None])
    return c, kpe


def mla_expand(c, w_ukv):
    B, K, _ = c.shape
    kv = (c @ w_ukv).reshape(B, K, MLA_HEADS, QK_NOPE + V_HEAD)
    return kv[..., :QK_NOPE], kv[..., QK_NOPE:]


def attend(qn, qp, q_pos, kn, kp, v, k_pos):
    s = (jnp.einsum('bqhd,bkhd->bhqk', qn, kn) + jnp.einsum('bqhr,bkr->bhqk', qp, kp)).astype(jnp.float32) * ATTN_SCALE
    mask = (k_pos[None, :] // CHUNK) <= (q_pos[:, None] // CHUNK)
    s = jnp.where(mask[None, None], s, -jnp.inf)
    p = jax.nn.softmax(s, axis=-1).astype(v.dtype)
    return jnp.einsum('bhqk,bkhd->bqhd', p, v)


def mla_layer(x, q_pos, k_nope, k_pe, v, k_pos, w_dq, q_g, w_uq, w_o):
    B, L, _ = x.shape
    cq = rms_norm(x @ w_dq, q_g)
    q = (cq @ w_uq).reshape(B, L, MLA_HEADS, QK_NOPE + QK_ROPE)
    cos, sin = rope_tables(q_pos, x.dtype)
    q_nope = q[..., :QK_NOPE]
    q_pe = apply_rope(q[..., QK_NOPE:], cos[None, :, None], sin[None, :, None])
    qb = min(L, Q_BLOCK)
    nb = L // qb

    def blocks(t):
        return jnp.moveaxis(t.reshape((B, nb, qb) + t.shape[2:]), 1, 0)

    def one_block(args):
        qn, qp, qpos = args
        return attend(qn, qp, qpos, k_nope, k_pe, v, k_pos)

    o = lax.map(one_block, (blocks(q_nope), blocks(q_pe), q_pos.reshape(nb, qb)))
    o = jnp.moveaxis(o, 0, 1).reshape(B, L, MLA_HEADS * V_HEAD)
    return o @ w_o


def peer(x, w_q, subkeys, u_tab, v_tab):
    shp = x.shape
    xt = x.reshape(-1, D_MODEL)
    T = xt.shape[0]
    nb = -(-T // PEER_BLOCK)
    xt = jnp.pad(xt, ((0, nb * PEER_BLOCK - T), (0, 0))).reshape(nb, PEER_BLOCK, D_MODEL)
    half = PEER_DK // 2

    def one_block(xb):
        q = (xb @ w_q).reshape(PEER_BLOCK, PEER_HEADS, PEER_DK)
        s1 = jnp.einsum('thd,nd->thn', q[..., :half], subkeys[0])
        s2 = jnp.einsum('thd,nd->thn', q[..., half:], subkeys[1])
        v1, i1 = lax.top_k(s1, PEER_TOPK)
        v2, i2 = lax.top_k(s2, PEER_TOPK)
        cand = (v1[..., :, None] + v2[..., None, :]).reshape(PEER_BLOCK, PEER_HEADS, PEER_TOPK * PEER_TOPK)
        cid = (i1[..., :, None] * PEER_NKEYS + i2[..., None, :]).reshape(PEER_BLOCK, PEER_HEADS, PEER_TOPK * PEER_TOPK)
        sc, sel = lax.top_k(cand, PEER_TOPK)
        eid = jnp.take_along_axis(cid, sel, axis=-1)
        g = jax.nn.softmax(sc.astype(jnp.float32), axis=-1).astype(xb.dtype)
        u = jnp.take(u_tab, eid, axis=0)
        hdn = jnp.einsum('td,thkd->thk', xb, u)
        a = g * jax.nn.gelu(hdn, approximate=False)
        return jnp.einsum('thk,thkd->td', a, jnp.take(v_tab, eid, axis=0))

    out = lax.map(one_block, xt).reshape(nb * PEER_BLOCK, D_MODEL)[:T]
    return out.reshape(shp)


def trunk(x, pos, past_c, past_kpe, p):
    gm_rows = []
    shared = None
    c_new = kpe_new = None
    for l in range(DEPTH):
        if l < N_A:
            mix, v_rows = gmlp_mixer(x, p['gm_w_in'][l], p['gm_b_in'][l], p['gm_ln_g'][l], p['gm_ln_b'][l],
                                     p['gm_w_s'][l], p['gm_b_s'][l], p['gm_w_out'][l])
            gm_rows.append(v_rows)
        else:
            if shared is None:
                c_new, kpe_new = mla_shared_kv(x, pos, p['mla_w_dkv'], p['mla_kv_norm_g'])
                if past_c is None:
                    c_all, kpe_all, k_pos = c_new, kpe_new, pos
                else:
                    c_all = jnp.concatenate([past_c, c_new], axis=1)
                    kpe_all = jnp.concatenate([past_kpe, kpe_new], axis=1)
                    k_pos = jnp.concatenate([jnp.arange(past_c.shape[1], dtype=pos.dtype), pos])
                k_nope, v_all = mla_expand(c_all, p['mla_w_ukv'])
                shared = (k_nope, kpe_all, v_all, k_pos)
            j = l - N_A
            mix = mla_layer(x, pos, shared[0], shared[1], shared[2], shared[3],
                            p['mla_w_dq'][j], p['mla_q_norm_g'][j], p['mla_w_uq'][j], p['mla_w_o'][j])
        x = layer_norm(ALPHA * x + mix, p['ln1_g'][l], p['ln1_b'][l])
        x = layer_norm(ALPHA * x + peer(x, p['peer_w_q'][l], p['peer_subkeys'][l], p['peer_u'][l], p['peer_v'][l]),
                       p['ln2_g'][l], p['ln2_b'][l])
    return x, gm_rows, c_new, kpe_new


def setup_inputs(seed: int = 0) -> dict:
    key = jax.random.key(seed)
    ks = jax.random.split(key, 32)
    f32 = jnp.float32

    def nrm(k, shape, scale):
        return jax.random.normal(k, shape, f32) * scale

    w_uk = nrm(ks[17], (KV_LORA, MLA_HEADS, QK_NOPE), KV_LORA ** -0.5)
    w_uv = nrm(ks[18], (KV_LORA, MLA_HEADS, V_HEAD), BETA * KV_LORA ** -0.5)
    return {
        'x_prompt': nrm(ks[0], (BATCH, SEQ, D_MODEL), 1.0),
        'x_sample': nrm(ks[1], (DEC_BATCH, DEC_SEQ, D_MODEL), 1.0),
        'cache_ckv': nrm(ks[2], (DEC_BATCH, PAST_LEN, KV_LORA), 1.0),
        'cache_kpe': nrm(ks[3], (DEC_BATCH, PAST_LEN, QK_ROPE), 1.0),
        'ln1_g': 1.0 + nrm(ks[4], (DEPTH, D_MODEL), 0.02),
        'ln1_b': nrm(ks[5], (DEPTH, D_MODEL), 0.02),
        'ln2_g': 1.0 + nrm(ks[6], (DEPTH, D_MODEL), 0.02),
        'ln2_b': nrm(ks[7], (DEPTH, D_MODEL), 0.02),
        'gm_w_in': nrm(ks[8], (N_A, D_MODEL, 2 * GM_HALF), D_MODEL ** -0.5),
        'gm_b_in': nrm(ks[9], (N_A, 2 * GM_HALF), 0.02),
        'gm_ln_g': 1.0 + nrm(ks[10], (N_A, GM_HALF), 0.02),
        'gm_ln_b': nrm(ks[11], (N_A, GM_HALF), 0.02),
        'gm_w_s': nrm(ks[12], (N_A, GM_GROUPS, GM_CHUNK, GM_CHUNK), GM_CHUNK ** -0.5),
        'gm_b_s': 1.0 + nrm(ks[13], (N_A, GM_GROUPS, GM_CHUNK), 0.1),
        'gm_w_out': nrm(ks[14], (N_A, GM_HALF, D_MODEL), BETA * GM_HALF ** -0.5),
        'mla_w_dkv': nrm(ks[15], (D_MODEL, KV_LORA + QK_ROPE), D_MODEL ** -0.5),
        'mla_kv_norm_g': 1.0 + nrm(ks[16], (KV_LORA,), 0.02),
        'mla_w_ukv': jnp.concatenate([w_uk, w_uv], axis=-1).reshape(KV_LORA, MLA_HEADS * (QK_NOPE + V_HEAD)),
        'mla_w_dq': nrm(ks[19], (N_B, D_MODEL, Q_LORA), D_MODEL ** -0.5),
        'mla_q_norm_g': 1.0 + nrm(ks[20], (N_B, Q_LORA), 0.02),
        'mla_w_uq': nrm(ks[21], (N_B, Q_LORA, MLA_HEADS * (QK_NOPE + QK_ROPE)), Q_LORA ** -0.5),
        'mla_w_o': nrm(ks[22], (N_B, MLA_HEADS * V_HEAD, D_MODEL), BETA * (MLA_HEADS * V_HEAD) ** -0.5),
        'peer_w_q': nrm(ks[23], (DEPTH, D_MODEL, PEER_HEADS * PEER_DK), D_MODEL ** -0.5),
        'peer_subkeys': nrm(ks[24], (DEPTH, 2, PEER_NKEYS, PEER_DK // 2), (PEER_DK // 2) ** -0.5),
        'peer_u': nrm(ks[25], (DEPTH, PEER_EXPERTS, D_MODEL), D_MODEL ** -0.5),
        'peer_v': nrm(ks[26], (DEPTH, PEER_EXPERTS, D_MODEL), BETA * PEER_HEADS ** -0.5),
    }


def reference(x_prompt, x_sample, cache_ckv, cache_kpe, ln1_g, ln1_b, ln2_g, ln2_b,
              gm_w_in, gm_b_in, gm_ln_g, gm_ln_b, gm_w_s, gm_b_s, gm_w_out,
              mla_w_dkv, mla_kv_norm_g, mla_w_ukv, mla_w_dq, mla_q_norm_g, mla_w_uq, mla_w_o,
              peer_w_q, peer_subkeys, peer_u, peer_v):
    p = dict(ln1_g=ln1_g, ln1_b=ln1_b, ln2_g=ln2_g, ln2_b=ln2_b,
             gm_w_in=gm_w_in, gm_b_in=gm_b_in, gm_ln_g=gm_ln_g, gm_ln_b=gm_ln_b,
             gm_w_s=gm_w_s, gm_b_s=gm_b_s, gm_w_out=gm_w_out,
             mla_w_dkv=mla_w_dkv, mla_kv_norm_g=mla_kv_norm_g, mla_w_ukv=mla_w_ukv,
             mla_w_dq=mla_w_dq, mla_q_norm_g=mla_q_norm_g, mla_w_uq=mla_w_uq, mla_w_o=mla_w_o,
             peer_w_q=peer_w_q, peer_subkeys=peer_subkeys, peer_u=peer_u, peer_v=peer_v)
    pos_prompt = jnp.arange(x_prompt.shape[1], dtype=jnp.int32)
    y_prompt, _, new_ckv_prompt, new_kpe_prompt = trunk(x_prompt, pos_prompt, None, None, p)
    pos_sample = cache_ckv.shape[1] + jnp.arange(x_sample.shape[1], dtype=jnp.int32)
    y_sample, gm_rows, new_ckv_sample, new_kpe_sample = trunk(x_sample, pos_sample, cache_ckv, cache_kpe, p)
    new_gmlp_v = jnp.stack(gm_rows, axis=0)
    return (y_prompt, y_sample, new_gmlp_v, new_ckv_prompt, new_kpe_prompt, new_ckv_sample, new_kpe_sample)
```

```python
import math
import numpy as np
from contextlib import ExitStack
import concourse.bass as bass
import concourse.mybir as mybir
from concourse.bass_utils import run_bass_kernel_spmd

F32 = mybir.dt.float32
BF16 = mybir.dt.bfloat16
I32 = mybir.dt.int32
U32 = mybir.dt.uint32
ALU = mybir.AluOpType
AF = mybir.ActivationFunctionType
AX = mybir.AxisListType

D = 1024
DEPTH = 4
N_A = 2
ALPHA = (2.0 * DEPTH) ** 0.25
LN_EPS = 1e-5
RMS_EPS = 1e-6
GH = 2048
KV_LORA = 256
QK_ROPE = 64
Q_LORA = 384
NH = 8
ATTN_SCALE = (128 + 64) ** -0.5
NEXP = 16384
NS = 32
NG = 22
GS = 8


class Buf:
    __slots__ = ("name", "t", "last_w", "readers", "dsem", "dcnt")

    def __init__(self, name, t=None):
        self.name = name
        self.t = t
        self.last_w = None
        self.readers = {}
        self.dsem = None
        self.dcnt = 0

    def __getitem__(self, idx):
        return self.t[idx]


class Prog:
    ENG = ("pe", "act", "dve", "pool", "sp")

    def __init__(self, nc, stack, n_dma_sems=64):
        self.nc = nc
        self.q = {e: [] for e in self.ENG}
        self.cnt = {e: 0 for e in self.ENG}
        self.waited = {e: {} for e in self.ENG}
        self.sems = {}
        for e in self.ENG:
            self.sems["e_" + e] = stack.enter_context(nc.semaphore("e_" + e))
        self.free = []
        for i in range(n_dma_sems):
            k = "d%d" % i
            self.sems[k] = stack.enter_context(nc.semaphore(k))
            self.free.append(k)
        self.dval = {k: 0 for k in self.free}
        self.inuse = []
        self.n_inst = 0
        self.epoch = 0
        self.nreset = 0
        self.sems["B1"] = stack.enter_context(nc.semaphore("B1"))
        self.sems["B2"] = stack.enter_context(nc.semaphore("B2"))

    def _dsem(self, b):
        if b.dsem is None:
            k = self.free.pop()
            b.dsem = k
            b.dcnt = self.dval[k]
            self.inuse.append(b)
        return b.dsem

    def _deps(self, eng, reads, writes):
        deps = {}

        def add(k, v):
            if k in self.dval:
                v = self.dval[k]
            if v > deps.get(k, 0):
                deps[k] = v
        ep = self.epoch
        for b in reads:
            if b.last_w is not None and b.last_w[2] == ep:
                add(b.last_w[0], b.last_w[1])
        for b in writes:
            if b.last_w is not None and b.last_w[2] == ep:
                add(b.last_w[0], b.last_w[1])
            for k, (v, e_) in b.readers.items():
                if e_ == ep:
                    add(k, v)
        out = []
        w = self.waited[eng]
        for k, v in deps.items():
            if eng == "pe" and k == "e_pe":
                continue
            if w.get(k, 0) >= v:
                continue
            w[k] = v
            out.append((k, v))
        return out

    def op(self, eng, fn, reads=(), writes=(), inc=True):
        waits = self._deps(eng, reads, writes)
        key = "e_" + eng
        if inc:
            self.cnt[eng] += 1
            val = self.cnt[eng]
        else:
            val = self.cnt[eng] + 1
        self.q[eng].append((waits, fn, key if inc else None, 1))
        for b in writes:
            b.last_w = (key, val, self.epoch)
            b.readers = {}
        for b in reads:
            o_ = b.readers.get(key)
            if o_ is None or o_[1] != self.epoch or o_[0] < val:
                b.readers[key] = (val, self.epoch)
        self.n_inst += 1

    def dma(self, eng, fn, reads=(), writes=(), owner=None, inc=16):
        waits = self._deps(eng, reads, writes)
        if owner is None:
            owner = writes[0]
        key = self._dsem(owner)
        owner.dcnt += inc
        val = owner.dcnt
        self.dval[key] = val
        self.q[eng].append((waits, fn, key, inc))
        for b in writes:
            b.last_w = (key, val, self.epoch)
            b.readers = {}
        for b in reads:
            o_ = b.readers.get(key)
            if o_ is None or o_[1] != self.epoch or o_[0] < val:
                b.readers[key] = (val, self.epoch)
        self.n_inst += 1

    def barrier(self):
        for e in self.ENG:
            waits = []
            w = self.waited[e]
            for k, v in self.dval.items():
                if v > 0 and w.get(k, 0) < v:
                    waits.append((k, v))
                    w[k] = v
            for e2 in self.ENG:
                k = "e_" + e2
                if e2 != e and self.cnt[e2] > w.get(k, 0):
                    waits.append((k, self.cnt[e2]))
                    w[k] = self.cnt[e2]
            self.q[e].append((waits, None, None, 0))
        for b in self.inuse:
            self.free.append(b.dsem)
            b.dsem = None
        self.inuse = []
        self.epoch += 1

    def reset(self):
        self.nreset += 1
        k_ = self.nreset
        B1, B2 = self.sems["B1"], self.sems["B2"]
        for e in self.ENG:
            self.q[e].append(([], lambda eng: eng.sem_inc(B1, 1), None, 0))
        self.q["sp"].append(([("B1", 5 * k_)], None, None, 0))
        for k, sm in self.sems.items():
            if k in ("B1", "B2"):
                continue
            self.q["sp"].append(([], lambda eng, sm=sm: eng.sem_clear(sm), None, 0))
        self.q["sp"].append(([], lambda eng: eng.sem_inc(B2, 1), None, 0))
        for e in self.ENG:
            self.q[e].append(([("B2", k_)], None, None, 0))
        self.cnt = {e: 0 for e in self.ENG}
        self.waited = {e: {} for e in self.ENG}
        for k in self.dval:
            self.dval[k] = 0
        self.epoch += 1

    def flush(self):
        nc = self.nc
        sems = self.sems
        with nc.Block() as block:
            engmap = {"pe": block.tensor, "act": block.scalar, "dve": block.vector,
                      "pool": block.gpsimd, "sp": block.sync}
            for e in self.ENG:
                items = self.q[e]

                def body(engine, items=items):
                    for waits, fn, key, inc in items:
                        for k, v in waits:
                            engine.wait_ge(sems[k], v)
                        if fn is None:
                            continue
                        ins = fn(engine)
                        if key is not None:
                            ins.then_inc(sems[key], inc)
                engmap[e](body)
        self.q = {e: [] for e in self.ENG}


def V(P, name, R, W, eng="dve", **kw):
    P.op(eng, lambda e: getattr(e, name)(**kw), reads=R, writes=W)


def MM(P, out, lhsT, rhs, start, stop, R, W, inc=None):
    P.op("pe", lambda e: e.matmul(out, lhsT=lhsT, rhs=rhs, start=start, stop=stop), reads=R, writes=W,
         inc=(stop if inc is None else inc))


def TP(P, out, in_, ident, R, W, inc=True):
    P.op("pe", lambda e: e.transpose(out=out, in_=in_, identity=ident), reads=R, writes=W, inc=inc)


def DMA(P, q, out, in_, R, W, owner=None):
    P.dma(q, lambda e: e.dma_start(out=out, in_=in_), reads=R, writes=W, owner=owner)


def build(SEQ, PAST):
    NT = SEQ // 256
    NPT = NT * 128
    NTOK = NPT + NS
    tiles = [(j * 128, 128) for j in range(NT)] + [(NPT, NS)]
    nc = bass.Bass("TRN2", target_bir_lowering=False)

    def din(name, shape, dt=F32):
        return nc.dram_tensor(name, list(shape), dt, kind="ExternalInput").ap()

    def dout(name, shape, dt=F32):
        return nc.dram_tensor(name, list(shape), dt, kind="ExternalOutput").ap()

    xin = din("xin", [NTOK, D])
    cache_ckv = din("cache_ckv", [PAST, KV_LORA])
    cache_kpe = din("cache_kpe", [PAST, QK_ROPE])
    rope_tok = din("rope_tok", [NTOK, 64])
    ropeT_cs = din("ropeT_cs", [64, NTOK])
    ropeT_sn = din("ropeT_sn", [64, NTOK])
    maskb = din("maskb", [128, 256])
    ident_d = din("ident", [128, 128])
    iota_d = din("iota16", [128, 16])
    ln1_g = din("ln1_g", [DEPTH, D]); ln1_b = din("ln1_b", [DEPTH, D])
    ln2_g = din("ln2_g", [DEPTH, D]); ln2_b = din("ln2_b", [DEPTH, D])
    gm_w_in = din("gm_w_in", [N_A, D, 2 * GH]); gm_b_in = din("gm_b_in", [N_A, 2 * GH])
    gm_ln_g = din("gm_ln_g", [N_A, GH]); gm_ln_b = din("gm_ln_b", [N_A, GH])
    gm_w_s = din("gm_w_s", [N_A, 8, 128, 128]); gm_b_sT = din("gm_b_sT", [N_A, 128, 8])
    gm_w_out = din("gm_w_out", [N_A, GH, D])
    w_dkv = din("mla_w_dkv", [D, 320]); kv_g = din("mla_kv_norm_g", [1, KV_LORA])
    w_ukv = din("mla_w_ukv", [KV_LORA, 2048])
    w_dq = din("mla_w_dq", [2, D, Q_LORA]); q_g = din("mla_q_norm_g", [2, Q_LORA])
    w_uq_n = din("w_uq_n", [2, Q_LORA, 1024]); w_uq_p = din("w_uq_p", [2, Q_LORA, 512])
    w_uq_r = din("w_uq_r", [2, Q_LORA, 512])
    w_o = din("mla_w_o", [2, 1024, D])
    peer_w_q = din("peer_w_q", [DEPTH, D, 2048]); peer_sk = din("peer_subkeys", [DEPTH, 2, 128, 128])
    peer_u_f = din("peer_u", [DEPTH * NEXP, D]); peer_v_f = din("peer_v", [DEPTH * NEXP, D])

    y_out = dout("y_out", [NTOK, D])
    gv_out = dout("gv_out", [N_A, NS, GH])
    ckv_out = dout("ckv_out", [NTOK, KV_LORA])
    kpe_out = dout("kpe_out", [NTOK, QK_ROPE])

    xa = nc.dram_tensor("xa", [NTOK, D], F32).ap()
    xb = nc.dram_tensor("xb", [NTOK, D], F32).ap()
    CHT = 4
    NCH = (NT + CHT - 1) // CHT
    chn = [min(CHT, NT - ch * CHT) for ch in range(NCH)]
    exin = [nc.dram_tensor("exin%d" % ch, [chn[ch] * 128, 320], F32).ap() for ch in range(NCH)]
    exout = [nc.dram_tensor("exout%d" % ch, [2 * chn[ch] * 128, 320], F32).ap() for ch in range(NCH)]
    samp_ck = nc.dram_tensor("samp_ck", [NS, 320], F32).ap()
    uvb = nc.dram_tensor("uvb", [DEPTH * NEXP, 2 * D], BF16).ap()
    uvb_b = Buf("uvb")

    xa_b = [Buf("xa%d" % i) for i in range(NT + 1)]
    xb_b = [Buf("xb%d" % i) for i in range(NT + 1)]
    exin_b = [Buf("exin%d" % ch) for ch in range(NCH)]; exout_b = [Buf("exout%d" % ch) for ch in range(NCH)]; samp_b = Buf("samp")
    outs_b = Buf("outs")

    with ExitStack() as top:
        P = Prog(nc, top)

        uid = [0]

        def SB(st, name, shape, dt):
            uid[0] += 1
            return Buf(name, st.enter_context(nc.sbuf_tensor("sb%d_%s" % (uid[0], name), list(shape), dt)))

        def PS(st, name, shape, dt):
            uid[0] += 1
            return Buf(name, st.enter_context(nc.psum_tensor("ps%d_%s" % (uid[0], name), list(shape), dt)))

        idf = SB(top, "idf", [128, 128], F32)
        idb = SB(top, "idb", [128, 128], BF16)
        iota16 = SB(top, "iota16", [128, 16], F32)
        ones_bf = SB(top, "ones_bf", [1, 128], BF16)
        DMA(P, "sp", idf[:], ident_d, [], [idf])
        DMA(P, "sp", iota16[:], iota_d, [], [iota16])
        V(P, "tensor_copy", [idf], [idb], out=idb[:], in_=idf[:])
        V(P, "memset", [], [ones_bf], ap=ones_bf[:], constant=1.0)
        cast_rr = [0]

        def cast(out, in_, R, W):
            cast_rr[0] += 1
            if cast_rr[0] % 2:
                V(P, "tensor_copy", R, W, out=out, in_=in_)
            else:
                V(P, "activation", R, W, eng="act", out=out, in_=in_, func=AF.Copy)

        def load_w(st_bufs, dram2d, K, cols, dst, SW):
            i = 0
            for kc in range((K + 127) // 128):
                kr = min(128, K - kc * 128)
                for c0 in range(0, cols, SW):
                    w = min(SW, cols - c0)
                    sg = st_bufs[i % len(st_bufs)]
                    i += 1
                    DMA(P, "sp" if i % 2 else "act", sg[0:kr, 0:w], dram2d[kc * 128:kc * 128 + kr, c0:c0 + w], [], [sg])
                    cast(dst[0:kr, kc, c0:c0 + w], sg[0:kr, 0:w], [sg], [dst])

        def bcast_row(dst, row_ap):
            DMA(P, "sp", dst[:], row_ap.partition_broadcast(128), [], [dst])

        def make_xT(x, n, xbf, tp, xT, KC=8):
            cast(xbf[0:n, 0:KC * 128], x[0:n, 0:KC * 128], [x], [xbf])
            for kc in range(KC):
                TP(P, tp[:, kc, 0:n], xbf[0:n, kc * 128:(kc + 1) * 128], idb[0:n, 0:n], [xbf, idb], [tp], inc=(kc == KC - 1))
            V(P, "tensor_copy", [tp], [xT], out=xT[:, 0:KC, 0:n], in_=tp[:, 0:KC, 0:n])

        def layer_norm(t, n, Dn, gB, bB, st6, mv, rstd):
            C = Dn // 512
            for c in range(C):
                V(P, "bn_stats", [t], [st6], out=st6[0:n, c * 6:(c + 1) * 6], in_=t[0:n, c * 512:(c + 1) * 512])
            V(P, "bn_aggr", [st6], [mv], out=mv[0:n, :], in_=st6[0:n, 0:C * 6])
            V(P, "tensor_scalar", [mv], [rstd], out=rstd[0:n, :], in0=mv[0:n, 1:2], scalar1=LN_EPS, scalar2=None, op0=ALU.add)
            V(P, "activation", [rstd], [rstd], eng="act", out=rstd[0:n, :], in_=rstd[0:n, :], func=AF.Sqrt)
            V(P, "reciprocal", [rstd], [rstd], out=rstd[0:n, :], in_=rstd[0:n, :])
            V(P, "tensor_scalar", [t, mv, rstd], [t], out=t[0:n, 0:Dn], in0=t[0:n, 0:Dn], scalar1=mv[0:n, 0:1],
              scalar2=rstd[0:n, 0:1], op0=ALU.subtract, op1=ALU.mult)
            V(P, "tensor_tensor", [t, gB], [t], out=t[0:n, 0:Dn], in0=t[0:n, 0:Dn], in1=gB[0:n, 0:Dn], op=ALU.mult)
            V(P, "tensor_tensor", [t, bB], [t], out=t[0:n, 0:Dn], in0=t[0:n, 0:Dn], in1=bB[0:n, 0:Dn], op=ALU.add)

        def src_tile(l_src, ti):
            r0, n = tiles[ti]
            if l_src == "in":
                return xin[r0:r0 + n, :], None
            if l_src == "a":
                return xa[r0:r0 + n, :], xa_b[ti]
            return xb[r0:r0 + n, :], xb_b[ti]

        def phase_C():
            with ExitStack() as st:
                RR = 4
                su = [SB(st, "c_su%d" % i, [128, RR, D], F32) for i in range(2)]
                sv = [SB(st, "c_sv%d" % i, [128, RR, D], F32) for i in range(2)]
                ot = [st.enter_context(nc.sbuf_tensor("c_ot%d" % i, [128, RR, 2 * D], BF16)) for i in range(2)]
                ou = [Buf("c_ou%d" % i, ot[i]) for i in range(2)]
                ov = [Buf("c_ov%d" % i, ot[i]) for i in range(2)]
                uview = peer_u_f.rearrange("(p r) d -> p r d", p=128)
                vview = peer_v_f.rearrange("(p r) d -> p r d", p=128)
                oview = uvb.rearrange("(p r) d -> p r d", p=128)
                npc = (DEPTH * NEXP // 128) // RR
                for i in range(npc):
                    a = su[i % 2]; b = sv[i % 2]; o1 = ou[i % 2]; o2 = ov[i % 2]
                    DMA(P, "sp", a[:], uview[:, i * RR:(i + 1) * RR, :], [], [a])
                    DMA(P, "act", b[:], vview[:, i * RR:(i + 1) * RR, :], [], [b])
                    V(P, "tensor_copy", [a], [o1], out=o1[:, :, 0:D], in_=a[:])
                    V(P, "activation", [b], [o2], eng="act", out=o2[:, :, D:2 * D], in_=b[:], func=AF.Copy)
                    DMA(P, "sp", oview[:, i * RR:(i + 1) * RR, :], o1[:], [o1, o2], [uvb_b], owner=o1)
                P.barrier()
                P.flush()

        def phase_G(l, src, dst):
            with ExitStack() as st:
                w_in = SB(st, "g_w_in", [128, 8, 4096], BF16)
                w_out = SB(st, "g_w_out", [128, 16, 1024], BF16)
                stg = [SB(st, "g_stg%d" % i, [128, 2048], F32) for i in range(2)]
                b_in = SB(st, "g_b_in", [1, 4096], BF16)
                glg = SB(st, "g_glg", [128, GH], F32); glb = SB(st, "g_glb", [128, GH], F32)
                l1g = SB(st, "g_l1g", [128, D], F32); l1b = SB(st, "g_l1b", [128, D], F32)
                ws_f = SB(st, "g_ws_f", [128, 8, 128], F32)
                ws_b = SB(st, "g_ws_b", [128, 8, 128], BF16)
                wsT = SB(st, "g_wsT", [128, 8, 128], BF16)
                bsT = SB(st, "g_bsT", [128, 8], F32)
                xs = [SB(st, "g_x%d" % i, [128, D], F32) for i in range(2)]
                xbf = SB(st, "g_xbf", [128, D], BF16)
                xT = SB(st, "g_xT", [128, 8, 128], BF16)
                u = SB(st, "g_u", [128, GH], BF16)
                v = SB(st, "g_v", [128, GH], F32)
                vb = SB(st, "g_vb", [128, GH], BF16)
                s = SB(st, "g_s", [128, GH], BF16)
                sT = SB(st, "g_sT", [128, 16, 128], BF16)
                st6 = SB(st, "g_st6", [128, 24], F32); mv = SB(st, "g_mv", [128, 2], F32); rstd = SB(st, "g_rstd", [128, 1], F32)
                tp = [PS(st, "g_tp%d" % i, [128, 8, 128], BF16) for i in range(2)]
                zp = [PS(st, "g_zp%d" % i, [128, 512], F32) for i in range(2)]
                svp = [PS(st, "g_svp%d" % i, [128, 512], F32) for i in range(2)]
                mxp = [PS(st, "g_mxp%d" % i, [128, 512], F32) for i in range(2)]

                load_w(stg, gm_w_in[l], D, 4096, w_in, 2048)
                load_w(stg, gm_w_out[l], GH, D, w_out, 2048)
                for hb in range(2):
                    DMA(P, "sp", stg[hb][0:1, :], gm_b_in[l:l + 1, hb * 2048:(hb + 1) * 2048], [], [stg[hb]])
                    V(P, "tensor_copy", [stg[hb]], [b_in], out=b_in[0:1, hb * 2048:(hb + 1) * 2048], in_=stg[hb][0:1, :])
                bcast_row(glg, gm_ln_g[l, :]); bcast_row(glb, gm_ln_b[l, :])
                bcast_row(l1g, ln1_g[l, :]); bcast_row(l1b, ln1_b[l, :])
                DMA(P, "sp", ws_f[:], gm_w_s[l].rearrange("g i j -> i g j"), [], [ws_f])
                DMA(P, "sp", bsT[:], gm_b_sT[l], [], [bsT])
                V(P, "memset", [ws_f], [ws_f], ap=ws_f[0:64, :, 64:128], constant=0.0)
                V(P, "tensor_copy", [ws_f], [ws_b], out=ws_b[:], in_=ws_f[:])
                for g in range(8):
                    TP(P, tp[0][:, g, :], ws_b[:, g, :], idb[:], [ws_b, idb], [tp[0]], inc=(g == 7))
                V(P, "tensor_copy", [tp[0]], [wsT], out=wsT[:], in_=tp[0][:])

                ot = [st.enter_context(nc.sbuf_tensor("g_ot%d_%d" % (l, i), [128, 2, 2 * D], BF16)) for i in range(2)]
                ou = [Buf("g_ou%d" % i, ot[i]) for i in range(2)]
                ov = [Buf("g_ov%d" % i, ot[i]) for i in range(2)]
                rlo = l * 2 * NEXP
                uview = peer_u_f[rlo:rlo + 2 * NEXP, :].rearrange("(p r) d -> p r d", p=128)
                vview = peer_v_f[rlo:rlo + 2 * NEXP, :].rearrange("(p r) d -> p r d", p=128)
                oview = uvb[rlo:rlo + 2 * NEXP, :].rearrange("(p r) d -> p r d", p=128)
                npc = (2 * NEXP // 128)
                cpi = [0]
                suh = [Buf("g_suh%d" % i, stg[0].t) for i in range(2)]
                svh = [Buf("g_svh%d" % i, stg[1].t) for i in range(2)]
                ouh = [[Buf("g_ouh%d%d" % (i, j), ot[i]) for j in range(2)] for i in range(2)]
                ovh = [[Buf("g_ovh%d%d" % (i, j), ot[i]) for j in range(2)] for i in range(2)]

                def conv_pieces(k):
                    for _ in range(k):
                        i = cpi[0]
                        if i >= npc:
                            return
                        cpi[0] += 1
                        hb = i % 2; ob_ = (i // 2) % 2
                        a_ = suh[hb]; b_ = svh[hb]; o1 = ouh[ob_][hb]; o2 = ovh[ob_][hb]
                        DMA(P, "sp", stg[0][:, hb * D:(hb + 1) * D], uview[:, i, :], [], [a_] + ([stg[0]] if i < 2 else []), owner=a_)
                        DMA(P, "act", stg[1][:, hb * D:(hb + 1) * D], vview[:, i, :], [], [b_] + ([stg[1]] if i < 2 else []), owner=b_)
                        V(P, "tensor_copy", [a_], [o1], out=ot[ob_][:, hb, 0:D], in_=stg[0][:, hb * D:(hb + 1) * D])
                        V(P, "activation", [b_], [o2], eng="act", out=ot[ob_][:, hb, D:2 * D], in_=stg[1][:, hb * D:(hb + 1) * D], func=AF.Copy)
                        DMA(P, "sp", oview[:, i, :], ot[ob_][:, hb, :], [o1, o2], [uvb_b], owner=o1)
                per_tile = (npc + len(tiles) - 1) // len(tiles)

                for ti, (r0, n) in enumerate(tiles):
                    x = xs[ti % 2]
                    sap, sbuf_ = src_tile(src, ti)
                    DMA(P, "sp", x[0:n, :], sap, [sbuf_] if sbuf_ else [], [x])
                    conv_pieces(per_tile)
                    make_xT(x, n, xbf, tp[ti % 2], xT)
                    for nb in range(8):
                        z = zp[nb % 2]
                        MM(P, z[0:n, :], ones_bf[0:1, 0:n], b_in[0:1, nb * 512:(nb + 1) * 512], True, False, [ones_bf, b_in], [z])
                        for kc in range(8):
                            MM(P, z[0:n, :], xT[:, kc, 0:n], w_in[:, kc, nb * 512:(nb + 1) * 512], False, kc == 7, [xT, w_in], [z])
                        if nb < 4:
                            V(P, "activation", [z], [u], eng="act", out=u[0:n, nb * 512:(nb + 1) * 512], in_=z[0:n, :], func=AF.Gelu)
                        else:
                            V(P, "activation", [z], [v], eng="act", out=v[0:n, (nb - 4) * 512:(nb - 3) * 512], in_=z[0:n, :], func=AF.Gelu)
                    layer_norm(v, n, GH, glg, glb, st6, mv, rstd)
                    if n == NS:
                        DMA(P, "sp", gv_out[l], v[0:n, :], [v], [outs_b], owner=v)
                    V(P, "activation", [v], [vb], eng="act", out=vb[0:n, :], in_=v[0:n, :], func=AF.Copy)
                    for g in range(8):
                        sp_ = svp[(g // 2) % 2]
                        MM(P, sp_[0:n, (g % 2) * 256:(g % 2) * 256 + 256], wsT[0:n, g, 0:n], vb[0:n, g * 256:(g + 1) * 256], True, True, [wsT, vb], [sp_])
                        V(P, "scalar_tensor_tensor", [sp_, bsT, u], [s], out=s[0:n, g * 256:(g + 1) * 256],
                          in0=sp_[0:n, (g % 2) * 256:(g % 2) * 256 + 256], scalar=bsT[0:n, g:g + 1], in1=u[0:n, g * 256:(g + 1) * 256],
                          op0=ALU.add, op1=ALU.mult)
                    for half in range(2):
                        t_ = tp[half]
                        for kc in range(8):
                            c = half * 8 + kc
                            TP(P, t_[:, kc, 0:n], s[0:n, c * 128:(c + 1) * 128], idb[0:n, 0:n], [s, idb], [t_], inc=(kc == 7))
                        V(P, "tensor_copy", [t_], [sT], out=sT[:, half * 8:half * 8 + 8, 0:n], in_=t_[:, :, 0:n])
                    for nb in range(2):
                        for kc in range(16):
                            MM(P, mxp[nb][0:n, :], sT[:, kc, 0:n], w_out[:, kc, nb * 512:(nb + 1) * 512], kc == 0, kc == 15, [sT, w_out], [mxp[nb]])
                        V(P, "scalar_tensor_tensor", [x, mxp[nb]], [x], out=x[0:n, nb * 512:(nb + 1) * 512], in0=x[0:n, nb * 512:(nb + 1) * 512],
                          scalar=ALPHA, in1=mxp[nb][0:n, :], op0=ALU.mult, op1=ALU.add)
                    layer_norm(x, n, D, l1g, l1b, st6, mv, rstd)
                    dap, dbuf = src_tile(dst, ti)
                    DMA(P, "sp", dap, x[0:n, :], [x], [dbuf], owner=x)
                P.barrier()
                pass
                P.flush()

        def phase_P(l, src, dst):
            with ExitStack() as st:
                w_q = SB(st, "p_w_q", [128, 8, 2048], BF16)
                sk_f = SB(st, "p_sk_f", [128, 2, 128], F32)
                sk_b = SB(st, "p_sk_b", [128, 2, 128], BF16)
                skT = SB(st, "p_skT", [128, 2, 128], BF16)
                l2g = SB(st, "p_l2g", [128, D], F32); l2b = SB(st, "p_l2b", [128, D], F32)
                xs = [SB(st, "p_x%d" % i, [128, D], F32) for i in range(2)]
                xbfs = [SB(st, "p_xbf%d" % i, [128, D], BF16) for i in range(2)]
                xT = SB(st, "p_xT", [128, 8, 128], BF16)
                qT = SB(st, "p_qT", [128, 16, 128], BF16)
                ssb = SB(st, "p_ssb", [128, 16, 128], F32)
                tv = SB(st, "p_tv", [128, 16, 16], F32)
                tix = SB(st, "p_tix", [128, 16, 16], U32)
                work = [SB(st, "p_work%d" % i, [128, 256], F32) for i in range(2)]
                cand = SB(st, "p_cand", [128, 8, 256], F32)
                sc = SB(st, "p_sc", [128, 8, 16], F32)
                sel = SB(st, "p_sel", [128, 8, 16], U32)
                gws = [SB(st, "p_gw%d" % i, [128, 8, 16], F32) for i in range(2)]
                ssum = SB(st, "p_ssum", [128, 8], F32)
                ai = SB(st, "p_ai", [128, 8, 16], I32); bi = SB(st, "p_bi", [128, 8, 16], I32)
                af = SB(st, "p_af", [128, 8, 16], F32); bf_ = SB(st, "p_bf", [128, 8, 16], F32)
                i1f = SB(st, "p_i1f", [128, 8, 16], F32); i2f = SB(st, "p_i2f", [128, 8, 16], F32)
                eq = SB(st, "p_eq", [128, 8, 256], F32)
                i1s = SB(st, "p_i1s", [128, 8, 16], F32); i2s = SB(st, "p_i2s", [128, 8, 16], F32)
                eidf = SB(st, "p_eidf", [128, 128], F32)
                eids = [SB(st, "p_eid%d" % i, [128, 128], I32) for i in range(2)]
                hdn = SB(st, "p_hdn", [128, 128], F32)
                aw = SB(st, "p_aw", [128, 128], F32)
                junk = SB(st, "p_junk", [128, D], BF16)
                dgs = [SB(st, "p_dg%d" % i, [128, GS, 128], BF16) for i in range(3)]
                prods = [SB(st, "p_prod%d" % i, [128, D], BF16) for i in range(4)]
                hslot = [Buf("p_hs%d" % i) for i in range(128)]
                awg = [Buf("p_awg%d" % i) for i in range(128 // GS)]
                st6 = SB(st, "p_st6", [128, 24], F32); mv = SB(st, "p_mv", [128, 2], F32); rstd = SB(st, "p_rstd", [128, 1], F32)
                tp = [PS(st, "p_tp%d" % i, [128, 8, 128], BF16) for i in range(2)]
                qp = [PS(st, "p_qp%d" % i, [128, 4, 128], F32) for i in range(2)]
                sp4 = [PS(st, "p_sp%d" % i, [128, 4, 128], F32) for i in range(2)] * 2
                accp = [PS(st, "p_acc%d" % i, [128, 512], F32) for i in range(2)]
                with ExitStack() as st_w:
                    stg = [SB(st_w, "p_stg%d" % i, [128, 2048], F32) for i in range(2)]
                    load_w(stg, peer_w_q[l], D, 2048, w_q, 2048)
                gb = [SB(st, "p_gb%d" % i, [128, 2 * D], BF16) for i in range(NG)]

                DMA(P, "sp", sk_f[:], peer_sk[l].rearrange("s n d -> n s d"), [], [sk_f])
                V(P, "tensor_copy", [sk_f], [sk_b], out=sk_b[:], in_=sk_f[:])
                for s_ in range(2):
                    TP(P, tp[0][:, s_, :], sk_b[:, s_, :], idb[:], [sk_b, idb], [tp[0]], inc=(s_ == 1))
                V(P, "tensor_copy", [tp[0]], [skT], out=skT[:], in_=tp[0][:, 0:2, :])
                bcast_row(l2g, ln2_g[l, :]); bcast_row(l2b, ln2_b[l, :])
                for eid in eids:
                    V(P, "memset", [], [eid], ap=eid[:].bitcast(F32), constant=0.0)
                gi = 0

                def front(ti):
                    r0, n = tiles[ti]
                    xbf = xbfs[ti % 2]; gw = gws[ti % 2]; eid = eids[ti % 2]
                    x = xs[ti % 2]
                    sap, sbuf_ = src_tile(src, ti)
                    DMA(P, "sp", x[0:n, :], sap, [sbuf_] if sbuf_ else [], [x])
                    make_xT(x, n, xbf, tp[ti % 2], xT)
                    yield
                    for c4 in range(4):
                        q_ = qp[c4 % 2]
                        for cc in range(4):
                            c = c4 * 4 + cc
                            for kc in range(8):
                                MM(P, q_[:, cc, 0:n], w_q[:, kc, c * 128:(c + 1) * 128], xT[:, kc, 0:n], kc == 0, kc == 7, [w_q, xT], [q_])
                        cast(qT[:, c4 * 4:c4 * 4 + 4, 0:n], q_[:, :, 0:n], [q_], [qT])
                        yield
                    for c4 in range(4):
                        sp_ = sp4[c4]
                        for cc in range(4):
                            c = c4 * 4 + cc
                            MM(P, sp_[0:n, cc, :], qT[:, c, 0:n], skT[:, c % 2, :], True, True, [qT, skT], [sp_])
                        cast(ssb[0:n, c4 * 4:c4 * 4 + 4, :], sp_[0:n, :, :], [sp_], [ssb])
                        yield
                    for c in range(16):
                        wk = work[c % 2]
                        V(P, "max", [ssb], [tv], out=tv[0:n, c, 0:8], in_=ssb[0:n, c, :])
                        V(P, "max_index", [ssb, tv], [tix], out=tix[0:n, c, 0:8], in_max=tv[0:n, c, 0:8], in_values=ssb[0:n, c, :])
                        V(P, "match_replace", [ssb, tv], [wk], out=wk[0:n, 0:128], in_to_replace=tv[0:n, c, 0:8], in_values=ssb[0:n, c, :], imm_value=-1e30)
                        V(P, "max", [wk], [tv], out=tv[0:n, c, 8:16], in_=wk[0:n, 0:128])
                        V(P, "max_index", [wk, tv], [tix], out=tix[0:n, c, 8:16], in_max=tv[0:n, c, 8:16], in_values=wk[0:n, 0:128])
                        yield
                    c4d = cand[0:n].rearrange("p h (a b) -> p h a b", a=16)
                    V(P, "tensor_tensor", [tv], [cand], out=c4d, in0=tv[0:n, 0:16:2, :].unsqueeze(3).to_broadcast([n, 8, 16, 16]),
                      in1=tv[0:n, 1:16:2, :].unsqueeze(2).to_broadcast([n, 8, 16, 16]), op=ALU.add)
                    for h in range(8):
                        wk = work[h % 2]
                        V(P, "max", [cand], [sc], out=sc[0:n, h, 0:8], in_=cand[0:n, h, :])
                        V(P, "max_index", [cand, sc], [sel], out=sel[0:n, h, 0:8], in_max=sc[0:n, h, 0:8], in_values=cand[0:n, h, :])
                        V(P, "match_replace", [cand, sc], [wk], out=wk[0:n, :], in_to_replace=sc[0:n, h, 0:8], in_values=cand[0:n, h, :], imm_value=-1e30)
                        V(P, "max", [wk], [sc], out=sc[0:n, h, 8:16], in_=wk[0:n, :])
                        V(P, "max_index", [wk, sc], [sel], out=sel[0:n, h, 8:16], in_max=sc[0:n, h, 8:16], in_values=wk[0:n, :])
                        yield
                    V(P, "tensor_tensor", [sc], [gw], out=gw[0:n], in0=sc[0:n], in1=sc[0:n, :, 0:1].to_broadcast([n, 8, 16]), op=ALU.subtract)
                    V(P, "activation", [gw], [gw], eng="act", out=gw[0:n], in_=gw[0:n], func=AF.Exp)
                    V(P, "tensor_reduce", [gw], [ssum], out=ssum[0:n, :], in_=gw[0:n], axis=AX.X, op=ALU.add)
                    V(P, "reciprocal", [ssum], [ssum], out=ssum[0:n, :], in_=ssum[0:n, :])
                    V(P, "tensor_tensor", [gw, ssum], [gw], out=gw[0:n], in0=gw[0:n], in1=ssum[0:n, :].unsqueeze(2).to_broadcast([n, 8, 16]), op=ALU.mult)
                    yield
                    V(P, "tensor_single_scalar", [sel], [ai], out=ai[0:n], in_=sel[0:n].bitcast(I32), scalar=4, op=ALU.arith_shift_right)
                    V(P, "tensor_single_scalar", [sel], [bi], out=bi[0:n], in_=sel[0:n].bitcast(I32), scalar=15, op=ALU.bitwise_and)
                    V(P, "tensor_copy", [ai], [af], out=af[0:n], in_=ai[0:n])
                    V(P, "tensor_copy", [bi], [bf_], out=bf_[0:n], in_=bi[0:n])
                    V(P, "tensor_copy", [tix], [i1f], out=i1f[0:n], in_=tix[0:n, 0:16:2, :])
                    V(P, "tensor_copy", [tix], [i2f], out=i2f[0:n], in_=tix[0:n, 1:16:2, :])
                    e4 = eq[0:n].rearrange("p h (a b) -> p h a b", a=16)
                    io4 = iota16[0:n, :].unsqueeze(1).unsqueeze(1).to_broadcast([n, 8, 16, 16])
                    for (xf, tf, outs_) in ((af, i1f, i1s), (bf_, i2f, i2s)):
                        V(P, "tensor_tensor", [iota16, xf], [eq], out=e4, in0=io4, in1=xf[0:n].unsqueeze(3).to_broadcast([n, 8, 16, 16]), op=ALU.is_equal)
                        V(P, "tensor_tensor", [eq, tf], [eq], out=e4, in0=e4, in1=tf[0:n].unsqueeze(2).to_broadcast([n, 8, 16, 16]), op=ALU.mult)
                        V(P, "tensor_reduce", [eq], [outs_], out=outs_[0:n], in_=e4, axis=AX.X, op=ALU.add)
                        yield
                    V(P, "scalar_tensor_tensor", [i1s, i2s], [eidf], out=eidf[0:n, :], in0=i1s[0:n].rearrange("p h k -> p (h k)"), scalar=128.0,
                      in1=i2s[0:n].rearrange("p h k -> p (h k)"), op0=ALU.mult, op1=ALU.add)
                    V(P, "tensor_copy", [eidf], [eid], out=eid[0:n, :], in_=eidf[0:n, :])
                def back(ti, gen):
                    nonlocal gi
                    r0, n = tiles[ti]
                    x = xs[ti % 2]; xbf = xbfs[ti % 2]; gw = gws[ti % 2]; eid = eids[ti % 2]
                    gwf = gw[0:n].rearrange("p h k -> p (h k)")
                    for grp in range(128 // GS):
                        bufs = []
                        for k in range(GS):
                            s_ = grp * GS + k
                            g_ = gb[gi % NG]; gi += 1
                            bufs.append(g_)
                            P.dma("pool", lambda e, g_=g_, s_=s_: e.indirect_dma_start(
                                out=g_[:, :], out_offset=None, in_=uvb,
                                in_offset=bass.IndirectOffsetOnAxis(ap=eid[:, s_:s_ + 1], axis=0), element_offset=l * NEXP * 2 * D),
                                reads=[eid, uvb_b], writes=[g_])
                            pr_ = prods[s_ % len(prods)]
                            V(P, "tensor_tensor", [g_, xbf], [pr_], out=pr_[0:n, :], in0=g_[0:n, 0:D], in1=xbf[0:n, :], op=ALU.mult)
                            V(P, "activation", [pr_], [hslot[s_]], eng="act", out=junk[0:n, :], in_=pr_[0:n, :], func=AF.Copy,
                              accum_out=hdn[0:n, s_:s_ + 1])
                        if gen is not None:
                            for _ in range(3):
                                next(gen, None)
                        sl = slice(grp * GS, (grp + 1) * GS)
                        ag = awg[grp]
                        V(P, "activation", hslot[sl], [ag], eng="act", out=aw[0:n, sl], in_=hdn[0:n, sl], func=AF.Gelu)
                        V(P, "tensor_tensor", [ag, gw], [ag], out=aw[0:n, sl], in0=aw[0:n, sl], in1=gwf[:, sl], op=ALU.mult)
                        dg = dgs[grp % 3]
                        V(P, "tensor_tensor", [ag, idb], [dg], out=dg[0:n, :, 0:n], in0=aw[0:n, sl].unsqueeze(2).to_broadcast([n, GS, n]),
                          in1=idb[0:n, 0:n].unsqueeze(1).to_broadcast([n, GS, n]), op=ALU.mult)
                        for k in range(GS):
                            s_ = grp * GS + k
                            for nb in range(2):
                                MM(P, accp[nb][0:n, :], dg[0:n, k, 0:n], bufs[k][0:n, D + nb * 512:D + (nb + 1) * 512], s_ == 0, s_ == 127,
                                   [dg, bufs[k]], [accp[nb]], inc=(s_ == 127 or (k == GS - 1 and nb == 1)))
                    if gen is not None:
                        for _ in gen:
                            pass
                    for nb in range(2):
                        V(P, "scalar_tensor_tensor", [x, accp[nb]], [x], out=x[0:n, nb * 512:(nb + 1) * 512], in0=x[0:n, nb * 512:(nb + 1) * 512],
                          scalar=ALPHA, in1=accp[nb][0:n, :], op0=ALU.mult, op1=ALU.add)
                    layer_norm(x, n, D, l2g, l2b, st6, mv, rstd)
                    if dst == "y":
                        DMA(P, "sp", y_out[r0:r0 + n, :], x[0:n, :], [x], [outs_b], owner=x)
                    else:
                        dap, dbuf = src_tile(dst, ti)
                        DMA(P, "sp", dap, x[0:n, :], [x], [dbuf], owner=x)
                for _ in front(0):
                    pass
                for ti in range(len(tiles)):
                    back(ti, front(ti + 1) if ti + 1 < len(tiles) else None)
                P.barrier()
                pass
                P.flush()

        def phase_K(src):
            with ExitStack() as st:
                wd = SB(st, "k_wd", [128, 8, 320], BF16)
                stg = [SB(st, "k_stg%d" % i, [128, 320], F32) for i in range(2)]
                kg = SB(st, "k_kg", [128, KV_LORA], F32)
                xs = [SB(st, "k_x%d" % i, [128, D], F32) for i in range(2)]
                xbf = SB(st, "k_xbf", [128, D], BF16)
                xT = SB(st, "k_xT", [128, 8, 128], BF16)
                kv = [SB(st, "k_kv%d" % i, [128, 320], F32) for i in range(2)]
                ck = [SB(st, "k_ck%d" % i, [128, 320], F32) for i in range(2)]
                rt = [SB(st, "k_rt%d" % i, [128, 64], F32) for i in range(2)]
                junk = SB(st, "k_junk", [128, 256], F32)
                ss = SB(st, "k_ss", [128, 1], F32)
                t1 = SB(st, "k_t1", [128, 32], F32); t2 = SB(st, "k_t2", [128, 32], F32)
                tp = [PS(st, "k_tp%d" % i, [128, 8, 128], BF16) for i in range(2)]
                kp = [PS(st, "k_kp%d" % i, [128, 512], F32) for i in range(2)]
                load_w(stg, w_dkv, D, 320, wd, 320)
                bcast_row(kg, kv_g[0, :])
                for ti, (r0, n) in enumerate(tiles):
                    x = xs[ti % 2]; kv_ = kv[ti % 2]; ck_ = ck[ti % 2]; rt_ = rt[ti % 2]
                    sap, sbuf_ = src_tile(src, ti)
                    DMA(P, "sp", x[0:n, :], sap, [sbuf_] if sbuf_ else [], [x])
                    DMA(P, "act", rt_[0:n, :], rope_tok[r0:r0 + n, :], [], [rt_])
                    make_xT(x, n, xbf, tp[ti % 2], xT)
                    kp_ = kp[ti % 2]
                    for kc in range(8):
                        MM(P, kp_[0:n, 0:320], xT[:, kc, 0:n], wd[:, kc, :], kc == 0, kc == 7, [xT, wd], [kp_])
                    V(P, "tensor_copy", [kp_], [kv_], out=kv_[0:n, :], in_=kp_[0:n, 0:320])
                    V(P, "scalar_tensor_tensor", [kv_], [junk, ss], out=junk[0:n, :], in0=kv_[0:n, 0:256], scalar=1.0, in1=kv_[0:n, 0:256],
                      op0=ALU.mult, op1=ALU.mult, accum_out=ss[0:n, :])
                    V(P, "tensor_scalar", [ss], [ss], out=ss[0:n, :], in0=ss[0:n, :], scalar1=1.0 / KV_LORA, scalar2=RMS_EPS, op0=ALU.mult, op1=ALU.add)
                    V(P, "activation", [ss], [ss], eng="act", out=ss[0:n, :], in_=ss[0:n, :], func=AF.Sqrt)
                    V(P, "reciprocal", [ss], [ss], out=ss[0:n, :], in_=ss[0:n, :])
                    V(P, "scalar_tensor_tensor", [kv_, ss, kg], [ck_], out=ck_[0:n, 0:256], in0=kv_[0:n, 0:256], scalar=ss[0:n, 0:1], in1=kg[0:n, :],
                      op0=ALU.mult, op1=ALU.mult)
                    x1 = kv_[0:n, 256:288]; x2 = kv_[0:n, 288:320]; cs = rt_[0:n, 0:32]; sn = rt_[0:n, 32:64]
                    V(P, "tensor_tensor", [kv_, rt_], [t1], out=t1[0:n, :], in0=x1, in1=cs, op=ALU.mult)
                    V(P, "tensor_tensor", [kv_, rt_], [t2], out=t2[0:n, :], in0=x2, in1=sn, op=ALU.mult)
                    V(P, "tensor_tensor", [t1, t2], [ck_], out=ck_[0:n, 256:288], in0=t1[0:n, :], in1=t2[0:n, :], op=ALU.subtract)
                    V(P, "tensor_tensor", [kv_, rt_], [t1], out=t1[0:n, :], in0=x1, in1=sn, op=ALU.mult)
                    V(P, "tensor_tensor", [kv_, rt_], [t2], out=t2[0:n, :], in0=x2, in1=cs, op=ALU.mult)
                    V(P, "tensor_tensor", [t1, t2], [ck_], out=ck_[0:n, 288:320], in0=t1[0:n, :], in1=t2[0:n, :], op=ALU.add)
                    DMA(P, "sp", ckv_out[r0:r0 + n, :], ck_[0:n, 0:256], [ck_], [outs_b], owner=ck_)
                    DMA(P, "sp", kpe_out[r0:r0 + n, :], ck_[0:n, 256:320], [ck_], [outs_b], owner=ck_)
                    if n == 128:
                        DMA(P, "sp", exin[ti // CHT][(ti % CHT) * 128:(ti % CHT) * 128 + 128, :], ck_[0:n, :], [ck_], [exin_b[ti // CHT]], owner=ck_)
                    else:
                        DMA(P, "sp", samp_ck[:, :], ck_[0:n, :], [ck_], [samp_b], owner=ck_)
                P.barrier()
                for ch in range(NCH):
                    P.dma("pool", lambda e, ch=ch: e.collective_compute(
                        "AllGather", ALU.bypass, replica_groups=[[0, 1], [2, 3], [4, 5], [6, 7]],
                        ins=[exin[ch]], outs=[exout[ch]]), reads=[exin_b[ch]], writes=[exout_b[ch]], inc=1)
                    P.inuse.remove(exout_b[ch])
                P.barrier()
                pass
                P.flush()

        def phase_A(jl, l, src, dst):
            NKT = 2 * NT
            NK = NKT * 128
            NKS = PAST + NS
            NKTS = (NKS + 127) // 128
            with ExitStack() as st:
                NKM = max(NKT, NKTS)
                cT = SB(st, "a_cT", [128, 2, NKM * 128], BF16)
                kpT = SB(st, "a_kpT", [64, NKM * 128], BF16)
                ctok = SB(st, "a_ctok", [128, NKM, 256], BF16)
                kl = [SB(st, "a_kl%d" % i, [128, 320], F32) for i in range(2)]
                klb = [SB(st, "a_klb%d" % i, [128, 320], BF16) for i in range(2)]
                stg = [SB(st, "a_stg%d" % i, [128, 512], F32) for i in range(2)]
                wdq = SB(st, "a_wdq", [128, 8, Q_LORA], BF16)
                qg = SB(st, "a_qg", [128, Q_LORA], F32)
                wun = SB(st, "a_wun", [128, 3, 1024], BF16)
                wup = SB(st, "a_wup", [128, 3, 512], BF16)
                wur = SB(st, "a_wur", [128, 3, 512], BF16)
                wukv = SB(st, "a_wukv", [128, 2, 2048], BF16)
                wukT = SB(st, "a_wukT", [128, 8, 256], BF16)
                wo = SB(st, "a_wo", [128, 8, D], BF16)
                l1g = SB(st, "a_l1g", [128, D], F32); l1b = SB(st, "a_l1b", [128, D], F32)
                mb = SB(st, "a_mb", [128, 256], F32)
                xs = [SB(st, "a_x%d" % i, [128, D], F32) for i in range(1)] * 2
                xbf = SB(st, "a_xbf", [128, D], BF16)
                xT = SB(st, "a_xT", [128, 8, 128], BF16)
                cq = SB(st, "a_cq", [128, Q_LORA], F32)
                cqb = SB(st, "a_cqb", [128, Q_LORA], BF16)
                cqT = SB(st, "a_cqT", [128, 3, 128], BF16)
                ss = SB(st, "a_ss", [128, 1], F32)
                junk = SB(st, "a_junk", [128, Q_LORA], F32)
                qnT = SB(st, "a_qnT", [128, 8, 128], BF16)
                qaT = SB(st, "a_qaT", [128, 2, 8, 128], BF16)
                qpA = SB(st, "a_qpA", [64, 8, 128], F32)
                qpB = SB(st, "a_qpB", [64, 8, 128], F32)
                qpT = SB(st, "a_qpT", [64, 8, 128], BF16)
                rcs = SB(st, "a_rcs", [64, 128], F32); rsn = SB(st, "a_rsn", [64, 128], F32)
                S = SB(st, "a_S", [128, 512], F32)
                Pb = [SB(st, "a_Pb%d" % i, [128, 512], BF16) for i in range(3)]
                PT = [SB(st, "a_PT%d" % i, [128, 4, 128], BF16) for i in range(3)]
                den = SB(st, "a_den", [128, 1], F32)
                ms = [SB(st, "a_m%d" % i, [128, 1], F32) for i in range(2)]
                bms = [SB(st, "a_bm%d" % i, [128, 32], F32) for i in range(2)]
                denbs = [SB(st, "a_denb%d" % i, [128, 32], F32) for i in range(2)]
                oc = SB(st, "a_oc", [128, 8, 256], BF16)
                ocT = SB(st, "a_ocT", [128, 16, 128], BF16)
                oT = SB(st, "a_oT", [128, 8, 128], BF16)
                st6 = SB(st, "a_st6", [128, 24], F32); mv = SB(st, "a_mv", [128, 2], F32); rstd = SB(st, "a_rstd", [128, 1], F32)
                tp = [PS(st, "a_tp%d" % i, [128, 8, 128], BF16) for i in range(2)]
                gp = [PS(st, "a_gp%d" % i, [128, 512], F32) for i in range(3)]
                gp.append(PS(st, "a_gp3", [128, 512], F32))
                mxp = [PS(st, "a_mxp%d" % i, [128, 512], F32) for i in range(2)]

                load_w(stg, w_dq[jl], D, Q_LORA, wdq, 512)
                load_w(stg, w_uq_n[jl], Q_LORA, 1024, wun, 512)
                load_w(stg, w_uq_p[jl], Q_LORA, 512, wup, 512)
                load_w(stg, w_uq_r[jl], Q_LORA, 512, wur, 512)
                load_w(stg, w_ukv, KV_LORA, 2048, wukv, 512)
                load_w(stg, w_o[jl], 1024, D, wo, 512)
                bcast_row(qg, q_g[jl, :]); bcast_row(l1g, ln1_g[l, :]); bcast_row(l1b, ln1_b[l, :])
                DMA(P, "sp", mb[:], maskb, [], [mb])
                for h in range(8):
                    t_ = tp[h % 2]
                    for cc in range(2):
                        TP(P, t_[:, cc, :], wukv[:, cc, h * 256:h * 256 + 128], idb[:], [wukv, idb], [t_], inc=(cc == 1))
                    V(P, "tensor_copy", [t_], [wukT], out=wukT[:, h, :].rearrange("p (c k) -> p c k", c=2), in_=t_[:, 0:2, :])

                def key_tile(srcap, srcbuf, nk, kt, cT_, kpT_, ctok_, i):
                    a = kl[i % 2]; b = klb[i % 2]; t_ = tp[i % 2]
                    for (sa, c0, c1) in srcap:
                        DMA(P, "sp" if i % 2 else "act", a[0:nk, c0:c1], sa, [srcbuf] if srcbuf else [], [a])
                    cast(b[0:nk, :], a[0:nk, :], [a], [b])
                    V(P, "tensor_copy", [b], [ctok_], eng="pool", out=ctok_[0:nk, kt, :], in_=b[0:nk, 0:256])
                    TP(P, t_[:, 0, 0:nk], b[0:nk, 0:128], idb[0:nk, 0:nk], [b, idb], [t_], inc=False)
                    TP(P, t_[:, 1, 0:nk], b[0:nk, 128:256], idb[0:nk, 0:nk], [b, idb], [t_], inc=False)
                    TP(P, t_[0:64, 2, 0:nk], b[0:nk, 256:320], idb[0:nk, 0:nk], [b, idb], [t_], inc=True)
                    V(P, "tensor_copy", [t_], [cT_], out=cT_[:, :, kt * 128:kt * 128 + nk], in_=t_[:, 0:2, 0:nk])
                    V(P, "tensor_copy", [t_], [kpT_], out=kpT_[0:64, kt * 128:kt * 128 + nk], in_=t_[0:64, 2, 0:nk])
                ki = 0
                for kt in range(NKT):
                    j_ = kt // 2
                    ch = j_ // CHT
                    r_ = (kt % 2) * chn[ch] * 128 + (j_ % CHT) * 128
                    key_tile([(exout[ch][r_:r_ + 128, :], 0, 320)], exout_b[ch], 128, kt, cT, kpT, ctok, ki); ki += 1

                def sample_keys():
                    ki2 = ki
                    for kt in range(NKTS):
                        k0 = kt * 128
                        if k0 + 128 <= PAST:
                            key_tile([(cache_ckv[k0:k0 + 128, :], 0, 256), (cache_kpe[k0:k0 + 128, :], 256, 320)], None, 128, kt, cT, kpT, ctok, ki2)
                        else:
                            key_tile([(samp_ck[:, :], 0, 320)], samp_b, NS, kt, cT, kpT, ctok, ki2)
                        ki2 += 1

                gpi = [0]

                def nxt():
                    gpi[0] += 1
                    return gp[gpi[0] % 4]
                for ti, (r0, n) in enumerate(tiles):
                    prompt = (n == 128)
                    nkeys = 256 * (ti + 1) if prompt else NKS
                    cT_, kpT_, ctok_ = (cT, kpT, ctok)
                    if not prompt:
                        sample_keys()
                    x = xs[ti % 2]
                    sap, sbuf_ = src_tile(src, ti)
                    DMA(P, "sp", x[0:n, :], sap, [sbuf_] if sbuf_ else [], [x])
                    DMA(P, "act", rcs[:, 0:n], ropeT_cs[:, r0:r0 + n], [], [rcs])
                    DMA(P, "act", rsn[:, 0:n], ropeT_sn[:, r0:r0 + n], [], [rsn])
                    make_xT(x, n, xbf, tp[ti % 2], xT)
                    g_ = nxt()
                    for kc in range(8):
                        MM(P, g_[0:n, 0:Q_LORA], xT[:, kc, 0:n], wdq[:, kc, :], kc == 0, kc == 7, [xT, wdq], [g_])
                    V(P, "tensor_copy", [g_], [cq], out=cq[0:n, :], in_=g_[0:n, 0:Q_LORA])
                    V(P, "scalar_tensor_tensor", [cq], [junk, ss], out=junk[0:n, :], in0=cq[0:n, :], scalar=1.0, in1=cq[0:n, :],
                      op0=ALU.mult, op1=ALU.mult, accum_out=ss[0:n, :])
                    V(P, "tensor_scalar", [ss], [ss], out=ss[0:n, :], in0=ss[0:n, :], scalar1=1.0 / Q_LORA, scalar2=RMS_EPS, op0=ALU.mult, op1=ALU.add)
                    V(P, "activation", [ss], [ss], eng="act", out=ss[0:n, :], in_=ss[0:n, :], func=AF.Sqrt)
                    V(P, "reciprocal", [ss], [ss], out=ss[0:n, :], in_=ss[0:n, :])
                    V(P, "scalar_tensor_tensor", [cq, ss, qg], [cqb], out=cqb[0:n, :], in0=cq[0:n, :], scalar=ss[0:n, 0:1], in1=qg[0:n, :],
                      op0=ALU.mult, op1=ALU.mult)
                    t_ = tp[(ti + 1) % 2]
                    for kc in range(3):
                        TP(P, t_[:, kc, 0:n], cqb[0:n, kc * 128:(kc + 1) * 128], idb[0:n, 0:n], [cqb, idb], [t_], inc=(kc == 2))
                    V(P, "tensor_copy", [t_], [cqT], out=cqT[:, :, 0:n], in_=t_[:, 0:3, 0:n])
                    for h4 in range(2):
                        g_ = nxt()
                        for hh in range(4):
                            h = h4 * 4 + hh
                            for kc in range(3):
                                MM(P, g_[:, hh * 128:hh * 128 + n], wun[:, kc, h * 128:(h + 1) * 128], cqT[:, kc, 0:n], kc == 0, kc == 2, [wun, cqT], [g_])
                        cast(qnT[:, h4 * 4:h4 * 4 + 4, 0:n], g_[:, :].rearrange("p (h q) -> p h q", h=4)[:, :, 0:n], [g_], [qnT])
                    for cc in range(2):
                        for h4 in range(2):
                            g_ = nxt()
                            for hh in range(4):
                                h = h4 * 4 + hh
                                MM(P, g_[:, hh * 128:hh * 128 + n], wukT[:, h, cc * 128:(cc + 1) * 128], qnT[:, h, 0:n], True, True, [wukT, qnT], [g_])
                            V(P, "activation", [g_], [qaT], eng="act", out=qaT[:, cc, h4 * 4:h4 * 4 + 4, 0:n],
                              in_=g_[:, :].rearrange("p (h q) -> p h q", h=4)[:, :, 0:n], func=AF.Copy, scale=ATTN_SCALE)
                    for (wsrc, dstb) in ((wup, qpA), (wur, qpB)):
                        for h4 in range(2):
                            g_ = nxt()
                            for hh in range(4):
                                h = h4 * 4 + hh
                                for kc in range(3):
                                    MM(P, g_[0:64, hh * 128:hh * 128 + n], wsrc[:, kc, h * 64:(h + 1) * 64], cqT[:, kc, 0:n], kc == 0, kc == 2, [wsrc, cqT], [g_])
                            V(P, "tensor_copy", [g_], [dstb], out=dstb[0:64, h4 * 4:h4 * 4 + 4, 0:n],
                              in_=g_[0:64, :].rearrange("p (h q) -> p h q", h=4)[:, :, 0:n])
                    V(P, "tensor_tensor", [qpA, rcs], [qpA], out=qpA[:, :, 0:n], in0=qpA[:, :, 0:n], in1=rcs[:, 0:n].unsqueeze(1).to_broadcast([64, 8, n]), op=ALU.mult)
                    V(P, "tensor_tensor", [qpB, rsn], [qpB], out=qpB[:, :, 0:n], in0=qpB[:, :, 0:n], in1=rsn[:, 0:n].unsqueeze(1).to_broadcast([64, 8, n]), op=ALU.mult)
                    V(P, "tensor_tensor", [qpA, qpB], [qpA], out=qpA[:, :, 0:n], in0=qpA[:, :, 0:n], in1=qpB[:, :, 0:n], op=ALU.add)
                    V(P, "activation", [qpA], [qpT], eng="act", out=qpT[:, :, 0:n], in_=qpA[:, :, 0:n], func=AF.Copy, scale=ATTN_SCALE)
                    nkt = (nkeys + 127) // 128
                    blocks = [(bi_, k0, min(512, nkeys - k0)) for bi_, k0 in enumerate(range(0, nkeys, 512))]
                    nblk = len(blocks)

                    def score_mm(g_, h, k0, kw):
                        MM(P, g_[0:n, 0:kw], qaT[:, 0, h, 0:n], cT_[:, 0, k0:k0 + kw], True, False, [qaT, cT_], [g_])
                        MM(P, g_[0:n, 0:kw], qaT[:, 1, h, 0:n], cT_[:, 1, k0:k0 + kw], False, False, [qaT, cT_], [g_])
                        MM(P, g_[0:n, 0:kw], qpT[0:64, h, 0:n], kpT_[0:64, k0:k0 + kw], False, True, [qpT, kpT_], [g_])

                    def pass1_thunks(h):
                        bm_ = bms[h % 2]; mh = ms[h % 2]
                        th = []
                        for (bi_, k0, kw) in blocks:
                            def f(bi_=bi_, k0=k0, kw=kw):
                                g_ = nxt()
                                score_mm(g_, h, k0, kw)
                                V(P, "tensor_reduce", [g_], [bm_], out=bm_[0:n, bi_:bi_ + 1], in_=g_[0:n, 0:kw], axis=AX.X, op=ALU.max)
                            th.append(f)

                        def fin():
                            V(P, "tensor_reduce", [bm_], [mh], out=mh[0:n, :], in_=bm_[0:n, 0:nblk], axis=AX.X, op=ALU.max)
                            V(P, "tensor_scalar", [mh], [mh], out=mh[0:n, :], in0=mh[0:n, :], scalar1=-1.0, scalar2=None, op0=ALU.mult)
                        th.append(fin)
                        return th

                    def pass2(h, extra):
                        mh = ms[h % 2]; dn = denbs[h % 2]; ocp = mxp[h % 2]

                        def A(b):
                            bi_, k0, kw = blocks[b]
                            g_ = nxt()
                            pb_ = Pb[b % 3]
                            score_mm(g_, h, k0, kw)
                            if prompt and k0 + kw == nkeys:
                                V(P, "tensor_copy", [g_], [S], out=S[0:n, 0:kw], in_=g_[0:n, 0:kw])
                                V(P, "tensor_tensor", [S, mb], [S], out=S[0:n, kw - 256:kw], in0=S[0:n, kw - 256:kw], in1=mb[0:n, :], op=ALU.add)
                                V(P, "activation", [S, mh], [pb_, dn], eng="act", out=pb_[0:n, 0:kw], in_=S[0:n, 0:kw], func=AF.Exp,
                                  bias=mh[0:n, 0:1], scale=1.0, accum_out=dn[0:n, bi_:bi_ + 1])
                            else:
                                V(P, "activation", [g_, mh], [pb_, dn], eng="act", out=pb_[0:n, 0:kw], in_=g_[0:n, 0:kw], func=AF.Exp,
                                  bias=mh[0:n, 0:1], scale=1.0, accum_out=dn[0:n, bi_:bi_ + 1])

                        def B(b):
                            bi_, k0, kw = blocks[b]
                            pb_ = Pb[b % 3]; t_ = tp[b % 2]; pt_ = PT[b % 3]
                            ne = (kw + 127) // 128
                            for kk in range(ne):
                                nk = min(128, kw - kk * 128)
                                TP(P, t_[0:nk, kk, 0:n], pb_[0:n, kk * 128:kk * 128 + nk], idb[0:n, 0:n], [pb_, idb], [t_], inc=(kk == ne - 1))
                            nlast = kw - (ne - 1) * 128
                            nfull = ne if nlast == 128 else ne - 1
                            if nfull > 0:
                                V(P, "tensor_copy", [t_], [pt_], out=pt_[:, 0:nfull, 0:n], in_=t_[:, 0:nfull, 0:n])
                            if nlast < 128:
                                V(P, "tensor_copy", [t_], [pt_], out=pt_[0:nlast, ne - 1, 0:n], in_=t_[0:nlast, ne - 1, 0:n])

                        def C(b):
                            bi_, k0, kw = blocks[b]
                            pt_ = PT[b % 3]
                            ne = (kw + 127) // 128
                            for kk in range(ne):
                                kt = k0 // 128 + kk
                                nk = min(128, kw - kk * 128)
                                MM(P, ocp[0:n, 0:256], pt_[0:nk, kk, 0:n], ctok_[0:nk, kt, :], kt == 0, kt == nkt - 1, [pt_, ctok_], [ocp])
                        for step in range(nblk + 2):
                            if step < nblk:
                                A(step)
                            if 0 <= step - 1 < nblk:
                                B(step - 1)
                            if 0 <= step - 2 < nblk:
                                C(step - 2)
                            if extra:
                                extra.pop(0)()
                        while extra:
                            extra.pop(0)()
                        V(P, "tensor_reduce", [dn], [den], out=den[0:n, :], in_=dn[0:n, 0:nblk], axis=AX.X, op=ALU.add)
                        V(P, "reciprocal", [den], [den], out=den[0:n, :], in_=den[0:n, :])
                        V(P, "tensor_scalar", [ocp, den], [oc], out=oc[0:n, h, :], in0=ocp[0:n, 0:256], scalar1=den[0:n, 0:1], scalar2=None, op0=ALU.mult)
                    for f in pass1_thunks(0):
                        f()
                    for h in range(8):
                        pass2(h, pass1_thunks(h + 1) if h < 7 else [])
                    for half in range(2):
                        t_ = tp[half]
                        for kk in range(8):
                            c = half * 8 + kk
                            TP(P, t_[:, kk, 0:n], oc[0:n, c // 2, (c % 2) * 128:(c % 2) * 128 + 128], idb[0:n, 0:n], [oc, idb], [t_], inc=(kk == 7))
                        V(P, "tensor_copy", [t_], [ocT], out=ocT[:, half * 8:half * 8 + 8, 0:n], in_=t_[:, :, 0:n])
                    for h4 in range(2):
                        g_ = nxt()
                        for hh in range(4):
                            h = h4 * 4 + hh
                            for cc in range(2):
                                MM(P, g_[:, hh * 128:hh * 128 + n], wukv[:, cc, h * 256 + 128:h * 256 + 256], ocT[:, h * 2 + cc, 0:n], cc == 0, cc == 1, [wukv, ocT], [g_])
                        cast(oT[:, h4 * 4:h4 * 4 + 4, 0:n], g_[:, :].rearrange("p (h q) -> p h q", h=4)[:, :, 0:n], [g_], [oT])
                    for nb in range(2):
                        for h in range(8):
                            MM(P, mxp[nb][0:n, :], oT[:, h, 0:n], wo[:, h, nb * 512:(nb + 1) * 512], h == 0, h == 7, [oT, wo], [mxp[nb]])
                        V(P, "scalar_tensor_tensor", [x, mxp[nb]], [x], out=x[0:n, nb * 512:(nb + 1) * 512], in0=x[0:n, nb * 512:(nb + 1) * 512],
                          scalar=ALPHA, in1=mxp[nb][0:n, :], op0=ALU.mult, op1=ALU.add)
                    layer_norm(x, n, D, l1g, l1b, st6, mv, rstd)
                    dap, dbuf = src_tile(dst, ti)
                    DMA(P, "sp", dap, x[0:n, :], [x], [dbuf], owner=x)
                P.barrier()
                pass
                P.flush()

        phase_G(0, "in", "b")
        phase_P(0, "b", "a")
        phase_G(1, "a", "b")
        phase_P(1, "b", "a")
        phase_K("a")
        phase_A(0, 2, "a", "b")
        phase_P(2, "b", "a")
        phase_A(1, 3, "a", "b")
        phase_P(3, "b", "y")
        print("n_inst", P.n_inst)
    return nc


def _host_inputs(SEQ, PAST, inp):
    NT = SEQ // 256
    NPT = NT * 128
    f32 = np.float32
    xp = np.asarray(inp["x_prompt"], f32)
    xs = np.asarray(inp["x_sample"], f32)
    B = xp.shape[0]
    inv = (1.0 / (10000.0 ** (np.arange(0, 64, 2, dtype=np.float32) / 64.0))).astype(np.float32)
    shared = {}
    for k in ("ln1_g", "ln1_b", "ln2_g", "ln2_b", "gm_w_in", "gm_b_in", "gm_ln_g", "gm_ln_b", "gm_w_s", "gm_w_out",
              "mla_w_dkv", "mla_w_ukv", "mla_w_dq", "mla_q_norm_g", "mla_w_o", "peer_w_q", "peer_subkeys", "peer_u", "peer_v"):
        shared[k] = np.ascontiguousarray(np.asarray(inp[k], f32))
    shared["peer_u"] = shared["peer_u"].reshape(DEPTH * NEXP, D)
    shared["peer_v"] = shared["peer_v"].reshape(DEPTH * NEXP, D)
    shared["gm_b_sT"] = np.ascontiguousarray(np.transpose(np.asarray(inp["gm_b_s"], f32), (0, 2, 1)))
    shared["mla_kv_norm_g"] = np.asarray(inp["mla_kv_norm_g"], f32).reshape(1, -1)
    wuq = np.asarray(inp["mla_w_uq"], f32).reshape(2, Q_LORA, NH, 192)
    shared["w_uq_n"] = np.ascontiguousarray(wuq[..., :128].reshape(2, Q_LORA, 1024))
    shared["w_uq_p"] = np.ascontiguousarray(wuq[..., 128:].reshape(2, Q_LORA, 512))
    shared["w_uq_r"] = np.ascontiguousarray(np.concatenate([wuq[..., 160:192], wuq[..., 128:160]], -1).reshape(2, Q_LORA, 512))
    shared["ident"] = np.eye(128, dtype=f32)
    shared["iota16"] = np.tile(np.arange(16, dtype=f32)[None], (128, 1))
    maps = []
    for c in range(2 * B):
        b, r = c // 2, c % 2
        gt = [2 * j + r for j in range(NT)]
        xin = np.concatenate([xp[b].reshape(-1, 128, D)[gt].reshape(NPT, D), xs[c]], 0)
        pos = np.concatenate([(np.array(gt)[:, None] * 128 + np.arange(128)[None]).reshape(-1), PAST + np.arange(NS)]).astype(np.float32)
        ang = pos[:, None] * inv[None, :]
        cs, sn = np.cos(ang).astype(f32), np.sin(ang).astype(f32)
        qi = np.arange(128)[:, None]; kk = np.arange(256)[None, :]
        kchunk = kk // 64
        qchunk = (r * 128 + qi) // 64
        mask = np.where(kchunk <= qchunk, 0.0, -1e30).astype(f32)
        m = dict(shared)
        m.update(xin=np.ascontiguousarray(xin), cache_ckv=np.ascontiguousarray(np.asarray(inp["cache_ckv"], f32)[c]),
                 cache_kpe=np.ascontiguousarray(np.asarray(inp["cache_kpe"], f32)[c]),
                 rope_tok=np.ascontiguousarray(np.concatenate([cs, sn], 1)),
                 ropeT_cs=np.ascontiguousarray(np.concatenate([cs.T, cs.T], 0)),
                 ropeT_sn=np.ascontiguousarray(np.concatenate([-sn.T, sn.T], 0)),
                 maskb=mask)
        maps.append(m)
    return maps


def _assemble(SEQ, PAST, res, B):
    NT = SEQ // 256
    NPT = NT * 128
    f32 = np.float32
    y_p = np.zeros((B, SEQ, D), f32); ckv_p = np.zeros((B, SEQ, KV_LORA), f32); kpe_p = np.zeros((B, SEQ, QK_ROPE), f32)
    y_s = np.zeros((2 * B, NS, D), f32); gv = np.zeros((N_A, 2 * B, NS, GH), f32)
    ckv_s = np.zeros((2 * B, NS, KV_LORA), f32); kpe_s = np.zeros((2 * B, NS, QK_ROPE), f32)
    for c in range(2 * B):
        b, r = c // 2, c % 2
        o = res[c]
        for arr, key, w in ((y_p, "y_out", D), (ckv_p, "ckv_out", KV_LORA), (kpe_p, "kpe_out", QK_ROPE)):
            arr[b].reshape(SEQ // 128, 128, w)[r::2] = o[key][:NPT].reshape(NT, 128, w)
        y_s[c] = o["y_out"][NPT:]; ckv_s[c] = o["ckv_out"][NPT:]; kpe_s[c] = o["kpe_out"][NPT:]
        gv[:, c] = o["gv_out"]
    return (y_p, y_s, gv, ckv_p, kpe_p, ckv_s, kpe_s)


def run(SEQ, PAST, inp):
    nc = build(SEQ, PAST)
    maps = _host_inputs(SEQ, PAST, inp)
    res = run_bass_kernel_spmd(nc, maps, core_ids=list(range(8)))
    return _assemble(SEQ, PAST, res.results, np.asarray(inp["x_prompt"]).shape[0])


def kernel(**inputs):
    return run(8192, 2048, inputs)
```

```python
import math
import numpy as np
from contextlib import ExitStack
import concourse.bass as bass
import concourse.mybir as mybir
from concourse.bass_utils import run_bass_kernel_spmd

F32 = mybir.dt.float32
BF16 = mybir.dt.bfloat16
I32 = mybir.dt.int32
U32 = mybir.dt.uint32
ALU = mybir.AluOpType
AF = mybir.ActivationFunctionType
AX = mybir.AxisListType

D = 1024
DEPTH = 4
N_A = 2
ALPHA = (2.0 * DEPTH) ** 0.25
LN_EPS = 1e-5
RMS_EPS = 1e-6
GH = 2048
KV_LORA = 256
QK_ROPE = 64
Q_LORA = 384
NH = 8
ATTN_SCALE = (128 + 64) ** -0.5
NEXP = 16384
NS = 32
NG = 22
GS = 8


class Buf:
    __slots__ = ("name", "t", "last_w", "readers", "dsem", "dcnt")

    def __init__(self, name, t=None):
        self.name = name
        self.t = t
        self.last_w = None
        self.readers = {}
        self.dsem = None
        self.dcnt = 0

    def __getitem__(self, idx):
        return self.t[idx]


class Prog:
    ENG = ("pe", "act", "dve", "pool", "sp")

    def __init__(self, nc, stack, n_dma_sems=64):
        self.nc = nc
        self.q = {e: [] for e in self.ENG}
        self.cnt = {e: 0 for e in self.ENG}
        self.waited = {e: {} for e in self.ENG}
        self.sems = {}
        for e in self.ENG:
            self.sems["e_" + e] = stack.enter_context(nc.semaphore("e_" + e))
        self.free = []
        for i in range(n_dma_sems):
            k = "d%d" % i
            self.sems[k] = stack.enter_context(nc.semaphore(k))
            self.free.append(k)
        self.dval = {k: 0 for k in self.free}
        self.inuse = []
        self.n_inst = 0
        self.epoch = 0
        self.nreset = 0
        self.sems["B1"] = stack.enter_context(nc.semaphore("B1"))
        self.sems["B2"] = stack.enter_context(nc.semaphore("B2"))

    def _dsem(self, b):
        if b.dsem is None:
            k = self.free.pop()
            b.dsem = k
            b.dcnt = self.dval[k]
            self.inuse.append(b)
        return b.dsem

    def _deps(self, eng, reads, writes):
        deps = {}

        def add(k, v):
            if k in self.dval:
                v = self.dval[k]
            if v > deps.get(k, 0):
                deps[k] = v
        ep = self.epoch
        for b in reads:
            if b.last_w is not None and b.last_w[2] == ep:
                add(b.last_w[0], b.last_w[1])
        for b in writes:
            if b.last_w is not None and b.last_w[2] == ep:
                add(b.last_w[0], b.last_w[1])
            for k, (v, e_) in b.readers.items():
                if e_ == ep:
                    add(k, v)
        out = []
        w = self.waited[eng]
        for k, v in deps.items():
            if eng == "pe" and k == "e_pe":
                continue
            if w.get(k, 0) >= v:
                continue
            w[k] = v
            out.append((k, v))
        return out

    def op(self, eng, fn, reads=(), writes=(), inc=True):
        waits = self._deps(eng, reads, writes)
        key = "e_" + eng
        if inc:
            self.cnt[eng] += 1
            val = self.cnt[eng]
        else:
            val = self.cnt[eng] + 1
        self.q[eng].append((waits, fn, key if inc else None, 1))
        for b in writes:
            b.last_w = (key, val, self.epoch)
            b.readers = {}
        for b in reads:
            o_ = b.readers.get(key)
            if o_ is None or o_[1] != self.epoch or o_[0] < val:
                b.readers[key] = (val, self.epoch)
        self.n_inst += 1

    def dma(self, eng, fn, reads=(), writes=(), owner=None, inc=16):
        waits = self._deps(eng, reads, writes)
        if owner is None:
            owner = writes[0]
        key = self._dsem(owner)
        owner.dcnt += inc
        val = owner.dcnt
        self.dval[key] = val
        self.q[eng].append((waits, fn, key, inc))
        for b in writes:
            b.last_w = (key, val, self.epoch)
            b.readers = {}
        for b in reads:
            o_ = b.readers.get(key)
            if o_ is None or o_[1] != self.epoch or o_[0] < val:
                b.readers[key] = (val, self.epoch)
        self.n_inst += 1

    def barrier(self):
        for e in self.ENG:
            waits = []
            w = self.waited[e]
            for k, v in self.dval.items():
                if v > 0 and w.get(k, 0) < v:
                    waits.append((k, v))
                    w[k] = v
            for e2 in self.ENG:
                k = "e_" + e2
                if e2 != e and self.cnt[e2] > w.get(k, 0):
                    waits.append((k, self.cnt[e2]))
                    w[k] = self.cnt[e2]
            self.q[e].append((waits, None, None, 0))
        for b in self.inuse:
            self.free.append(b.dsem)
            b.dsem = None
        self.inuse = []
        self.epoch += 1

    def reset(self):
        self.nreset += 1
        k_ = self.nreset
        B1, B2 = self.sems["B1"], self.sems["B2"]
        for e in self.ENG:
            self.q[e].append(([], lambda eng: eng.sem_inc(B1, 1), None, 0))
        self.q["sp"].append(([("B1", 5 * k_)], None, None, 0))
        for k, sm in self.sems.items():
            if k in ("B1", "B2"):
                continue
            self.q["sp"].append(([], lambda eng, sm=sm: eng.sem_clear(sm), None, 0))
        self.q["sp"].append(([], lambda eng: eng.sem_inc(B2, 1), None, 0))
        for e in self.ENG:
            self.q[e].append(([("B2", k_)], None, None, 0))
        self.cnt = {e: 0 for e in self.ENG}
        self.waited = {e: {} for e in self.ENG}
        for k in self.dval:
            self.dval[k] = 0
        self.epoch += 1

    def flush(self):
        nc = self.nc
        sems = self.sems
        with nc.Block() as block:
            engmap = {"pe": block.tensor, "act": block.scalar, "dve": block.vector,
                      "pool": block.gpsimd, "sp": block.sync}
            for e in self.ENG:
                items = self.q[e]

                def body(engine, items=items):
                    for waits, fn, key, inc in items:
                        for k, v in waits:
                            engine.wait_ge(sems[k], v)
                        if fn is None:
                            continue
                        ins = fn(engine)
                        if key is not None:
                            ins.then_inc(sems[key], inc)
                engmap[e](body)
        self.q = {e: [] for e in self.ENG}


def V(P, name, R, W, eng="dve", **kw):
    P.op(eng, lambda e: getattr(e, name)(**kw), reads=R, writes=W)


def MM(P, out, lhsT, rhs, start, stop, R, W, inc=None):
    P.op("pe", lambda e: e.matmul(out, lhsT=lhsT, rhs=rhs, start=start, stop=stop), reads=R, writes=W,
         inc=(stop if inc is None else inc))


def TP(P, out, in_, ident, R, W, inc=True):
    P.op("pe", lambda e: e.transpose(out=out, in_=in_, identity=ident), reads=R, writes=W, inc=inc)


def DMA(P, q, out, in_, R, W, owner=None):
    P.dma(q, lambda e: e.dma_start(out=out, in_=in_), reads=R, writes=W, owner=owner)


def build(SEQ, PAST):
    NT = SEQ // 256
    NPT = NT * 128
    NTOK = NPT + NS
    tiles = [(j * 128, 128) for j in range(NT)] + [(NPT, NS)]
    nc = bass.Bass("TRN2", target_bir_lowering=False)

    def din(name, shape, dt=F32):
        return nc.dram_tensor(name, list(shape), dt, kind="ExternalInput").ap()

    def dout(name, shape, dt=F32):
        return nc.dram_tensor(name, list(shape), dt, kind="ExternalOutput").ap()

    xin = din("xin", [NTOK, D])
    cache_ckv = din("cache_ckv", [PAST, KV_LORA])
    cache_kpe = din("cache_kpe", [PAST, QK_ROPE])
    rope_tok = din("rope_tok", [NTOK, 64])
    ropeT_cs = din("ropeT_cs", [64, NTOK])
    ropeT_sn = din("ropeT_sn", [64, NTOK])
    maskb = din("maskb", [128, 256])
    ident_d = din("ident", [128, 128])
    iota_d = din("iota16", [128, 16])
    ln1_g = din("ln1_g", [DEPTH, D]); ln1_b = din("ln1_b", [DEPTH, D])
    ln2_g = din("ln2_g", [DEPTH, D]); ln2_b = din("ln2_b", [DEPTH, D])
    gm_w_in = din("gm_w_in", [N_A, D, 2 * GH]); gm_b_in = din("gm_b_in", [N_A, 2 * GH])
    gm_ln_g = din("gm_ln_g", [N_A, GH]); gm_ln_b = din("gm_ln_b", [N_A, GH])
    gm_w_s = din("gm_w_s", [N_A, 8, 128, 128]); gm_b_sT = din("gm_b_sT", [N_A, 128, 8])
    gm_w_out = din("gm_w_out", [N_A, GH, D])
    w_dkv = din("mla_w_dkv", [D, 320]); kv_g = din("mla_kv_norm_g", [1, KV_LORA])
    w_ukv = din("mla_w_ukv", [KV_LORA, 2048])
    w_dq = din("mla_w_dq", [2, D, Q_LORA]); q_g = din("mla_q_norm_g", [2, Q_LORA])
    w_uq_n = din("w_uq_n", [2, Q_LORA, 1024]); w_uq_p = din("w_uq_p", [2, Q_LORA, 512])
    w_uq_r = din("w_uq_r", [2, Q_LORA, 512])
    w_o = din("mla_w_o", [2, 1024, D])
    peer_w_q = din("peer_w_q", [DEPTH, D, 2048]); peer_sk = din("peer_subkeys", [DEPTH, 2, 128, 128])
    peer_u_f = din("peer_u", [DEPTH * NEXP, D]); peer_v_f = din("peer_v", [DEPTH * NEXP, D])

    y_out = dout("y_out", [NTOK, D])
    gv_out = dout("gv_out", [N_A, NS, GH])
    ckv_out = dout("ckv_out", [NTOK, KV_LORA])
    kpe_out = dout("kpe_out", [NTOK, QK_ROPE])

    xa = nc.dram_tensor("xa", [NTOK, D], F32).ap()
    xb = nc.dram_tensor("xb", [NTOK, D], F32).ap()
    CHT = 4
    NCH = (NT + CHT - 1) // CHT
    chn = [min(CHT, NT - ch * CHT) for ch in range(NCH)]
    exin = [nc.dram_tensor("exin%d" % ch, [chn[ch] * 128, 320], F32).ap() for ch in range(NCH)]
    exout = [nc.dram_tensor("exout%d" % ch, [2 * chn[ch] * 128, 320], F32).ap() for ch in range(NCH)]
    samp_ck = nc.dram_tensor("samp_ck", [NS, 320], F32).ap()
    uvb = nc.dram_tensor("uvb", [DEPTH * NEXP, 2 * D], BF16).ap()
    uvb_b = Buf("uvb")

    xa_b = [Buf("xa%d" % i) for i in range(NT + 1)]
    xb_b = [Buf("xb%d" % i) for i in range(NT + 1)]
    exin_b = [Buf("exin%d" % ch) for ch in range(NCH)]; exout_b = [Buf("exout%d" % ch) for ch in range(NCH)]; samp_b = Buf("samp")
    outs_b = Buf("outs")

    with ExitStack() as top:
        P = Prog(nc, top)

        uid = [0]

        def SB(st, name, shape, dt):
            uid[0] += 1
            return Buf(name, st.enter_context(nc.sbuf_tensor("sb%d_%s" % (uid[0], name), list(shape), dt)))

        def PS(st, name, shape, dt):
            uid[0] += 1
            return Buf(name, st.enter_context(nc.psum_tensor("ps%d_%s" % (uid[0], name), list(shape), dt)))

        idf = SB(top, "idf", [128, 128], F32)
        idb = SB(top, "idb", [128, 128], BF16)
        iota16 = SB(top, "iota16", [128, 16], F32)
        ones_bf = SB(top, "ones_bf", [1, 128], BF16)
        DMA(P, "sp", idf[:], ident_d, [], [idf])
        DMA(P, "sp", iota16[:], iota_d, [], [iota16])
        V(P, "tensor_copy", [idf], [idb], out=idb[:], in_=idf[:])
        V(P, "memset", [], [ones_bf], ap=ones_bf[:], constant=1.0)
        cast_rr = [0]

        def cast(out, in_, R, W):
            cast_rr[0] += 1
            if cast_rr[0] % 2:
                V(P, "tensor_copy", R, W, out=out, in_=in_)
            else:
                V(P, "activation", R, W, eng="act", out=out, in_=in_, func=AF.Copy)

        def load_w(st_bufs, dram2d, K, cols, dst, SW):
            i = 0
            for kc in range((K + 127) // 128):
                kr = min(128, K - kc * 128)
                for c0 in range(0, cols, SW):
                    w = min(SW, cols - c0)
                    sg = st_bufs[i % len(st_bufs)]
                    i += 1
                    DMA(P, "sp" if i % 2 else "act", sg[0:kr, 0:w], dram2d[kc * 128:kc * 128 + kr, c0:c0 + w], [], [sg])
                    cast(dst[0:kr, kc, c0:c0 + w], sg[0:kr, 0:w], [sg], [dst])

        def bcast_row(dst, row_ap):
            DMA(P, "sp", dst[:], row_ap.partition_broadcast(128), [], [dst])

        def make_xT(x, n, xbf, tp, xT, KC=8):
            cast(xbf[0:n, 0:KC * 128], x[0:n, 0:KC * 128], [x], [xbf])
            for kc in range(KC):
                TP(P, tp[:, kc, 0:n], xbf[0:n, kc * 128:(kc + 1) * 128], idb[0:n, 0:n], [xbf, idb], [tp], inc=(kc == KC - 1))
            V(P, "tensor_copy", [tp], [xT], out=xT[:, 0:KC, 0:n], in_=tp[:, 0:KC, 0:n])

        def layer_norm(t, n, Dn, gB, bB, st6, mv, rstd):
            C = Dn // 512
            for c in range(C):
                V(P, "bn_stats", [t], [st6], out=st6[0:n, c * 6:(c + 1) * 6], in_=t[0:n, c * 512:(c + 1) * 512])
            V(P, "bn_aggr", [st6], [mv], out=mv[0:n, :], in_=st6[0:n, 0:C * 6])
            V(P, "tensor_scalar", [mv], [rstd], out=rstd[0:n, :], in0=mv[0:n, 1:2], scalar1=LN_EPS, scalar2=None, op0=ALU.add)
            V(P, "activation", [rstd], [rstd], eng="act", out=rstd[0:n, :], in_=rstd[0:n, :], func=AF.Sqrt)
            V(P, "reciprocal", [rstd], [rstd], out=rstd[0:n, :], in_=rstd[0:n, :])
            V(P, "tensor_scalar", [t, mv, rstd], [t], out=t[0:n, 0:Dn], in0=t[0:n, 0:Dn], scalar1=mv[0:n, 0:1],
              scalar2=rstd[0:n, 0:1], op0=ALU.subtract, op1=ALU.mult)
            V(P, "tensor_tensor", [t, gB], [t], out=t[0:n, 0:Dn], in0=t[0:n, 0:Dn], in1=gB[0:n, 0:Dn], op=ALU.mult)
            V(P, "tensor_tensor", [t, bB], [t], out=t[0:n, 0:Dn], in0=t[0:n, 0:Dn], in1=bB[0:n, 0:Dn], op=ALU.add)

        def src_tile(l_src, ti):
            r0, n = tiles[ti]
            if l_src == "in":
                return xin[r0:r0 + n, :], None
            if l_src == "a":
                return xa[r0:r0 + n, :], xa_b[ti]
            return xb[r0:r0 + n, :], xb_b[ti]

        def phase_C():
            with ExitStack() as st:
                RR = 4
                su = [SB(st, "c_su%d" % i, [128, RR, D], F32) for i in range(2)]
                sv = [SB(st, "c_sv%d" % i, [128, RR, D], F32) for i in range(2)]
                ot = [st.enter_context(nc.sbuf_tensor("c_ot%d" % i, [128, RR, 2 * D], BF16)) for i in range(2)]
                ou = [Buf("c_ou%d" % i, ot[i]) for i in range(2)]
                ov = [Buf("c_ov%d" % i, ot[i]) for i in range(2)]
                uview = peer_u_f.rearrange("(p r) d -> p r d", p=128)
                vview = peer_v_f.rearrange("(p r) d -> p r d", p=128)
                oview = uvb.rearrange("(p r) d -> p r d", p=128)
                npc = (DEPTH * NEXP // 128) // RR
                for i in range(npc):
                    a = su[i % 2]; b = sv[i % 2]; o1 = ou[i % 2]; o2 = ov[i % 2]
                    DMA(P, "sp", a[:], uview[:, i * RR:(i + 1) * RR, :], [], [a])
                    DMA(P, "act", b[:], vview[:, i * RR:(i + 1) * RR, :], [], [b])
                    V(P, "tensor_copy", [a], [o1], out=o1[:, :, 0:D], in_=a[:])
                    V(P, "activation", [b], [o2], eng="act", out=o2[:, :, D:2 * D], in_=b[:], func=AF.Copy)
                    DMA(P, "sp", oview[:, i * RR:(i + 1) * RR, :], o1[:], [o1, o2], [uvb_b], owner=o1)
                P.barrier()
                P.flush()

        def phase_G(l, src, dst):
            with ExitStack() as st:
                w_in = SB(st, "g_w_in", [128, 8, 4096], BF16)
                w_out = SB(st, "g_w_out", [128, 16, 1024], BF16)
                stg = [SB(st, "g_stg%d" % i, [128, 2048], F32) for i in range(2)]
                b_in = SB(st, "g_b_in", [1, 4096], BF16)
                glg = SB(st, "g_glg", [128, GH], F32); glb = SB(st, "g_glb", [128, GH], F32)
                l1g = SB(st, "g_l1g", [128, D], F32); l1b = SB(st, "g_l1b", [128, D], F32)
                ws_f = SB(st, "g_ws_f", [128, 8, 128], F32)
                ws_b = SB(st, "g_ws_b", [128, 8, 128], BF16)
                wsT = SB(st, "g_wsT", [128, 8, 128], BF16)
                bsT = SB(st, "g_bsT", [128, 8], F32)
                xs = [SB(st, "g_x%d" % i, [128, D], F32) for i in range(2)]
                xbf = SB(st, "g_xbf", [128, D], BF16)
                xT = SB(st, "g_xT", [128, 8, 128], BF16)
                u = SB(st, "g_u", [128, GH], BF16)
                v = SB(st, "g_v", [128, GH], F32)
                vb = SB(st, "g_vb", [128, GH], BF16)
                s = SB(st, "g_s", [128, GH], BF16)
                sT = SB(st, "g_sT", [128, 16, 128], BF16)
                st6 = SB(st, "g_st6", [128, 24], F32); mv = SB(st, "g_mv", [128, 2], F32); rstd = SB(st, "g_rstd", [128, 1], F32)
                tp = [PS(st, "g_tp%d" % i, [128, 8, 128], BF16) for i in range(2)]
                zp = [PS(st, "g_zp%d" % i, [128, 512], F32) for i in range(2)]
                svp = [PS(st, "g_svp%d" % i, [128, 512], F32) for i in range(2)]
                mxp = [PS(st, "g_mxp%d" % i, [128, 512], F32) for i in range(2)]

                load_w(stg, gm_w_in[l], D, 4096, w_in, 2048)
                load_w(stg, gm_w_out[l], GH, D, w_out, 2048)
                for hb in range(2):
                    DMA(P, "sp", stg[hb][0:1, :], gm_b_in[l:l + 1, hb * 2048:(hb + 1) * 2048], [], [stg[hb]])
                    V(P, "tensor_copy", [stg[hb]], [b_in], out=b_in[0:1, hb * 2048:(hb + 1) * 2048], in_=stg[hb][0:1, :])
                bcast_row(glg, gm_ln_g[l, :]); bcast_row(glb, gm_ln_b[l, :])
                bcast_row(l1g, ln1_g[l, :]); bcast_row(l1b, ln1_b[l, :])
                DMA(P, "sp", ws_f[:], gm_w_s[l].rearrange("g i j -> i g j"), [], [ws_f])
                DMA(P, "sp", bsT[:], gm_b_sT[l], [], [bsT])
                V(P, "memset", [ws_f], [ws_f], ap=ws_f[0:64, :, 64:128], constant=0.0)
                V(P, "tensor_copy", [ws_f], [ws_b], out=ws_b[:], in_=ws_f[:])
                for g in range(8):
                    TP(P, tp[0][:, g, :], ws_b[:, g, :], idb[:], [ws_b, idb], [tp[0]], inc=(g == 7))
                V(P, "tensor_copy", [tp[0]], [wsT], out=wsT[:], in_=tp[0][:])

                ot = [st.enter_context(nc.sbuf_tensor("g_ot%d_%d" % (l, i), [128, 2, 2 * D], BF16)) for i in range(2)]
                ou = [Buf("g_ou%d" % i, ot[i]) for i in range(2)]
                ov = [Buf("g_ov%d" % i, ot[i]) for i in range(2)]
                rlo = l * 2 * NEXP
                uview = peer_u_f[rlo:rlo + 2 * NEXP, :].rearrange("(p r) d -> p r d", p=128)
                vview = peer_v_f[rlo:rlo + 2 * NEXP, :].rearrange("(p r) d -> p r d", p=128)
                oview = uvb[rlo:rlo + 2 * NEXP, :].rearrange("(p r) d -> p r d", p=128)
                npc = (2 * NEXP // 128) // 2
                cpi = [0]

                def conv_pieces(k):
                    for _ in range(k):
                        i = cpi[0]
                        if i >= npc:
                            return
                        cpi[0] += 1
                        o1 = ou[i % 2]; o2 = ov[i % 2]
                        DMA(P, "sp", stg[0][:].rearrange("p (r d) -> p r d", r=2), uview[:, i * 2:(i + 1) * 2, :], [], [stg[0]])
                        DMA(P, "act", stg[1][:].rearrange("p (r d) -> p r d", r=2), vview[:, i * 2:(i + 1) * 2, :], [], [stg[1]])
                        V(P, "tensor_copy", [stg[0]], [o1], out=o1[:, :, 0:D], in_=stg[0][:].rearrange("p (r d) -> p r d", r=2))
                        V(P, "activation", [stg[1]], [o2], eng="act", out=o2[:, :, D:2 * D], in_=stg[1][:].rearrange("p (r d) -> p r d", r=2), func=AF.Copy)
                        DMA(P, "sp", oview[:, i * 2:(i + 1) * 2, :], o1[:], [o1, o2], [uvb_b], owner=o1)
                per_tile = (npc + len(tiles) - 1) // len(tiles)

                for ti, (r0, n) in enumerate(tiles):
                    x = xs[ti % 2]
                    sap, sbuf_ = src_tile(src, ti)
                    DMA(P, "sp", x[0:n, :], sap, [sbuf_] if sbuf_ else [], [x])
                    conv_pieces(per_tile)
                    make_xT(x, n, xbf, tp[ti % 2], xT)
                    for nb in range(8):
                        z = zp[nb % 2]
                        MM(P, z[0:n, :], ones_bf[0:1, 0:n], b_in[0:1, nb * 512:(nb + 1) * 512], True, False, [ones_bf, b_in], [z])
                        for kc in range(8):
                            MM(P, z[0:n, :], xT[:, kc, 0:n], w_in[:, kc, nb * 512:(nb + 1) * 512], False, kc == 7, [xT, w_in], [z])
                        if nb < 4:
                            V(P, "activation", [z], [u], eng="act", out=u[0:n, nb * 512:(nb + 1) * 512], in_=z[0:n, :], func=AF.Gelu)
                        else:
                            V(P, "activation", [z], [v], eng="act", out=v[0:n, (nb - 4) * 512:(nb - 3) * 512], in_=z[0:n, :], func=AF.Gelu)
                    layer_norm(v, n, GH, glg, glb, st6, mv, rstd)
                    if n == NS:
                        DMA(P, "sp", gv_out[l], v[0:n, :], [v], [outs_b], owner=v)
                    V(P, "activation", [v], [vb], eng="act", out=vb[0:n, :], in_=v[0:n, :], func=AF.Copy)
                    for g in range(8):
                        sp_ = svp[(g // 2) % 2]
                        MM(P, sp_[0:n, (g % 2) * 256:(g % 2) * 256 + 256], wsT[0:n, g, 0:n], vb[0:n, g * 256:(g + 1) * 256], True, True, [wsT, vb], [sp_])
                        V(P, "scalar_tensor_tensor", [sp_, bsT, u], [s], out=s[0:n, g * 256:(g + 1) * 256],
                          in0=sp_[0:n, (g % 2) * 256:(g % 2) * 256 + 256], scalar=bsT[0:n, g:g + 1], in1=u[0:n, g * 256:(g + 1) * 256],
                          op0=ALU.add, op1=ALU.mult)
                    for half in range(2):
                        t_ = tp[half]
                        for kc in range(8):
                            c = half * 8 + kc
                            TP(P, t_[:, kc, 0:n], s[0:n, c * 128:(c + 1) * 128], idb[0:n, 0:n], [s, idb], [t_], inc=(kc == 7))
                        V(P, "tensor_copy", [t_], [sT], out=sT[:, half * 8:half * 8 + 8, 0:n], in_=t_[:, :, 0:n])
                    for nb in range(2):
                        for kc in range(16):
                            MM(P, mxp[nb][0:n, :], sT[:, kc, 0:n], w_out[:, kc, nb * 512:(nb + 1) * 512], kc == 0, kc == 15, [sT, w_out], [mxp[nb]])
                        V(P, "scalar_tensor_tensor", [x, mxp[nb]], [x], out=x[0:n, nb * 512:(nb + 1) * 512], in0=x[0:n, nb * 512:(nb + 1) * 512],
                          scalar=ALPHA, in1=mxp[nb][0:n, :], op0=ALU.mult, op1=ALU.add)
                    layer_norm(x, n, D, l1g, l1b, st6, mv, rstd)
                    dap, dbuf = src_tile(dst, ti)
                    DMA(P, "sp", dap, x[0:n, :], [x], [dbuf], owner=x)
                P.barrier()
                pass
                P.flush()

        def phase_P(l, src, dst):
            with ExitStack() as st:
                w_q = SB(st, "p_w_q", [128, 8, 2048], BF16)
                sk_f = SB(st, "p_sk_f", [128, 2, 128], F32)
                sk_b = SB(st, "p_sk_b", [128, 2, 128], BF16)
                skT = SB(st, "p_skT", [128, 2, 128], BF16)
                l2g = SB(st, "p_l2g", [128, D], F32); l2b = SB(st, "p_l2b", [128, D], F32)
                xs = [SB(st, "p_x%d" % i, [128, D], F32) for i in range(2)]
                xbfs = [SB(st, "p_xbf%d" % i, [128, D], BF16) for i in range(2)]
                xT = SB(st, "p_xT", [128, 8, 128], BF16)
                qT = SB(st, "p_qT", [128, 16, 128], BF16)
                ssb = SB(st, "p_ssb", [128, 16, 128], F32)
                tv = SB(st, "p_tv", [128, 16, 16], F32)
                tix = SB(st, "p_tix", [128, 16, 16], U32)
                work = [SB(st, "p_work%d" % i, [128, 256], F32) for i in range(2)]
                cand = SB(st, "p_cand", [128, 8, 256], F32)
                sc = SB(st, "p_sc", [128, 8, 16], F32)
                sel = SB(st, "p_sel", [128, 8, 16], U32)
                gws = [SB(st, "p_gw%d" % i, [128, 8, 16], F32) for i in range(2)]
                ssum = SB(st, "p_ssum", [128, 8], F32)
                ai = SB(st, "p_ai", [128, 8, 16], I32); bi = SB(st, "p_bi", [128, 8, 16], I32)
                af = SB(st, "p_af", [128, 8, 16], F32); bf_ = SB(st, "p_bf", [128, 8, 16], F32)
                i1f = SB(st, "p_i1f", [128, 8, 16], F32); i2f = SB(st, "p_i2f", [128, 8, 16], F32)
                eq = SB(st, "p_eq", [128, 8, 256], F32)
                i1s = SB(st, "p_i1s", [128, 8, 16], F32); i2s = SB(st, "p_i2s", [128, 8, 16], F32)
                eidf = SB(st, "p_eidf", [128, 128], F32)
                eids = [SB(st, "p_eid%d" % i, [128, 128], I32) for i in range(2)]
                hdn = SB(st, "p_hdn", [128, 128], F32)
                aw = SB(st, "p_aw", [128, 128], F32)
                junk = SB(st, "p_junk", [128, D], BF16)
                dgs = [SB(st, "p_dg%d" % i, [128, GS, 128], BF16) for i in range(3)]
                prods = [SB(st, "p_prod%d" % i, [128, D], BF16) for i in range(4)]
                hslot = [Buf("p_hs%d" % i) for i in range(128)]
                awg = [Buf("p_awg%d" % i) for i in range(128 // GS)]
                st6 = SB(st, "p_st6", [128, 24], F32); mv = SB(st, "p_mv", [128, 2], F32); rstd = SB(st, "p_rstd", [128, 1], F32)
                tp = [PS(st, "p_tp%d" % i, [128, 8, 128], BF16) for i in range(2)]
                qp = [PS(st, "p_qp%d" % i, [128, 4, 128], F32) for i in range(2)]
                sp4 = [PS(st, "p_sp%d" % i, [128, 4, 128], F32) for i in range(2)] * 2
                accp = [PS(st, "p_acc%d" % i, [128, 512], F32) for i in range(2)]
                with ExitStack() as st_w:
                    stg = [SB(st_w, "p_stg%d" % i, [128, 2048], F32) for i in range(2)]
                    load_w(stg, peer_w_q[l], D, 2048, w_q, 2048)
                gb = [SB(st, "p_gb%d" % i, [128, 2 * D], BF16) for i in range(NG)]

                DMA(P, "sp", sk_f[:], peer_sk[l].rearrange("s n d -> n s d"), [], [sk_f])
                V(P, "tensor_copy", [sk_f], [sk_b], out=sk_b[:], in_=sk_f[:])
                for s_ in range(2):
                    TP(P, tp[0][:, s_, :], sk_b[:, s_, :], idb[:], [sk_b, idb], [tp[0]], inc=(s_ == 1))
                V(P, "tensor_copy", [tp[0]], [skT], out=skT[:], in_=tp[0][:, 0:2, :])
                bcast_row(l2g, ln2_g[l, :]); bcast_row(l2b, ln2_b[l, :])
                for eid in eids:
                    V(P, "memset", [], [eid], ap=eid[:].bitcast(F32), constant=0.0)
                gi = 0

                def front(ti):
                    r0, n = tiles[ti]
                    xbf = xbfs[ti % 2]; gw = gws[ti % 2]; eid = eids[ti % 2]
                    x = xs[ti % 2]
                    sap, sbuf_ = src_tile(src, ti)
                    DMA(P, "sp", x[0:n, :], sap, [sbuf_] if sbuf_ else [], [x])
                    make_xT(x, n, xbf, tp[ti % 2], xT)
                    yield
                    for c4 in range(4):
                        q_ = qp[c4 % 2]
                        for cc in range(4):
                            c = c4 * 4 + cc
                            for kc in range(8):
                                MM(P, q_[:, cc, 0:n], w_q[:, kc, c * 128:(c + 1) * 128], xT[:, kc, 0:n], kc == 0, kc == 7, [w_q, xT], [q_])
                        cast(qT[:, c4 * 4:c4 * 4 + 4, 0:n], q_[:, :, 0:n], [q_], [qT])
                        yield
                    for c4 in range(4):
                        sp_ = sp4[c4]
                        for cc in range(4):
                            c = c4 * 4 + cc
                            MM(P, sp_[0:n, cc, :], qT[:, c, 0:n], skT[:, c % 2, :], True, True, [qT, skT], [sp_])
                        cast(ssb[0:n, c4 * 4:c4 * 4 + 4, :], sp_[0:n, :, :], [sp_], [ssb])
                        yield
                    for c in range(16):
                        wk = work[c % 2]
                        V(P, "max", [ssb], [tv], out=tv[0:n, c, 0:8], in_=ssb[0:n, c, :])
                        V(P, "max_index", [ssb, tv], [tix], out=tix[0:n, c, 0:8], in_max=tv[0:n, c, 0:8], in_values=ssb[0:n, c, :])
                        V(P, "match_replace", [ssb, tv], [wk], out=wk[0:n, 0:128], in_to_replace=tv[0:n, c, 0:8], in_values=ssb[0:n, c, :], imm_value=-1e30)
                        V(P, "max", [wk], [tv], out=tv[0:n, c, 8:16], in_=wk[0:n, 0:128])
                        V(P, "max_index", [wk, tv], [tix], out=tix[0:n, c, 8:16], in_max=tv[0:n, c, 8:16], in_values=wk[0:n, 0:128])
                        yield
                    c4d = cand[0:n].rearrange("p h (a b) -> p h a b", a=16)
                    V(P, "tensor_tensor", [tv], [cand], out=c4d, in0=tv[0:n, 0:16:2, :].unsqueeze(3).to_broadcast([n, 8, 16, 16]),
                      in1=tv[0:n, 1:16:2, :].unsqueeze(2).to_broadcast([n, 8, 16, 16]), op=ALU.add)
                    for h in range(8):
                        wk = work[h % 2]
                        V(P, "max", [cand], [sc], out=sc[0:n, h, 0:8], in_=cand[0:n, h, :])
                        V(P, "max_index", [cand, sc], [sel], out=sel[0:n, h, 0:8], in_max=sc[0:n, h, 0:8], in_values=cand[0:n, h, :])
                        V(P, "match_replace", [cand, sc], [wk], out=wk[0:n, :], in_to_replace=sc[0:n, h, 0:8], in_values=cand[0:n, h, :], imm_value=-1e30)
                        V(P, "max", [wk], [sc], out=sc[0:n, h, 8:16], in_=wk[0:n, :])
                        V(P, "max_index", [wk, sc], [sel], out=sel[0:n, h, 8:16], in_max=sc[0:n, h, 8:16], in_values=wk[0:n, :])
                        yield
                    V(P, "tensor_tensor", [sc], [gw], out=gw[0:n], in0=sc[0:n], in1=sc[0:n, :, 0:1].to_broadcast([n, 8, 16]), op=ALU.subtract)
                    V(P, "activation", [gw], [gw], eng="act", out=gw[0:n], in_=gw[0:n], func=AF.Exp)
                    V(P, "tensor_reduce", [gw], [ssum], out=ssum[0:n, :], in_=gw[0:n], axis=AX.X, op=ALU.add)
                    V(P, "reciprocal", [ssum], [ssum], out=ssum[0:n, :], in_=ssum[0:n, :])
                    V(P, "tensor_tensor", [gw, ssum], [gw], out=gw[0:n], in0=gw[0:n], in1=ssum[0:n, :].unsqueeze(2).to_broadcast([n, 8, 16]), op=ALU.mult)
                    yield
                    V(P, "tensor_single_scalar", [sel], [ai], out=ai[0:n], in_=sel[0:n].bitcast(I32), scalar=4, op=ALU.arith_shift_right)
                    V(P, "tensor_single_scalar", [sel], [bi], out=bi[0:n], in_=sel[0:n].bitcast(I32), scalar=15, op=ALU.bitwise_and)
                    V(P, "tensor_copy", [ai], [af], out=af[0:n], in_=ai[0:n])
                    V(P, "tensor_copy", [bi], [bf_], out=bf_[0:n], in_=bi[0:n])
                    V(P, "tensor_copy", [tix], [i1f], out=i1f[0:n], in_=tix[0:n, 0:16:2, :])
                    V(P, "tensor_copy", [tix], [i2f], out=i2f[0:n], in_=tix[0:n, 1:16:2, :])
                    e4 = eq[0:n].rearrange("p h (a b) -> p h a b", a=16)
                    io4 = iota16[0:n, :].unsqueeze(1).unsqueeze(1).to_broadcast([n, 8, 16, 16])
                    for (xf, tf, outs_) in ((af, i1f, i1s), (bf_, i2f, i2s)):
                        V(P, "tensor_tensor", [iota16, xf], [eq], out=e4, in0=io4, in1=xf[0:n].unsqueeze(3).to_broadcast([n, 8, 16, 16]), op=ALU.is_equal)
                        V(P, "tensor_tensor", [eq, tf], [eq], out=e4, in0=e4, in1=tf[0:n].unsqueeze(2).to_broadcast([n, 8, 16, 16]), op=ALU.mult)
                        V(P, "tensor_reduce", [eq], [outs_], out=outs_[0:n], in_=e4, axis=AX.X, op=ALU.add)
                        yield
                    V(P, "scalar_tensor_tensor", [i1s, i2s], [eidf], out=eidf[0:n, :], in0=i1s[0:n].rearrange("p h k -> p (h k)"), scalar=128.0,
                      in1=i2s[0:n].rearrange("p h k -> p (h k)"), op0=ALU.mult, op1=ALU.add)
                    V(P, "tensor_copy", [eidf], [eid], out=eid[0:n, :], in_=eidf[0:n, :])
                def back(ti, gen):
                    nonlocal gi
                    r0, n = tiles[ti]
                    x = xs[ti % 2]; xbf = xbfs[ti % 2]; gw = gws[ti % 2]; eid = eids[ti % 2]
                    gwf = gw[0:n].rearrange("p h k -> p (h k)")
                    for grp in range(128 // GS):
                        bufs = []
                        for k in range(GS):
                            s_ = grp * GS + k
                            g_ = gb[gi % NG]; gi += 1
                            bufs.append(g_)
                            P.dma("pool", lambda e, g_=g_, s_=s_: e.indirect_dma_start(
                                out=g_[:, :], out_offset=None, in_=uvb,
                                in_offset=bass.IndirectOffsetOnAxis(ap=eid[:, s_:s_ + 1], axis=0), element_offset=l * NEXP * 2 * D),
                                reads=[eid, uvb_b], writes=[g_])
                            pr_ = prods[s_ % len(prods)]
                            V(P, "tensor_tensor", [g_, xbf], [pr_], out=pr_[0:n, :], in0=g_[0:n, 0:D], in1=xbf[0:n, :], op=ALU.mult)
                            V(P, "activation", [pr_], [hslot[s_]], eng="act", out=junk[0:n, :], in_=pr_[0:n, :], func=AF.Copy,
                              accum_out=hdn[0:n, s_:s_ + 1])
                        if gen is not None:
                            for _ in range(3):
                                next(gen, None)
                        sl = slice(grp * GS, (grp + 1) * GS)
                        ag = awg[grp]
                        V(P, "activation", hslot[sl], [ag], eng="act", out=aw[0:n, sl], in_=hdn[0:n, sl], func=AF.Gelu)
                        V(P, "tensor_tensor", [ag, gw], [ag], out=aw[0:n, sl], in0=aw[0:n, sl], in1=gwf[:, sl], op=ALU.mult)
                        dg = dgs[grp % 3]
                        V(P, "tensor_tensor", [ag, idb], [dg], out=dg[0:n, :, 0:n], in0=aw[0:n, sl].unsqueeze(2).to_broadcast([n, GS, n]),
                          in1=idb[0:n, 0:n].unsqueeze(1).to_broadcast([n, GS, n]), op=ALU.mult)
                        for k in range(GS):
                            s_ = grp * GS + k
                            for nb in range(2):
                                MM(P, accp[nb][0:n, :], dg[0:n, k, 0:n], bufs[k][0:n, D + nb * 512:D + (nb + 1) * 512], s_ == 0, s_ == 127,
                                   [dg, bufs[k]], [accp[nb]], inc=(s_ == 127 or (k == GS - 1 and nb == 1)))
                    if gen is not None:
                        for _ in gen:
                            pass
                    for nb in range(2):
                        V(P, "scalar_tensor_tensor", [x, accp[nb]], [x], out=x[0:n, nb * 512:(nb + 1) * 512], in0=x[0:n, nb * 512:(nb + 1) * 512],
                          scalar=ALPHA, in1=accp[nb][0:n, :], op0=ALU.mult, op1=ALU.add)
                    layer_norm(x, n, D, l2g, l2b, st6, mv, rstd)
                    if dst == "y":
                        DMA(P, "sp", y_out[r0:r0 + n, :], x[0:n, :], [x], [outs_b], owner=x)
                    else:
                        dap, dbuf = src_tile(dst, ti)
                        DMA(P, "sp", dap, x[0:n, :], [x], [dbuf], owner=x)
                for _ in front(0):
                    pass
                for ti in range(len(tiles)):
                    back(ti, front(ti + 1) if ti + 1 < len(tiles) else None)
                P.barrier()
                pass
                P.flush()

        def phase_K(src):
            with ExitStack() as st:
                wd = SB(st, "k_wd", [128, 8, 320], BF16)
                stg = [SB(st, "k_stg%d" % i, [128, 320], F32) for i in range(2)]
                kg = SB(st, "k_kg", [128, KV_LORA], F32)
                xs = [SB(st, "k_x%d" % i, [128, D], F32) for i in range(2)]
                xbf = SB(st, "k_xbf", [128, D], BF16)
                xT = SB(st, "k_xT", [128, 8, 128], BF16)
                kv = [SB(st, "k_kv%d" % i, [128, 320], F32) for i in range(2)]
                ck = [SB(st, "k_ck%d" % i, [128, 320], F32) for i in range(2)]
                rt = [SB(st, "k_rt%d" % i, [128, 64], F32) for i in range(2)]
                junk = SB(st, "k_junk", [128, 256], F32)
                ss = SB(st, "k_ss", [128, 1], F32)
                t1 = SB(st, "k_t1", [128, 32], F32); t2 = SB(st, "k_t2", [128, 32], F32)
                tp = [PS(st, "k_tp%d" % i, [128, 8, 128], BF16) for i in range(2)]
                kp = [PS(st, "k_kp%d" % i, [128, 512], F32) for i in range(2)]
                load_w(stg, w_dkv, D, 320, wd, 320)
                bcast_row(kg, kv_g[0, :])
                for ti, (r0, n) in enumerate(tiles):
                    x = xs[ti % 2]; kv_ = kv[ti % 2]; ck_ = ck[ti % 2]; rt_ = rt[ti % 2]
                    sap, sbuf_ = src_tile(src, ti)
                    DMA(P, "sp", x[0:n, :], sap, [sbuf_] if sbuf_ else [], [x])
                    DMA(P, "act", rt_[0:n, :], rope_tok[r0:r0 + n, :], [], [rt_])
                    make_xT(x, n, xbf, tp[ti % 2], xT)
                    kp_ = kp[ti % 2]
                    for kc in range(8):
                        MM(P, kp_[0:n, 0:320], xT[:, kc, 0:n], wd[:, kc, :], kc == 0, kc == 7, [xT, wd], [kp_])
                    V(P, "tensor_copy", [kp_], [kv_], out=kv_[0:n, :], in_=kp_[0:n, 0:320])
                    V(P, "scalar_tensor_tensor", [kv_], [junk, ss], out=junk[0:n, :], in0=kv_[0:n, 0:256], scalar=1.0, in1=kv_[0:n, 0:256],
                      op0=ALU.mult, op1=ALU.mult, accum_out=ss[0:n, :])
                    V(P, "tensor_scalar", [ss], [ss], out=ss[0:n, :], in0=ss[0:n, :], scalar1=1.0 / KV_LORA, scalar2=RMS_EPS, op0=ALU.mult, op1=ALU.add)
                    V(P, "activation", [ss], [ss], eng="act", out=ss[0:n, :], in_=ss[0:n, :], func=AF.Sqrt)
                    V(P, "reciprocal", [ss], [ss], out=ss[0:n, :], in_=ss[0:n, :])
                    V(P, "scalar_tensor_tensor", [kv_, ss, kg], [ck_], out=ck_[0:n, 0:256], in0=kv_[0:n, 0:256], scalar=ss[0:n, 0:1], in1=kg[0:n, :],
                      op0=ALU.mult, op1=ALU.mult)
                    x1 = kv_[0:n, 256:288]; x2 = kv_[0:n, 288:320]; cs = rt_[0:n, 0:32]; sn = rt_[0:n, 32:64]
                    V(P, "tensor_tensor", [kv_, rt_], [t1], out=t1[0:n, :], in0=x1, in1=cs, op=ALU.mult)
                    V(P, "tensor_tensor", [kv_, rt_], [t2], out=t2[0:n, :], in0=x2, in1=sn, op=ALU.mult)
                    V(P, "tensor_tensor", [t1, t2], [ck_], out=ck_[0:n, 256:288], in0=t1[0:n, :], in1=t2[0:n, :], op=ALU.subtract)
                    V(P, "tensor_tensor", [kv_, rt_], [t1], out=t1[0:n, :], in0=x1, in1=sn, op=ALU.mult)
                    V(P, "tensor_tensor", [kv_, rt_], [t2], out=t2[0:n, :], in0=x2, in1=cs, op=ALU.mult)
                    V(P, "tensor_tensor", [t1, t2], [ck_], out=ck_[0:n, 288:320], in0=t1[0:n, :], in1=t2[0:n, :], op=ALU.add)
                    DMA(P, "sp", ckv_out[r0:r0 + n, :], ck_[0:n, 0:256], [ck_], [outs_b], owner=ck_)
                    DMA(P, "sp", kpe_out[r0:r0 + n, :], ck_[0:n, 256:320], [ck_], [outs_b], owner=ck_)
                    if n == 128:
                        DMA(P, "sp", exin[ti // CHT][(ti % CHT) * 128:(ti % CHT) * 128 + 128, :], ck_[0:n, :], [ck_], [exin_b[ti // CHT]], owner=ck_)
                    else:
                        DMA(P, "sp", samp_ck[:, :], ck_[0:n, :], [ck_], [samp_b], owner=ck_)
                P.barrier()
                for ch in range(NCH):
                    P.dma("pool", lambda e, ch=ch: e.collective_compute(
                        "AllGather", ALU.bypass, replica_groups=[[0, 1], [2, 3], [4, 5], [6, 7]],
                        ins=[exin[ch]], outs=[exout[ch]]), reads=[exin_b[ch]], writes=[exout_b[ch]], inc=1)
                    P.inuse.remove(exout_b[ch])
                P.barrier()
                pass
                P.flush()

        def phase_A(jl, l, src, dst):
            NKT = 2 * NT
            NK = NKT * 128
            NKS = PAST + NS
            NKTS = (NKS + 127) // 128
            with ExitStack() as st:
                NKM = max(NKT, NKTS)
                cT = SB(st, "a_cT", [128, 2, NKM * 128], BF16)
                kpT = SB(st, "a_kpT", [64, NKM * 128], BF16)
                ctok = SB(st, "a_ctok", [128, NKM, 256], BF16)
                kl = [SB(st, "a_kl%d" % i, [128, 320], F32) for i in range(2)]
                klb = [SB(st, "a_klb%d" % i, [128, 320], BF16) for i in range(2)]
                stg = [SB(st, "a_stg%d" % i, [128, 512], F32) for i in range(2)]
                wdq = SB(st, "a_wdq", [128, 8, Q_LORA], BF16)
                qg = SB(st, "a_qg", [128, Q_LORA], F32)
                wun = SB(st, "a_wun", [128, 3, 1024], BF16)
                wup = SB(st, "a_wup", [128, 3, 512], BF16)
                wur = SB(st, "a_wur", [128, 3, 512], BF16)
                wukv = SB(st, "a_wukv", [128, 2, 2048], BF16)
                wukT = SB(st, "a_wukT", [128, 8, 256], BF16)
                wo = SB(st, "a_wo", [128, 8, D], BF16)
                l1g = SB(st, "a_l1g", [128, D], F32); l1b = SB(st, "a_l1b", [128, D], F32)
                mb = SB(st, "a_mb", [128, 256], F32)
                xs = [SB(st, "a_x%d" % i, [128, D], F32) for i in range(1)] * 2
                xbf = SB(st, "a_xbf", [128, D], BF16)
                xT = SB(st, "a_xT", [128, 8, 128], BF16)
                cq = SB(st, "a_cq", [128, Q_LORA], F32)
                cqb = SB(st, "a_cqb", [128, Q_LORA], BF16)
                cqT = SB(st, "a_cqT", [128, 3, 128], BF16)
                ss = SB(st, "a_ss", [128, 1], F32)
                junk = SB(st, "a_junk", [128, Q_LORA], F32)
                qnT = SB(st, "a_qnT", [128, 8, 128], BF16)
                qaT = SB(st, "a_qaT", [128, 2, 8, 128], BF16)
                qpA = SB(st, "a_qpA", [64, 8, 128], F32)
                qpB = SB(st, "a_qpB", [64, 8, 128], F32)
                qpT = SB(st, "a_qpT", [64, 8, 128], BF16)
                rcs = SB(st, "a_rcs", [64, 128], F32); rsn = SB(st, "a_rsn", [64, 128], F32)
                S = SB(st, "a_S", [128, 512], F32)
                Pb = [SB(st, "a_Pb%d" % i, [128, 512], BF16) for i in range(3)]
                PT = [SB(st, "a_PT%d" % i, [128, 4, 128], BF16) for i in range(3)]
                den = SB(st, "a_den", [128, 1], F32)
                ms = [SB(st, "a_m%d" % i, [128, 1], F32) for i in range(2)]
                bms = [SB(st, "a_bm%d" % i, [128, 32], F32) for i in range(2)]
                denbs = [SB(st, "a_denb%d" % i, [128, 32], F32) for i in range(2)]
                oc = SB(st, "a_oc", [128, 8, 256], BF16)
                ocT = SB(st, "a_ocT", [128, 16, 128], BF16)
                oT = SB(st, "a_oT", [128, 8, 128], BF16)
                st6 = SB(st, "a_st6", [128, 24], F32); mv = SB(st, "a_mv", [128, 2], F32); rstd = SB(st, "a_rstd", [128, 1], F32)
                tp = [PS(st, "a_tp%d" % i, [128, 8, 128], BF16) for i in range(2)]
                gp = [PS(st, "a_gp%d" % i, [128, 512], F32) for i in range(3)]
                gp.append(PS(st, "a_gp3", [128, 512], F32))
                mxp = [PS(st, "a_mxp%d" % i, [128, 512], F32) for i in range(2)]

                load_w(stg, w_dq[jl], D, Q_LORA, wdq, 512)
                load_w(stg, w_uq_n[jl], Q_LORA, 1024, wun, 512)
                load_w(stg, w_uq_p[jl], Q_LORA, 512, wup, 512)
                load_w(stg, w_uq_r[jl], Q_LORA, 512, wur, 512)
                load_w(stg, w_ukv, KV_LORA, 2048, wukv, 512)
                load_w(stg, w_o[jl], 1024, D, wo, 512)
                bcast_row(qg, q_g[jl, :]); bcast_row(l1g, ln1_g[l, :]); bcast_row(l1b, ln1_b[l, :])
                DMA(P, "sp", mb[:], maskb, [], [mb])
                for h in range(8):
                    t_ = tp[h % 2]
                    for cc in range(2):
                        TP(P, t_[:, cc, :], wukv[:, cc, h * 256:h * 256 + 128], idb[:], [wukv, idb], [t_], inc=(cc == 1))
                    V(P, "tensor_copy", [t_], [wukT], out=wukT[:, h, :].rearrange("p (c k) -> p c k", c=2), in_=t_[:, 0:2, :])

                def key_tile(srcap, srcbuf, nk, kt, cT_, kpT_, ctok_, i):
                    a = kl[i % 2]; b = klb[i % 2]; t_ = tp[i % 2]
                    for (sa, c0, c1) in srcap:
                        DMA(P, "sp" if i % 2 else "act", a[0:nk, c0:c1], sa, [srcbuf] if srcbuf else [], [a])
                    cast(b[0:nk, :], a[0:nk, :], [a], [b])
                    V(P, "tensor_copy", [b], [ctok_], eng="pool", out=ctok_[0:nk, kt, :], in_=b[0:nk, 0:256])
                    TP(P, t_[:, 0, 0:nk], b[0:nk, 0:128], idb[0:nk, 0:nk], [b, idb], [t_], inc=False)
                    TP(P, t_[:, 1, 0:nk], b[0:nk, 128:256], idb[0:nk, 0:nk], [b, idb], [t_], inc=False)
                    TP(P, t_[0:64, 2, 0:nk], b[0:nk, 256:320], idb[0:nk, 0:nk], [b, idb], [t_], inc=True)
                    V(P, "tensor_copy", [t_], [cT_], out=cT_[:, :, kt * 128:kt * 128 + nk], in_=t_[:, 0:2, 0:nk])
                    V(P, "tensor_copy", [t_], [kpT_], out=kpT_[0:64, kt * 128:kt * 128 + nk], in_=t_[0:64, 2, 0:nk])
                ki = 0
                for kt in range(NKT):
                    j_ = kt // 2
                    ch = j_ // CHT
                    r_ = (kt % 2) * chn[ch] * 128 + (j_ % CHT) * 128
                    key_tile([(exout[ch][r_:r_ + 128, :], 0, 320)], exout_b[ch], 128, kt, cT, kpT, ctok, ki); ki += 1

                def sample_keys():
                    ki2 = ki
                    for kt in range(NKTS):
                        k0 = kt * 128
                        if k0 + 128 <= PAST:
                            key_tile([(cache_ckv[k0:k0 + 128, :], 0, 256), (cache_kpe[k0:k0 + 128, :], 256, 320)], None, 128, kt, cT, kpT, ctok, ki2)
                        else:
                            key_tile([(samp_ck[:, :], 0, 320)], samp_b, NS, kt, cT, kpT, ctok, ki2)
                        ki2 += 1

                gpi = [0]

                def nxt():
                    gpi[0] += 1
                    return gp[gpi[0] % 4]
                for ti, (r0, n) in enumerate(tiles):
                    prompt = (n == 128)
                    nkeys = 256 * (ti + 1) if prompt else NKS
                    cT_, kpT_, ctok_ = (cT, kpT, ctok)
                    if not prompt:
                        sample_keys()
                    x = xs[ti % 2]
                    sap, sbuf_ = src_tile(src, ti)
                    DMA(P, "sp", x[0:n, :], sap, [sbuf_] if sbuf_ else [], [x])
                    DMA(P, "act", rcs[:, 0:n], ropeT_cs[:, r0:r0 + n], [], [rcs])
                    DMA(P, "act", rsn[:, 0:n], ropeT_sn[:, r0:r0 + n], [], [rsn])
                    make_xT(x, n, xbf, tp[ti % 2], xT)
                    g_ = nxt()
                    for kc in range(8):
                        MM(P, g_[0:n, 0:Q_LORA], xT[:, kc, 0:n], wdq[:, kc, :], kc == 0, kc == 7, [xT, wdq], [g_])
                    V(P, "tensor_copy", [g_], [cq], out=cq[0:n, :], in_=g_[0:n, 0:Q_LORA])
                    V(P, "scalar_tensor_tensor", [cq], [junk, ss], out=junk[0:n, :], in0=cq[0:n, :], scalar=1.0, in1=cq[0:n, :],
                      op0=ALU.mult, op1=ALU.mult, accum_out=ss[0:n, :])
                    V(P, "tensor_scalar", [ss], [ss], out=ss[0:n, :], in0=ss[0:n, :], scalar1=1.0 / Q_LORA, scalar2=RMS_EPS, op0=ALU.mult, op1=ALU.add)
                    V(P, "activation", [ss], [ss], eng="act", out=ss[0:n, :], in_=ss[0:n, :], func=AF.Sqrt)
                    V(P, "reciprocal", [ss], [ss], out=ss[0:n, :], in_=ss[0:n, :])
                    V(P, "scalar_tensor_tensor", [cq, ss, qg], [cqb], out=cqb[0:n, :], in0=cq[0:n, :], scalar=ss[0:n, 0:1], in1=qg[0:n, :],
                      op0=ALU.mult, op1=ALU.mult)
                    t_ = tp[(ti + 1) % 2]
                    for kc in range(3):
                        TP(P, t_[:, kc, 0:n], cqb[0:n, kc * 128:(kc + 1) * 128], idb[0:n, 0:n], [cqb, idb], [t_], inc=(kc == 2))
                    V(P, "tensor_copy", [t_], [cqT], out=cqT[:, :, 0:n], in_=t_[:, 0:3, 0:n])
                    for h4 in range(2):
                        g_ = nxt()
                        for hh in range(4):
                            h = h4 * 4 + hh
                            for kc in range(3):
                                MM(P, g_[:, hh * 128:hh * 128 + n], wun[:, kc, h * 128:(h + 1) * 128], cqT[:, kc, 0:n], kc == 0, kc == 2, [wun, cqT], [g_])
                        cast(qnT[:, h4 * 4:h4 * 4 + 4, 0:n], g_[:, :].rearrange("p (h q) -> p h q", h=4)[:, :, 0:n], [g_], [qnT])
                    for cc in range(2):
                        for h4 in range(2):
                            g_ = nxt()
                            for hh in range(4):
                                h = h4 * 4 + hh
                                MM(P, g_[:, hh * 128:hh * 128 + n], wukT[:, h, cc * 128:(cc + 1) * 128], qnT[:, h, 0:n], True, True, [wukT, qnT], [g_])
                            V(P, "activation", [g_], [qaT], eng="act", out=qaT[:, cc, h4 * 4:h4 * 4 + 4, 0:n],
                              in_=g_[:, :].rearrange("p (h q) -> p h q", h=4)[:, :, 0:n], func=AF.Copy, scale=ATTN_SCALE)
                    for (wsrc, dstb) in ((wup, qpA), (wur, qpB)):
                        for h4 in range(2):
                            g_ = nxt()
                            for hh in range(4):
                                h = h4 * 4 + hh
                                for kc in range(3):
                                    MM(P, g_[0:64, hh * 128:hh * 128 + n], wsrc[:, kc, h * 64:(h + 1) * 64], cqT[:, kc, 0:n], kc == 0, kc == 2, [wsrc, cqT], [g_])
                            V(P, "tensor_copy", [g_], [dstb], out=dstb[0:64, h4 * 4:h4 * 4 + 4, 0:n],
                              in_=g_[0:64, :].rearrange("p (h q) -> p h q", h=4)[:, :, 0:n])
                    V(P, "tensor_tensor", [qpA, rcs], [qpA], out=qpA[:, :, 0:n], in0=qpA[:, :, 0:n], in1=rcs[:, 0:n].unsqueeze(1).to_broadcast([64, 8, n]), op=ALU.mult)
                    V(P, "tensor_tensor", [qpB, rsn], [qpB], out=qpB[:, :, 0:n], in0=qpB[:, :, 0:n], in1=rsn[:, 0:n].unsqueeze(1).to_broadcast([64, 8, n]), op=ALU.mult)
                    V(P, "tensor_tensor", [qpA, qpB], [qpA], out=qpA[:, :, 0:n], in0=qpA[:, :, 0:n], in1=qpB[:, :, 0:n], op=ALU.add)
                    V(P, "activation", [qpA], [qpT], eng="act", out=qpT[:, :, 0:n], in_=qpA[:, :, 0:n], func=AF.Copy, scale=ATTN_SCALE)
                    nkt = (nkeys + 127) // 128
                    blocks = [(bi_, k0, min(512, nkeys - k0)) for bi_, k0 in enumerate(range(0, nkeys, 512))]
                    nblk = len(blocks)

                    def score_mm(g_, h, k0, kw):
                        MM(P, g_[0:n, 0:kw], qaT[:, 0, h, 0:n], cT_[:, 0, k0:k0 + kw], True, False, [qaT, cT_], [g_])
                        MM(P, g_[0:n, 0:kw], qaT[:, 1, h, 0:n], cT_[:, 1, k0:k0 + kw], False, False, [qaT, cT_], [g_])
                        MM(P, g_[0:n, 0:kw], qpT[0:64, h, 0:n], kpT_[0:64, k0:k0 + kw], False, True, [qpT, kpT_], [g_])

                    def pass1_thunks(h):
                        bm_ = bms[h % 2]; mh = ms[h % 2]
                        th = []
                        for (bi_, k0, kw) in blocks:
                            def f(bi_=bi_, k0=k0, kw=kw):
                                g_ = nxt()
                                MM(P, g_[0:n, 0:kw], qaT[:, 0, h, 0:n], cT_[:, 0, k0:k0 + kw], True, False, [qaT, cT_], [g_])
                                MM(P, g_[0:n, 0:kw], qaT[:, 1, h, 0:n], cT_[:, 1, k0:k0 + kw], False, True, [qaT, cT_], [g_])
                                V(P, "tensor_reduce", [g_], [bm_], out=bm_[0:n, bi_:bi_ + 1], in_=g_[0:n, 0:kw], axis=AX.X, op=ALU.max)
                            th.append(f)

                        def fin():
                            V(P, "tensor_reduce", [bm_], [mh], out=mh[0:n, :], in_=bm_[0:n, 0:nblk], axis=AX.X, op=ALU.max)
                            V(P, "tensor_scalar", [mh], [mh], out=mh[0:n, :], in0=mh[0:n, :], scalar1=-1.0, scalar2=None, op0=ALU.mult)
                        th.append(fin)
                        return th

                    def pass2(h, extra):
                        mh = ms[h % 2]; dn = denbs[h % 2]; ocp = mxp[h % 2]

                        def A(b):
                            bi_, k0, kw = blocks[b]
                            g_ = nxt()
                            pb_ = Pb[b % 3]
                            score_mm(g_, h, k0, kw)
                            if prompt and k0 + kw == nkeys:
                                V(P, "tensor_copy", [g_], [S], out=S[0:n, 0:kw], in_=g_[0:n, 0:kw])
                                V(P, "tensor_tensor", [S, mb], [S], out=S[0:n, kw - 256:kw], in0=S[0:n, kw - 256:kw], in1=mb[0:n, :], op=ALU.add)
                                V(P, "activation", [S, mh], [pb_, dn], eng="act", out=pb_[0:n, 0:kw], in_=S[0:n, 0:kw], func=AF.Exp,
                                  bias=mh[0:n, 0:1], scale=1.0, accum_out=dn[0:n, bi_:bi_ + 1])
                            else:
                                V(P, "activation", [g_, mh], [pb_, dn], eng="act", out=pb_[0:n, 0:kw], in_=g_[0:n, 0:kw], func=AF.Exp,
                                  bias=mh[0:n, 0:1], scale=1.0, accum_out=dn[0:n, bi_:bi_ + 1])

                        def B(b):
                            bi_, k0, kw = blocks[b]
                            pb_ = Pb[b % 3]; t_ = tp[b % 2]; pt_ = PT[b % 3]
                            ne = (kw + 127) // 128
                            for kk in range(ne):
                                nk = min(128, kw - kk * 128)
                                TP(P, t_[0:nk, kk, 0:n], pb_[0:n, kk * 128:kk * 128 + nk], idb[0:n, 0:n], [pb_, idb], [t_], inc=(kk == ne - 1))
                            nlast = kw - (ne - 1) * 128
                            nfull = ne if nlast == 128 else ne - 1
                            if nfull > 0:
                                V(P, "tensor_copy", [t_], [pt_], out=pt_[:, 0:nfull, 0:n], in_=t_[:, 0:nfull, 0:n])
                            if nlast < 128:
                                V(P, "tensor_copy", [t_], [pt_], out=pt_[0:nlast, ne - 1, 0:n], in_=t_[0:nlast, ne - 1, 0:n])

                        def C(b):
                            bi_, k0, kw = blocks[b]
                            pt_ = PT[b % 3]
                            ne = (kw + 127) // 128
                            for kk in range(ne):
                                kt = k0 // 128 + kk
                                nk = min(128, kw - kk * 128)
                                MM(P, ocp[0:n, 0:256], pt_[0:nk, kk, 0:n], ctok_[0:nk, kt, :], kt == 0, kt == nkt - 1, [pt_, ctok_], [ocp])
                        for step in range(nblk + 2):
                            if step < nblk:
                                A(step)
                            if 0 <= step - 1 < nblk:
                                B(step - 1)
                            if 0 <= step - 2 < nblk:
                                C(step - 2)
                            if extra:
                                extra.pop(0)()
                        while extra:
                            extra.pop(0)()
                        V(P, "tensor_reduce", [dn], [den], out=den[0:n, :], in_=dn[0:n, 0:nblk], axis=AX.X, op=ALU.add)
                        V(P, "reciprocal", [den], [den], out=den[0:n, :], in_=den[0:n, :])
                        V(P, "tensor_scalar", [ocp, den], [oc], out=oc[0:n, h, :], in0=ocp[0:n, 0:256], scalar1=den[0:n, 0:1], scalar2=None, op0=ALU.mult)
                    for f in pass1_thunks(0):
                        f()
                    for h in range(8):
                        pass2(h, pass1_thunks(h + 1) if h < 7 else [])
                    for half in range(2):
                        t_ = tp[half]
                        for kk in range(8):
                            c = half * 8 + kk
                            TP(P, t_[:, kk, 0:n], oc[0:n, c // 2, (c % 2) * 128:(c % 2) * 128 + 128], idb[0:n, 0:n], [oc, idb], [t_], inc=(kk == 7))
                        V(P, "tensor_copy", [t_], [ocT], out=ocT[:, half * 8:half * 8 + 8, 0:n], in_=t_[:, :, 0:n])
                    for h4 in range(2):
                        g_ = nxt()
                        for hh in range(4):
                            h = h4 * 4 + hh
                            for cc in range(2):
                                MM(P, g_[:, hh * 128:hh * 128 + n], wukv[:, cc, h * 256 + 128:h * 256 + 256], ocT[:, h * 2 + cc, 0:n], cc == 0, cc == 1, [wukv, ocT], [g_])
                        cast(oT[:, h4 * 4:h4 * 4 + 4, 0:n], g_[:, :].rearrange("p (h q) -> p h q", h=4)[:, :, 0:n], [g_], [oT])
                    for nb in range(2):
                        for h in range(8):
                            MM(P, mxp[nb][0:n, :], oT[:, h, 0:n], wo[:, h, nb * 512:(nb + 1) * 512], h == 0, h == 7, [oT, wo], [mxp[nb]])
                        V(P, "scalar_tensor_tensor", [x, mxp[nb]], [x], out=x[0:n, nb * 512:(nb + 1) * 512], in0=x[0:n, nb * 512:(nb + 1) * 512],
                          scalar=ALPHA, in1=mxp[nb][0:n, :], op0=ALU.mult, op1=ALU.add)
                    layer_norm(x, n, D, l1g, l1b, st6, mv, rstd)
                    dap, dbuf = src_tile(dst, ti)
                    DMA(P, "sp", dap, x[0:n, :], [x], [dbuf], owner=x)
                P.barrier()
                pass
                P.flush()

        phase_G(0, "in", "b")
        phase_P(0, "b", "a")
        phase_G(1, "a", "b")
        phase_P(1, "b", "a")
        phase_K("a")
        phase_A(0, 2, "a", "b")
        phase_P(2, "b", "a")
        phase_A(1, 3, "a", "b")
        phase_P(3, "b", "y")
        print("n_inst", P.n_inst)
    return nc


def _host_inputs(SEQ, PAST, inp):
    NT = SEQ // 256
    NPT = NT * 128
    f32 = np.float32
    xp = np.asarray(inp["x_prompt"], f32)
    xs = np.asarray(inp["x_sample"], f32)
    B = xp.shape[0]
    inv = (1.0 / (10000.0 ** (np.arange(0, 64, 2, dtype=np.float32) / 64.0))).astype(np.float32)
    shared = {}
    for k in ("ln1_g", "ln1_b", "ln2_g", "ln2_b", "gm_w_in", "gm_b_in", "gm_ln_g", "gm_ln_b", "gm_w_s", "gm_w_out",
              "mla_w_dkv", "mla_w_ukv", "mla_w_dq", "mla_q_norm_g", "mla_w_o", "peer_w_q", "peer_subkeys", "peer_u", "peer_v"):
        shared[k] = np.ascontiguousarray(np.asarray(inp[k], f32))
    shared["peer_u"] = shared["peer_u"].reshape(DEPTH * NEXP, D)
    shared["peer_v"] = shared["peer_v"].reshape(DEPTH * NEXP, D)
    shared["gm_b_sT"] = np.ascontiguousarray(np.transpose(np.asarray(inp["gm_b_s"], f32), (0, 2, 1)))
    shared["mla_kv_norm_g"] = np.asarray(inp["mla_kv_norm_g"], f32).reshape(1, -1)
    wuq = np.asarray(inp["mla_w_uq"], f32).reshape(2, Q_LORA, NH, 192)
    shared["w_uq_n"] = np.ascontiguousarray(wuq[..., :128].reshape(2, Q_LORA, 1024))
    shared["w_uq_p"] = np.ascontiguousarray(wuq[..., 128:].reshape(2, Q_LORA, 512))
    shared["w_uq_r"] = np.ascontiguousarray(np.concatenate([wuq[..., 160:192], wuq[..., 128:160]], -1).reshape(2, Q_LORA, 512))
    shared["ident"] = np.eye(128, dtype=f32)
    shared["iota16"] = np.tile(np.arange(16, dtype=f32)[None], (128, 1))
    maps = []
    for c in range(2 * B):
        b, r = c // 2, c % 2
        gt = [2 * j + r for j in range(NT)]
        xin = np.concatenate([xp[b].reshape(-1, 128, D)[gt].reshape(NPT, D), xs[c]], 0)
        pos = np.concatenate([(np.array(gt)[:, None] * 128 + np.arange(128)[None]).reshape(-1), PAST + np.arange(NS)]).astype(np.float32)
        ang = pos[:, None] * inv[None, :]
        cs, sn = np.cos(ang).astype(f32), np.sin(ang).astype(f32)
        qi = np.arange(128)[:, None]; kk = np.arange(256)[None, :]
        kchunk = kk // 64
        qchunk = (r * 128 + qi) // 64
        mask = np.where(kchunk <= qchunk, 0.0, -1e30).astype(f32)
        m = dict(shared)
        m.update(xin=np.ascontiguousarray(xin), cache_ckv=np.ascontiguousarray(np.asarray(inp["cache_ckv"], f32)[c]),
                 cache_kpe=np.ascontiguousarray(np.asarray(inp["cache_kpe"], f32)[c]),
                 rope_tok=np.ascontiguousarray(np.concatenate([cs, sn], 1)),
                 ropeT_cs=np.ascontiguousarray(np.concatenate([cs.T, cs.T], 0)),
                 ropeT_sn=np.ascontiguousarray(np.concatenate([-sn.T, sn.T], 0)),
                 maskb=mask)
        maps.append(m)
    return maps


def _assemble(SEQ, PAST, res, B):
    NT = SEQ // 256
    NPT = NT * 128
    f32 = np.float32
    y_p = np.zeros((B, SEQ, D), f32); ckv_p = np.zeros((B, SEQ, KV_LORA), f32); kpe_p = np.zeros((B, SEQ, QK_ROPE), f32)
    y_s = np.zeros((2 * B, NS, D), f32); gv = np.zeros((N_A, 2 * B, NS, GH), f32)
    ckv_s = np.zeros((2 * B, NS, KV_LORA), f32); kpe_s = np.zeros((2 * B, NS, QK_ROPE), f32)
    for c in range(2 * B):
        b, r = c // 2, c % 2
        o = res[c]
        for arr, key, w in ((y_p, "y_out", D), (ckv_p, "ckv_out", KV_LORA), (kpe_p, "kpe_out", QK_ROPE)):
            arr[b].reshape(SEQ // 128, 128, w)[r::2] = o[key][:NPT].reshape(NT, 128, w)
        y_s[c] = o["y_out"][NPT:]; ckv_s[c] = o["ckv_out"][NPT:]; kpe_s[c] = o["kpe_out"][NPT:]
        gv[:, c] = o["gv_out"]
    return (y_p, y_s, gv, ckv_p, kpe_p, ckv_s, kpe_s)


def run(SEQ, PAST, inp):
    nc = build(SEQ, PAST)
    maps = _host_inputs(SEQ, PAST, inp)
    res = run_bass_kernel_spmd(nc, maps, core_ids=list(range(8)))
    return _assemble(SEQ, PAST, res.results, np.asarray(inp["x_prompt"]).shape[0])


def kernel(**inputs):
    return run(8192, 2048, inputs)
```

```python
import math
import numpy as np
from contextlib import ExitStack
import concourse.bass as bass
import concourse.mybir as mybir
from concourse.bass_utils import run_bass_kernel_spmd

F32 = mybir.dt.float32
BF16 = mybir.dt.bfloat16
I32 = mybir.dt.int32
U32 = mybir.dt.uint32
ALU = mybir.AluOpType
AF = mybir.ActivationFunctionType
AX = mybir.AxisListType

D = 1024
DEPTH = 4
N_A = 2
ALPHA = (2.0 * DEPTH) ** 0.25
LN_EPS = 1e-5
RMS_EPS = 1e-6
GH = 2048
KV_LORA = 256
QK_ROPE = 64
Q_LORA = 384
NH = 8
ATTN_SCALE = (128 + 64) ** -0.5
NEXP = 16384
NS = 32
NG = 22
GS = 8


class Buf:
    __slots__ = ("name", "t", "last_w", "readers", "dsem", "dcnt")

    def __init__(self, name, t=None):
        self.name = name
        self.t = t
        self.last_w = None
        self.readers = {}
        self.dsem = None
        self.dcnt = 0

    def __getitem__(self, idx):
        return self.t[idx]


class Prog:
    ENG = ("pe", "act", "dve", "pool", "sp")

    def __init__(self, nc, stack, n_dma_sems=64):
        self.nc = nc
        self.q = {e: [] for e in self.ENG}
        self.cnt = {e: 0 for e in self.ENG}
        self.waited = {e: {} for e in self.ENG}
        self.sems = {}
        for e in self.ENG:
            self.sems["e_" + e] = stack.enter_context(nc.semaphore("e_" + e))
        self.free = []
        for i in range(n_dma_sems):
            k = "d%d" % i
            self.sems[k] = stack.enter_context(nc.semaphore(k))
            self.free.append(k)
        self.dval = {k: 0 for k in self.free}
        self.inuse = []
        self.n_inst = 0
        self.epoch = 0
        self.nreset = 0
        self.sems["B1"] = stack.enter_context(nc.semaphore("B1"))
        self.sems["B2"] = stack.enter_context(nc.semaphore("B2"))

    def _dsem(self, b):
        if b.dsem is None:
            k = self.free.pop()
            b.dsem = k
            b.dcnt = self.dval[k]
            self.inuse.append(b)
        return b.dsem

    def _deps(self, eng, reads, writes):
        deps = {}

        def add(k, v):
            if k in self.dval:
                v = self.dval[k]
            if v > deps.get(k, 0):
                deps[k] = v
        ep = self.epoch
        for b in reads:
            if b.last_w is not None and b.last_w[2] == ep:
                add(b.last_w[0], b.last_w[1])
        for b in writes:
            if b.last_w is not None and b.last_w[2] == ep:
                add(b.last_w[0], b.last_w[1])
            for k, (v, e_) in b.readers.items():
                if e_ == ep:
                    add(k, v)
        out = []
        w = self.waited[eng]
        for k, v in deps.items():
            if eng == "pe" and k == "e_pe":
                continue
            if w.get(k, 0) >= v:
                continue
            w[k] = v
            out.append((k, v))
        return out

    def op(self, eng, fn, reads=(), writes=(), inc=True):
        waits = self._deps(eng, reads, writes)
        key = "e_" + eng
        if inc:
            self.cnt[eng] += 1
            val = self.cnt[eng]
        else:
            val = self.cnt[eng] + 1
        self.q[eng].append((waits, fn, key if inc else None, 1))
        for b in writes:
            b.last_w = (key, val, self.epoch)
            b.readers = {}
        for b in reads:
            o_ = b.readers.get(key)
            if o_ is None or o_[1] != self.epoch or o_[0] < val:
                b.readers[key] = (val, self.epoch)
        self.n_inst += 1

    def dma(self, eng, fn, reads=(), writes=(), owner=None, inc=16):
        waits = self._deps(eng, reads, writes)
        if owner is None:
            owner = writes[0]
        key = self._dsem(owner)
        owner.dcnt += inc
        val = owner.dcnt
        self.dval[key] = val
        self.q[eng].append((waits, fn, key, inc))
        for b in writes:
            b.last_w = (key, val, self.epoch)
            b.readers = {}
        for b in reads:
            o_ = b.readers.get(key)
            if o_ is None or o_[1] != self.epoch or o_[0] < val:
                b.readers[key] = (val, self.epoch)
        self.n_inst += 1

    def barrier(self):
        for e in self.ENG:
            waits = []
            w = self.waited[e]
            for k, v in self.dval.items():
                if v > 0 and w.get(k, 0) < v:
                    waits.append((k, v))
                    w[k] = v
            for e2 in self.ENG:
                k = "e_" + e2
                if e2 != e and self.cnt[e2] > w.get(k, 0):
                    waits.append((k, self.cnt[e2]))
                    w[k] = self.cnt[e2]
            self.q[e].append((waits, None, None, 0))
        for b in self.inuse:
            self.free.append(b.dsem)
            b.dsem = None
        self.inuse = []
        self.epoch += 1

    def reset(self):
        self.nreset += 1
        k_ = self.nreset
        B1, B2 = self.sems["B1"], self.sems["B2"]
        for e in self.ENG:
            self.q[e].append(([], lambda eng: eng.sem_inc(B1, 1), None, 0))
        self.q["sp"].append(([("B1", 5 * k_)], None, None, 0))
        for k, sm in self.sems.items():
            if k in ("B1", "B2"):
                continue
            self.q["sp"].append(([], lambda eng, sm=sm: eng.sem_clear(sm), None, 0))
        self.q["sp"].append(([], lambda eng: eng.sem_inc(B2, 1), None, 0))
        for e in self.ENG:
            self.q[e].append(([("B2", k_)], None, None, 0))
        self.cnt = {e: 0 for e in self.ENG}
        self.waited = {e: {} for e in self.ENG}
        for k in self.dval:
            self.dval[k] = 0
        self.epoch += 1

    def flush(self):
        nc = self.nc
        sems = self.sems
        with nc.Block() as block:
            engmap = {"pe": block.tensor, "act": block.scalar, "dve": block.vector,
                      "pool": block.gpsimd, "sp": block.sync}
            for e in self.ENG:
                items = self.q[e]

                def body(engine, items=items):
                    for waits, fn, key, inc in items:
                        for k, v in waits:
                            engine.wait_ge(sems[k], v)
                        if fn is None:
                            continue
                        ins = fn(engine)
                        if key is not None:
                            ins.then_inc(sems[key], inc)
                engmap[e](body)
        self.q = {e: [] for e in self.ENG}


def V(P, name, R, W, eng="dve", **kw):
    P.op(eng, lambda e: getattr(e, name)(**kw), reads=R, writes=W)


def MM(P, out, lhsT, rhs, start, stop, R, W, inc=None):
    P.op("pe", lambda e: e.matmul(out, lhsT=lhsT, rhs=rhs, start=start, stop=stop), reads=R, writes=W,
         inc=(stop if inc is None else inc))


def TP(P, out, in_, ident, R, W, inc=True):
    P.op("pe", lambda e: e.transpose(out=out, in_=in_, identity=ident), reads=R, writes=W, inc=inc)


def DMA(P, q, out, in_, R, W, owner=None):
    P.dma(q, lambda e: e.dma_start(out=out, in_=in_), reads=R, writes=W, owner=owner)


def build(SEQ, PAST):
    NT = SEQ // 256
    NPT = NT * 128
    NTOK = NPT + NS
    tiles = [(j * 128, 128) for j in range(NT)] + [(NPT, NS)]
    nc = bass.Bass("TRN2", target_bir_lowering=False)

    def din(name, shape, dt=F32):
        return nc.dram_tensor(name, list(shape), dt, kind="ExternalInput").ap()

    def dout(name, shape, dt=F32):
        return nc.dram_tensor(name, list(shape), dt, kind="ExternalOutput").ap()

    xin = din("xin", [NTOK, D])
    cache_ckv = din("cache_ckv", [PAST, KV_LORA])
    cache_kpe = din("cache_kpe", [PAST, QK_ROPE])
    rope_tok = din("rope_tok", [NTOK, 64])
    ropeT_cs = din("ropeT_cs", [64, NTOK])
    ropeT_sn = din("ropeT_sn", [64, NTOK])
    maskb = din("maskb", [128, 256])
    ident_d = din("ident", [128, 128])
    iota_d = din("iota16", [128, 16])
    ln1_g = din("ln1_g", [DEPTH, D]); ln1_b = din("ln1_b", [DEPTH, D])
    ln2_g = din("ln2_g", [DEPTH, D]); ln2_b = din("ln2_b", [DEPTH, D])
    gm_w_in = din("gm_w_in", [N_A, D, 2 * GH]); gm_b_in = din("gm_b_in", [N_A, 2 * GH])
    gm_ln_g = din("gm_ln_g", [N_A, GH]); gm_ln_b = din("gm_ln_b", [N_A, GH])
    gm_w_s = din("gm_w_s", [N_A, 8, 128, 128]); gm_b_sT = din("gm_b_sT", [N_A, 128, 8])
    gm_w_out = din("gm_w_out", [N_A, GH, D])
    w_dkv = din("mla_w_dkv", [D, 320]); kv_g = din("mla_kv_norm_g", [1, KV_LORA])
    w_ukv = din("mla_w_ukv", [KV_LORA, 2048])
    w_dq = din("mla_w_dq", [2, D, Q_LORA]); q_g = din("mla_q_norm_g", [2, Q_LORA])
    w_uq_n = din("w_uq_n", [2, Q_LORA, 1024]); w_uq_p = din("w_uq_p", [2, Q_LORA, 512])
    w_uq_r = din("w_uq_r", [2, Q_LORA, 512])
    w_o = din("mla_w_o", [2, 1024, D])
    peer_w_q = din("peer_w_q", [DEPTH, D, 2048]); peer_sk = din("peer_subkeys", [DEPTH, 2, 128, 128])
    peer_u_f = din("peer_u", [DEPTH * NEXP, D]); peer_v_f = din("peer_v", [DEPTH * NEXP, D])

    y_out = dout("y_out", [NTOK, D])
    gv_out = dout("gv_out", [N_A, NS, GH])
    ckv_out = dout("ckv_out", [NTOK, KV_LORA])
    kpe_out = dout("kpe_out", [NTOK, QK_ROPE])

    xa = nc.dram_tensor("xa", [NTOK, D], F32).ap()
    xb = nc.dram_tensor("xb", [NTOK, D], F32).ap()
    CHT = 4
    NCH = (NT + CHT - 1) // CHT
    chn = [min(CHT, NT - ch * CHT) for ch in range(NCH)]
    exin = [nc.dram_tensor("exin%d" % ch, [chn[ch] * 128, 320], F32).ap() for ch in range(NCH)]
    exout = [nc.dram_tensor("exout%d" % ch, [2 * chn[ch] * 128, 320], F32).ap() for ch in range(NCH)]
    samp_ck = nc.dram_tensor("samp_ck", [NS, 320], F32).ap()
    uvb = nc.dram_tensor("uvb", [DEPTH * NEXP, 2 * D], BF16).ap()
    uvb_b = Buf("uvb")

    xa_b = [Buf("xa%d" % i) for i in range(NT + 1)]
    xb_b = [Buf("xb%d" % i) for i in range(NT + 1)]
    exin_b = [Buf("exin%d" % ch) for ch in range(NCH)]; exout_b = [Buf("exout%d" % ch) for ch in range(NCH)]; samp_b = Buf("samp")
    outs_b = Buf("outs")

    with ExitStack() as top:
        P = Prog(nc, top)

        uid = [0]

        def SB(st, name, shape, dt):
            uid[0] += 1
            return Buf(name, st.enter_context(nc.sbuf_tensor("sb%d_%s" % (uid[0], name), list(shape), dt)))

        def PS(st, name, shape, dt):
            uid[0] += 1
            return Buf(name, st.enter_context(nc.psum_tensor("ps%d_%s" % (uid[0], name), list(shape), dt)))

        idf = SB(top, "idf", [128, 128], F32)
        idb = SB(top, "idb", [128, 128], BF16)
        iota16 = SB(top, "iota16", [128, 16], F32)
        ones_bf = SB(top, "ones_bf", [1, 128], BF16)
        DMA(P, "sp", idf[:], ident_d, [], [idf])
        DMA(P, "sp", iota16[:], iota_d, [], [iota16])
        V(P, "tensor_copy", [idf], [idb], out=idb[:], in_=idf[:])
        V(P, "memset", [], [ones_bf], ap=ones_bf[:], constant=1.0)
        cast_rr = [0]

        def cast(out, in_, R, W):
            cast_rr[0] += 1
            if cast_rr[0] % 2:
                V(P, "tensor_copy", R, W, out=out, in_=in_)
            else:
                V(P, "activation", R, W, eng="act", out=out, in_=in_, func=AF.Copy)

        def load_w(st_bufs, dram2d, K, cols, dst, SW):
            i = 0
            for kc in range((K + 127) // 128):
                kr = min(128, K - kc * 128)
                for c0 in range(0, cols, SW):
                    w = min(SW, cols - c0)
                    sg = st_bufs[i % len(st_bufs)]
                    i += 1
                    DMA(P, "sp" if i % 2 else "act", sg[0:kr, 0:w], dram2d[kc * 128:kc * 128 + kr, c0:c0 + w], [], [sg])
                    cast(dst[0:kr, kc, c0:c0 + w], sg[0:kr, 0:w], [sg], [dst])

        def bcast_row(dst, row_ap):
            DMA(P, "sp", dst[:], row_ap.partition_broadcast(128), [], [dst])

        def make_xT(x, n, xbf, tp, xT, KC=8):
            cast(xbf[0:n, 0:KC * 128], x[0:n, 0:KC * 128], [x], [xbf])
            for kc in range(KC):
                TP(P, tp[:, kc, 0:n], xbf[0:n, kc * 128:(kc + 1) * 128], idb[0:n, 0:n], [xbf, idb], [tp], inc=(kc == KC - 1))
            V(P, "tensor_copy", [tp], [xT], out=xT[:, 0:KC, 0:n], in_=tp[:, 0:KC, 0:n])

        def layer_norm(t, n, Dn, gB, bB, st6, mv, rstd):
            C = Dn // 512
            for c in range(C):
                V(P, "bn_stats", [t], [st6], out=st6[0:n, c * 6:(c + 1) * 6], in_=t[0:n, c * 512:(c + 1) * 512])
            V(P, "bn_aggr", [st6], [mv], out=mv[0:n, :], in_=st6[0:n, 0:C * 6])
            V(P, "tensor_scalar", [mv], [rstd], out=rstd[0:n, :], in0=mv[0:n, 1:2], scalar1=LN_EPS, scalar2=None, op0=ALU.add)
            V(P, "activation", [rstd], [rstd], eng="act", out=rstd[0:n, :], in_=rstd[0:n, :], func=AF.Sqrt)
            V(P, "reciprocal", [rstd], [rstd], out=rstd[0:n, :], in_=rstd[0:n, :])
            V(P, "tensor_scalar", [t, mv, rstd], [t], out=t[0:n, 0:Dn], in0=t[0:n, 0:Dn], scalar1=mv[0:n, 0:1],
              scalar2=rstd[0:n, 0:1], op0=ALU.subtract, op1=ALU.mult)
            V(P, "tensor_tensor", [t, gB], [t], out=t[0:n, 0:Dn], in0=t[0:n, 0:Dn], in1=gB[0:n, 0:Dn], op=ALU.mult)
            V(P, "tensor_tensor", [t, bB], [t], out=t[0:n, 0:Dn], in0=t[0:n, 0:Dn], in1=bB[0:n, 0:Dn], op=ALU.add)

        def src_tile(l_src, ti):
            r0, n = tiles[ti]
            if l_src == "in":
                return xin[r0:r0 + n, :], None
            if l_src == "a":
                return xa[r0:r0 + n, :], xa_b[ti]
            return xb[r0:r0 + n, :], xb_b[ti]

        def phase_C():
            with ExitStack() as st:
                RR = 4
                su = [SB(st, "c_su%d" % i, [128, RR, D], F32) for i in range(2)]
                sv = [SB(st, "c_sv%d" % i, [128, RR, D], F32) for i in range(2)]
                ot = [st.enter_context(nc.sbuf_tensor("c_ot%d" % i, [128, RR, 2 * D], BF16)) for i in range(2)]
                ou = [Buf("c_ou%d" % i, ot[i]) for i in range(2)]
                ov = [Buf("c_ov%d" % i, ot[i]) for i in range(2)]
                uview = peer_u_f.rearrange("(p r) d -> p r d", p=128)
                vview = peer_v_f.rearrange("(p r) d -> p r d", p=128)
                oview = uvb.rearrange("(p r) d -> p r d", p=128)
                npc = (DEPTH * NEXP // 128) // RR
                for i in range(npc):
                    a = su[i % 2]; b = sv[i % 2]; o1 = ou[i % 2]; o2 = ov[i % 2]
                    DMA(P, "sp", a[:], uview[:, i * RR:(i + 1) * RR, :], [], [a])
                    DMA(P, "act", b[:], vview[:, i * RR:(i + 1) * RR, :], [], [b])
                    V(P, "tensor_copy", [a], [o1], out=o1[:, :, 0:D], in_=a[:])
                    V(P, "activation", [b], [o2], eng="act", out=o2[:, :, D:2 * D], in_=b[:], func=AF.Copy)
                    DMA(P, "sp", oview[:, i * RR:(i + 1) * RR, :], o1[:], [o1, o2], [uvb_b], owner=o1)
                P.barrier()
                P.flush()

        def phase_G(l, src, dst):
            with ExitStack() as st:
                w_in = SB(st, "g_w_in", [128, 8, 4096], BF16)
                w_out = SB(st, "g_w_out", [128, 16, 1024], BF16)
                stg = [SB(st, "g_stg%d" % i, [128, 2048], F32) for i in range(2)]
                b_in = SB(st, "g_b_in", [1, 4096], BF16)
                glg = SB(st, "g_glg", [128, GH], F32); glb = SB(st, "g_glb", [128, GH], F32)
                l1g = SB(st, "g_l1g", [128, D], F32); l1b = SB(st, "g_l1b", [128, D], F32)
                ws_f = SB(st, "g_ws_f", [128, 8, 128], F32)
                ws_b = SB(st, "g_ws_b", [128, 8, 128], BF16)
                wsT = SB(st, "g_wsT", [128, 8, 128], BF16)
                bsT = SB(st, "g_bsT", [128, 8], F32)
                xs = [SB(st, "g_x%d" % i, [128, D], F32) for i in range(2)]
                xbf = SB(st, "g_xbf", [128, D], BF16)
                xT = SB(st, "g_xT", [128, 8, 128], BF16)
                u = SB(st, "g_u", [128, GH], BF16)
                v = SB(st, "g_v", [128, GH], F32)
                vb = SB(st, "g_vb", [128, GH], BF16)
                s = SB(st, "g_s", [128, GH], BF16)
                sT = SB(st, "g_sT", [128, 16, 128], BF16)
                st6 = SB(st, "g_st6", [128, 24], F32); mv = SB(st, "g_mv", [128, 2], F32); rstd = SB(st, "g_rstd", [128, 1], F32)
                tp = [PS(st, "g_tp%d" % i, [128, 8, 128], BF16) for i in range(2)]
                zp = [PS(st, "g_zp%d" % i, [128, 512], F32) for i in range(2)]
                svp = [PS(st, "g_svp%d" % i, [128, 512], F32) for i in range(2)]
                mxp = [PS(st, "g_mxp%d" % i, [128, 512], F32) for i in range(2)]

                load_w(stg, gm_w_in[l], D, 4096, w_in, 2048)
                load_w(stg, gm_w_out[l], GH, D, w_out, 2048)
                for hb in range(2):
                    DMA(P, "sp", stg[hb][0:1, :], gm_b_in[l:l + 1, hb * 2048:(hb + 1) * 2048], [], [stg[hb]])
                    V(P, "tensor_copy", [stg[hb]], [b_in], out=b_in[0:1, hb * 2048:(hb + 1) * 2048], in_=stg[hb][0:1, :])
                bcast_row(glg, gm_ln_g[l, :]); bcast_row(glb, gm_ln_b[l, :])
                bcast_row(l1g, ln1_g[l, :]); bcast_row(l1b, ln1_b[l, :])
                DMA(P, "sp", ws_f[:], gm_w_s[l].rearrange("g i j -> i g j"), [], [ws_f])
                DMA(P, "sp", bsT[:], gm_b_sT[l], [], [bsT])
                V(P, "memset", [ws_f], [ws_f], ap=ws_f[0:64, :, 64:128], constant=0.0)
                V(P, "tensor_copy", [ws_f], [ws_b], out=ws_b[:], in_=ws_f[:])
                for g in range(8):
                    TP(P, tp[0][:, g, :], ws_b[:, g, :], idb[:], [ws_b, idb], [tp[0]], inc=(g == 7))
                V(P, "tensor_copy", [tp[0]], [wsT], out=wsT[:], in_=tp[0][:])

                ot = [st.enter_context(nc.sbuf_tensor("g_ot%d_%d" % (l, i), [128, 2, 2 * D], BF16)) for i in range(2)]
                ou = [Buf("g_ou%d" % i, ot[i]) for i in range(2)]
                ov = [Buf("g_ov%d" % i, ot[i]) for i in range(2)]
                rlo = l * 2 * NEXP
                uview = peer_u_f[rlo:rlo + 2 * NEXP, :].rearrange("(p r) d -> p r d", p=128)
                vview = peer_v_f[rlo:rlo + 2 * NEXP, :].rearrange("(p r) d -> p r d", p=128)
                oview = uvb[rlo:rlo + 2 * NEXP, :].rearrange("(p r) d -> p r d", p=128)
                npc = (2 * NEXP // 128) // 2
                cpi = [0]

                def conv_pieces(k):
                    for _ in range(k):
                        i = cpi[0]
                        if i >= npc:
                            return
                        cpi[0] += 1
                        o1 = ou[i % 2]; o2 = ov[i % 2]
                        DMA(P, "sp", stg[0][:].rearrange("p (r d) -> p r d", r=2), uview[:, i * 2:(i + 1) * 2, :], [], [stg[0]])
                        DMA(P, "act", stg[1][:].rearrange("p (r d) -> p r d", r=2), vview[:, i * 2:(i + 1) * 2, :], [], [stg[1]])
                        V(P, "tensor_copy", [stg[0]], [o1], out=o1[:, :, 0:D], in_=stg[0][:].rearrange("p (r d) -> p r d", r=2))
                        V(P, "activation", [stg[1]], [o2], eng="act", out=o2[:, :, D:2 * D], in_=stg[1][:].rearrange("p (r d) -> p r d", r=2), func=AF.Copy)
                        DMA(P, "sp", oview[:, i * 2:(i + 1) * 2, :], o1[:], [o1, o2], [uvb_b], owner=o1)
                per_tile = (npc + len(tiles) - 1) // len(tiles)

                for ti, (r0, n) in enumerate(tiles):
                    x = xs[ti % 2]
                    sap, sbuf_ = src_tile(src, ti)
                    DMA(P, "sp", x[0:n, :], sap, [sbuf_] if sbuf_ else [], [x])
                    conv_pieces(per_tile)
                    make_xT(x, n, xbf, tp[ti % 2], xT)
                    for nb in range(8):
                        z = zp[nb % 2]
                        MM(P, z[0:n, :], ones_bf[0:1, 0:n], b_in[0:1, nb * 512:(nb + 1) * 512], True, False, [ones_bf, b_in], [z])
                        for kc in range(8):
                            MM(P, z[0:n, :], xT[:, kc, 0:n], w_in[:, kc, nb * 512:(nb + 1) * 512], False, kc == 7, [xT, w_in], [z])
                        if nb < 4:
                            V(P, "activation", [z], [u], eng="act", out=u[0:n, nb * 512:(nb + 1) * 512], in_=z[0:n, :], func=AF.Gelu)
                        else:
                            V(P, "activation", [z], [v], eng="act", out=v[0:n, (nb - 4) * 512:(nb - 3) * 512], in_=z[0:n, :], func=AF.Gelu)
                    layer_norm(v, n, GH, glg, glb, st6, mv, rstd)
                    if n == NS:
                        DMA(P, "sp", gv_out[l], v[0:n, :], [v], [outs_b], owner=v)
                    V(P, "activation", [v], [vb], eng="act", out=vb[0:n, :], in_=v[0:n, :], func=AF.Copy)
                    for g in range(8):
                        sp_ = svp[(g // 2) % 2]
                        MM(P, sp_[0:n, (g % 2) * 256:(g % 2) * 256 + 256], wsT[0:n, g, 0:n], vb[0:n, g * 256:(g + 1) * 256], True, True, [wsT, vb], [sp_])
                        V(P, "scalar_tensor_tensor", [sp_, bsT, u], [s], out=s[0:n, g * 256:(g + 1) * 256],
                          in0=sp_[0:n, (g % 2) * 256:(g % 2) * 256 + 256], scalar=bsT[0:n, g:g + 1], in1=u[0:n, g * 256:(g + 1) * 256],
                          op0=ALU.add, op1=ALU.mult)
                    for half in range(2):
                        t_ = tp[half]
                        for kc in range(8):
                            c = half * 8 + kc
                            TP(P, t_[:, kc, 0:n], s[0:n, c * 128:(c + 1) * 128], idb[0:n, 0:n], [s, idb], [t_], inc=(kc == 7))
                        V(P, "tensor_copy", [t_], [sT], out=sT[:, half * 8:half * 8 + 8, 0:n], in_=t_[:, :, 0:n])
                    for nb in range(2):
                        for kc in range(16):
                            MM(P, mxp[nb][0:n, :], sT[:, kc, 0:n], w_out[:, kc, nb * 512:(nb + 1) * 512], kc == 0, kc == 15, [sT, w_out], [mxp[nb]])
                        V(P, "scalar_tensor_tensor", [x, mxp[nb]], [x], out=x[0:n, nb * 512:(nb + 1) * 512], in0=x[0:n, nb * 512:(nb + 1) * 512],
                          scalar=ALPHA, in1=mxp[nb][0:n, :], op0=ALU.mult, op1=ALU.add)
                    layer_norm(x, n, D, l1g, l1b, st6, mv, rstd)
                    dap, dbuf = src_tile(dst, ti)
                    DMA(P, "sp", dap, x[0:n, :], [x], [dbuf], owner=x)
                P.barrier()
                pass
                P.flush()

        def phase_P(l, src, dst):
            with ExitStack() as st:
                w_q = SB(st, "p_w_q", [128, 8, 2048], BF16)
                sk_f = SB(st, "p_sk_f", [128, 2, 128], F32)
                sk_b = SB(st, "p_sk_b", [128, 2, 128], BF16)
                skT = SB(st, "p_skT", [128, 2, 128], BF16)
                l2g = SB(st, "p_l2g", [128, D], F32); l2b = SB(st, "p_l2b", [128, D], F32)
                xs = [SB(st, "p_x%d" % i, [128, D], F32) for i in range(2)]
                xbfs = [SB(st, "p_xbf%d" % i, [128, D], BF16) for i in range(2)]
                xT = SB(st, "p_xT", [128, 8, 128], BF16)
                qT = SB(st, "p_qT", [128, 16, 128], BF16)
                ssb = SB(st, "p_ssb", [128, 16, 128], F32)
                tv = SB(st, "p_tv", [128, 16, 16], F32)
                tix = SB(st, "p_tix", [128, 16, 16], U32)
                work = [SB(st, "p_work%d" % i, [128, 256], F32) for i in range(2)]
                cand = SB(st, "p_cand", [128, 8, 256], F32)
                sc = SB(st, "p_sc", [128, 8, 16], F32)
                sel = SB(st, "p_sel", [128, 8, 16], U32)
                gws = [SB(st, "p_gw%d" % i, [128, 8, 16], F32) for i in range(2)]
                ssum = SB(st, "p_ssum", [128, 8], F32)
                ai = SB(st, "p_ai", [128, 8, 16], I32); bi = SB(st, "p_bi", [128, 8, 16], I32)
                af = SB(st, "p_af", [128, 8, 16], F32); bf_ = SB(st, "p_bf", [128, 8, 16], F32)
                i1f = SB(st, "p_i1f", [128, 8, 16], F32); i2f = SB(st, "p_i2f", [128, 8, 16], F32)
                eq = SB(st, "p_eq", [128, 8, 256], F32)
                i1s = SB(st, "p_i1s", [128, 8, 16], F32); i2s = SB(st, "p_i2s", [128, 8, 16], F32)
                eidf = SB(st, "p_eidf", [128, 128], F32)
                eids = [SB(st, "p_eid%d" % i, [128, 128], I32) for i in range(2)]
                hdn = SB(st, "p_hdn", [128, 128], F32)
                aw = SB(st, "p_aw", [128, 128], F32)
                junk = SB(st, "p_junk", [128, D], BF16)
                dgs = [SB(st, "p_dg%d" % i, [128, GS, 128], BF16) for i in range(3)]
                prods = [SB(st, "p_prod%d" % i, [128, D], BF16) for i in range(4)]
                hslot = [Buf("p_hs%d" % i) for i in range(128)]
                awg = [Buf("p_awg%d" % i) for i in range(128 // GS)]
                st6 = SB(st, "p_st6", [128, 24], F32); mv = SB(st, "p_mv", [128, 2], F32); rstd = SB(st, "p_rstd", [128, 1], F32)
                tp = [PS(st, "p_tp%d" % i, [128, 8, 128], BF16) for i in range(2)]
                qp = [PS(st, "p_qp%d" % i, [128, 4, 128], F32) for i in range(2)]
                sp4 = [PS(st, "p_sp%d" % i, [128, 4, 128], F32) for i in range(2)] * 2
                accp = [PS(st, "p_acc%d" % i, [128, 512], F32) for i in range(2)]
                with ExitStack() as st_w:
                    stg = [SB(st_w, "p_stg%d" % i, [128, 2048], F32) for i in range(2)]
                    load_w(stg, peer_w_q[l], D, 2048, w_q, 2048)
                gb = [SB(st, "p_gb%d" % i, [128, 2 * D], BF16) for i in range(NG)]

                DMA(P, "sp", sk_f[:], peer_sk[l].rearrange("s n d -> n s d"), [], [sk_f])
                V(P, "tensor_copy", [sk_f], [sk_b], out=sk_b[:], in_=sk_f[:])
                for s_ in range(2):
                    TP(P, tp[0][:, s_, :], sk_b[:, s_, :], idb[:], [sk_b, idb], [tp[0]], inc=(s_ == 1))
                V(P, "tensor_copy", [tp[0]], [skT], out=skT[:], in_=tp[0][:, 0:2, :])
                bcast_row(l2g, ln2_g[l, :]); bcast_row(l2b, ln2_b[l, :])
                for eid in eids:
                    V(P, "memset", [], [eid], ap=eid[:].bitcast(F32), constant=0.0)
                gi = 0

                def front(ti):
                    r0, n = tiles[ti]
                    xbf = xbfs[ti % 2]; gw = gws[ti % 2]; eid = eids[ti % 2]
                    x = xs[ti % 2]
                    sap, sbuf_ = src_tile(src, ti)
                    DMA(P, "sp", x[0:n, :], sap, [sbuf_] if sbuf_ else [], [x])
                    make_xT(x, n, xbf, tp[ti % 2], xT)
                    yield
                    for c4 in range(4):
                        q_ = qp[c4 % 2]
                        for cc in range(4):
                            c = c4 * 4 + cc
                            for kc in range(8):
                                MM(P, q_[:, cc, 0:n], w_q[:, kc, c * 128:(c + 1) * 128], xT[:, kc, 0:n], kc == 0, kc == 7, [w_q, xT], [q_])
                        cast(qT[:, c4 * 4:c4 * 4 + 4, 0:n], q_[:, :, 0:n], [q_], [qT])
                        yield
                    for c4 in range(4):
                        sp_ = sp4[c4]
                        for cc in range(4):
                            c = c4 * 4 + cc
                            MM(P, sp_[0:n, cc, :], qT[:, c, 0:n], skT[:, c % 2, :], True, True, [qT, skT], [sp_])
                        cast(ssb[0:n, c4 * 4:c4 * 4 + 4, :], sp_[0:n, :, :], [sp_], [ssb])
                        yield
                    for c in range(16):
                        wk = work[c % 2]
                        V(P, "max", [ssb], [tv], out=tv[0:n, c, 0:8], in_=ssb[0:n, c, :])
                        V(P, "max_index", [ssb, tv], [tix], out=tix[0:n, c, 0:8], in_max=tv[0:n, c, 0:8], in_values=ssb[0:n, c, :])
                        V(P, "match_replace", [ssb, tv], [wk], out=wk[0:n, 0:128], in_to_replace=tv[0:n, c, 0:8], in_values=ssb[0:n, c, :], imm_value=-1e30)
                        V(P, "max", [wk], [tv], out=tv[0:n, c, 8:16], in_=wk[0:n, 0:128])
                        V(P, "max_index", [wk, tv], [tix], out=tix[0:n, c, 8:16], in_max=tv[0:n, c, 8:16], in_values=wk[0:n, 0:128])
                        yield
                    c4d = cand[0:n].rearrange("p h (a b) -> p h a b", a=16)
                    V(P, "tensor_tensor", [tv], [cand], out=c4d, in0=tv[0:n, 0:16:2, :].unsqueeze(3).to_broadcast([n, 8, 16, 16]),
                      in1=tv[0:n, 1:16:2, :].unsqueeze(2).to_broadcast([n, 8, 16, 16]), op=ALU.add)
                    for h in range(8):
                        wk = work[h % 2]
                        V(P, "max", [cand], [sc], out=sc[0:n, h, 0:8], in_=cand[0:n, h, :])
                        V(P, "max_index", [cand, sc], [sel], out=sel[0:n, h, 0:8], in_max=sc[0:n, h, 0:8], in_values=cand[0:n, h, :])
                        V(P, "match_replace", [cand, sc], [wk], out=wk[0:n, :], in_to_replace=sc[0:n, h, 0:8], in_values=cand[0:n, h, :], imm_value=-1e30)
                        V(P, "max", [wk], [sc], out=sc[0:n, h, 8:16], in_=wk[0:n, :])
                        V(P, "max_index", [wk, sc], [sel], out=sel[0:n, h, 8:16], in_max=sc[0:n, h, 8:16], in_values=wk[0:n, :])
                        yield
                    V(P, "tensor_tensor", [sc], [gw], out=gw[0:n], in0=sc[0:n], in1=sc[0:n, :, 0:1].to_broadcast([n, 8, 16]), op=ALU.subtract)
                    V(P, "activation", [gw], [gw], eng="act", out=gw[0:n], in_=gw[0:n], func=AF.Exp)
                    V(P, "tensor_reduce", [gw], [ssum], out=ssum[0:n, :], in_=gw[0:n], axis=AX.X, op=ALU.add)
                    V(P, "reciprocal", [ssum], [ssum], out=ssum[0:n, :], in_=ssum[0:n, :])
                    V(P, "tensor_tensor", [gw, ssum], [gw], out=gw[0:n], in0=gw[0:n], in1=ssum[0:n, :].unsqueeze(2).to_broadcast([n, 8, 16]), op=ALU.mult)
                    yield
                    V(P, "tensor_single_scalar", [sel], [ai], out=ai[0:n], in_=sel[0:n].bitcast(I32), scalar=4, op=ALU.arith_shift_right)
                    V(P, "tensor_single_scalar", [sel], [bi], out=bi[0:n], in_=sel[0:n].bitcast(I32), scalar=15, op=ALU.bitwise_and)
                    V(P, "tensor_copy", [ai], [af], out=af[0:n], in_=ai[0:n])
                    V(P, "tensor_copy", [bi], [bf_], out=bf_[0:n], in_=bi[0:n])
                    V(P, "tensor_copy", [tix], [i1f], out=i1f[0:n], in_=tix[0:n, 0:16:2, :])
                    V(P, "tensor_copy", [tix], [i2f], out=i2f[0:n], in_=tix[0:n, 1:16:2, :])
                    e4 = eq[0:n].rearrange("p h (a b) -> p h a b", a=16)
                    io4 = iota16[0:n, :].unsqueeze(1).unsqueeze(1).to_broadcast([n, 8, 16, 16])
                    for (xf, tf, outs_) in ((af, i1f, i1s), (bf_, i2f, i2s)):
                        V(P, "tensor_tensor", [iota16, xf], [eq], out=e4, in0=io4, in1=xf[0:n].unsqueeze(3).to_broadcast([n, 8, 16, 16]), op=ALU.is_equal)
                        V(P, "tensor_tensor", [eq, tf], [eq], out=e4, in0=e4, in1=tf[0:n].unsqueeze(2).to_broadcast([n, 8, 16, 16]), op=ALU.mult)
                        V(P, "tensor_reduce", [eq], [outs_], out=outs_[0:n], in_=e4, axis=AX.X, op=ALU.add)
                        yield
                    V(P, "scalar_tensor_tensor", [i1s, i2s], [eidf], out=eidf[0:n, :], in0=i1s[0:n].rearrange("p h k -> p (h k)"), scalar=128.0,
                      in1=i2s[0:n].rearrange("p h k -> p (h k)"), op0=ALU.mult, op1=ALU.add)
                    V(P, "tensor_copy", [eidf], [eid], out=eid[0:n, :], in_=eidf[0:n, :])
                def back(ti, gen):
                    nonlocal gi
                    r0, n = tiles[ti]
                    x = xs[ti % 2]; xbf = xbfs[ti % 2]; gw = gws[ti % 2]; eid = eids[ti % 2]
                    gwf = gw[0:n].rearrange("p h k -> p (h k)")
                    for grp in range(128 // GS):
                        bufs = []
                        for k in range(GS):
                            s_ = grp * GS + k
                            g_ = gb[gi % NG]; gi += 1
                            bufs.append(g_)
                            P.dma("pool", lambda e, g_=g_, s_=s_: e.indirect_dma_start(
                                out=g_[:, :], out_offset=None, in_=uvb,
                                in_offset=bass.IndirectOffsetOnAxis(ap=eid[:, s_:s_ + 1], axis=0), element_offset=l * NEXP * 2 * D),
                                reads=[eid, uvb_b], writes=[g_])
                            pr_ = prods[s_ % len(prods)]
                            V(P, "tensor_tensor", [g_, xbf], [pr_], out=pr_[0:n, :], in0=g_[0:n, 0:D], in1=xbf[0:n, :], op=ALU.mult)
                            V(P, "activation", [pr_], [hslot[s_]], eng="act", out=junk[0:n, :], in_=pr_[0:n, :], func=AF.Copy,
                              accum_out=hdn[0:n, s_:s_ + 1])
                        if gen is not None:
                            for _ in range(3):
                                next(gen, None)
                        sl = slice(grp * GS, (grp + 1) * GS)
                        ag = awg[grp]
                        V(P, "activation", hslot[sl], [ag], eng="act", out=aw[0:n, sl], in_=hdn[0:n, sl], func=AF.Gelu)
                        V(P, "tensor_tensor", [ag, gw], [ag], out=aw[0:n, sl], in0=aw[0:n, sl], in1=gwf[:, sl], op=ALU.mult)
                        dg = dgs[grp % 3]
                        V(P, "tensor_tensor", [ag, idb], [dg], out=dg[0:n, :, 0:n], in0=aw[0:n, sl].unsqueeze(2).to_broadcast([n, GS, n]),
                          in1=idb[0:n, 0:n].unsqueeze(1).to_broadcast([n, GS, n]), op=ALU.mult)
                        for k in range(GS):
                            s_ = grp * GS + k
                            for nb in range(2):
                                MM(P, accp[nb][0:n, :], dg[0:n, k, 0:n], bufs[k][0:n, D + nb * 512:D + (nb + 1) * 512], s_ == 0, s_ == 127,
                                   [dg, bufs[k]], [accp[nb]], inc=(s_ == 127 or (k == GS - 1 and nb == 1)))
                    if gen is not None:
                        for _ in gen:
                            pass
                    for nb in range(2):
                        V(P, "scalar_tensor_tensor", [x, accp[nb]], [x], out=x[0:n, nb * 512:(nb + 1) * 512], in0=x[0:n, nb * 512:(nb + 1) * 512],
                          scalar=ALPHA, in1=accp[nb][0:n, :], op0=ALU.mult, op1=ALU.add)
                    layer_norm(x, n, D, l2g, l2b, st6, mv, rstd)
                    if dst == "y":
                        DMA(P, "sp", y_out[r0:r0 + n, :], x[0:n, :], [x], [outs_b], owner=x)
                    else:
                        dap, dbuf = src_tile(dst, ti)
                        DMA(P, "sp", dap, x[0:n, :], [x], [dbuf], owner=x)
                for _ in front(0):
                    pass
                for ti in range(len(tiles)):
                    back(ti, front(ti + 1) if ti + 1 < len(tiles) else None)
                P.barrier()
                pass
                P.flush()

        def phase_K(src):
            with ExitStack() as st:
                wd = SB(st, "k_wd", [128, 8, 320], BF16)
                stg = [SB(st, "k_stg%d" % i, [128, 320], F32) for i in range(2)]
                kg = SB(st, "k_kg", [128, KV_LORA], F32)
                xs = [SB(st, "k_x%d" % i, [128, D], F32) for i in range(2)]
                xbf = SB(st, "k_xbf", [128, D], BF16)
                xT = SB(st, "k_xT", [128, 8, 128], BF16)
                kv = [SB(st, "k_kv%d" % i, [128, 320], F32) for i in range(2)]
                ck = [SB(st, "k_ck%d" % i, [128, 320], F32) for i in range(2)]
                rt = [SB(st, "k_rt%d" % i, [128, 64], F32) for i in range(2)]
                junk = SB(st, "k_junk", [128, 256], F32)
                ss = SB(st, "k_ss", [128, 1], F32)
                t1 = SB(st, "k_t1", [128, 32], F32); t2 = SB(st, "k_t2", [128, 32], F32)
                tp = [PS(st, "k_tp%d" % i, [128, 8, 128], BF16) for i in range(2)]
                kp = [PS(st, "k_kp%d" % i, [128, 512], F32) for i in range(2)]
                load_w(stg, w_dkv, D, 320, wd, 320)
                bcast_row(kg, kv_g[0, :])
                for ti, (r0, n) in enumerate(tiles):
                    x = xs[ti % 2]; kv_ = kv[ti % 2]; ck_ = ck[ti % 2]; rt_ = rt[ti % 2]
                    sap, sbuf_ = src_tile(src, ti)
                    DMA(P, "sp", x[0:n, :], sap, [sbuf_] if sbuf_ else [], [x])
                    DMA(P, "act", rt_[0:n, :], rope_tok[r0:r0 + n, :], [], [rt_])
                    make_xT(x, n, xbf, tp[ti % 2], xT)
                    kp_ = kp[ti % 2]
                    for kc in range(8):
                        MM(P, kp_[0:n, 0:320], xT[:, kc, 0:n], wd[:, kc, :], kc == 0, kc == 7, [xT, wd], [kp_])
                    V(P, "tensor_copy", [kp_], [kv_], out=kv_[0:n, :], in_=kp_[0:n, 0:320])
                    V(P, "scalar_tensor_tensor", [kv_], [junk, ss], out=junk[0:n, :], in0=kv_[0:n, 0:256], scalar=1.0, in1=kv_[0:n, 0:256],
                      op0=ALU.mult, op1=ALU.mult, accum_out=ss[0:n, :])
                    V(P, "tensor_scalar", [ss], [ss], out=ss[0:n, :], in0=ss[0:n, :], scalar1=1.0 / KV_LORA, scalar2=RMS_EPS, op0=ALU.mult, op1=ALU.add)
                    V(P, "activation", [ss], [ss], eng="act", out=ss[0:n, :], in_=ss[0:n, :], func=AF.Sqrt)
                    V(P, "reciprocal", [ss], [ss], out=ss[0:n, :], in_=ss[0:n, :])
                    V(P, "scalar_tensor_tensor", [kv_, ss, kg], [ck_], out=ck_[0:n, 0:256], in0=kv_[0:n, 0:256], scalar=ss[0:n, 0:1], in1=kg[0:n, :],
                      op0=ALU.mult, op1=ALU.mult)
                    x1 = kv_[0:n, 256:288]; x2 = kv_[0:n, 288:320]; cs = rt_[0:n, 0:32]; sn = rt_[0:n, 32:64]
                    V(P, "tensor_tensor", [kv_, rt_], [t1], out=t1[0:n, :], in0=x1, in1=cs, op=ALU.mult)
                    V(P, "tensor_tensor", [kv_, rt_], [t2], out=t2[0:n, :], in0=x2, in1=sn, op=ALU.mult)
                    V(P, "tensor_tensor", [t1, t2], [ck_], out=ck_[0:n, 256:288], in0=t1[0:n, :], in1=t2[0:n, :], op=ALU.subtract)
                    V(P, "tensor_tensor", [kv_, rt_], [t1], out=t1[0:n, :], in0=x1, in1=sn, op=ALU.mult)
                    V(P, "tensor_tensor", [kv_, rt_], [t2], out=t2[0:n, :], in0=x2, in1=cs, op=ALU.mult)
                    V(P, "tensor_tensor", [t1, t2], [ck_], out=ck_[0:n, 288:320], in0=t1[0:n, :], in1=t2[0:n, :], op=ALU.add)
                    DMA(P, "sp", ckv_out[r0:r0 + n, :], ck_[0:n, 0:256], [ck_], [outs_b], owner=ck_)
                    DMA(P, "sp", kpe_out[r0:r0 + n, :], ck_[0:n, 256:320], [ck_], [outs_b], owner=ck_)
                    if n == 128:
                        DMA(P, "sp", exin[ti // CHT][(ti % CHT) * 128:(ti % CHT) * 128 + 128, :], ck_[0:n, :], [ck_], [exin_b[ti // CHT]], owner=ck_)
                    else:
                        DMA(P, "sp", samp_ck[:, :], ck_[0:n, :], [ck_], [samp_b], owner=ck_)
                P.barrier()
                for ch in range(NCH):
                    P.dma("pool", lambda e, ch=ch: e.collective_compute(
                        "AllGather", ALU.bypass, replica_groups=[[0, 1], [2, 3], [4, 5], [6, 7]],
                        ins=[exin[ch]], outs=[exout[ch]]), reads=[exin_b[ch]], writes=[exout_b[ch]], inc=1)
                    P.inuse.remove(exout_b[ch])
                P.barrier()
                pass
                P.flush()

        def phase_A(jl, l, src, dst):
            NKT = 2 * NT
            NK = NKT * 128
            NKS = PAST + NS
            NKTS = (NKS + 127) // 128
            with ExitStack() as st:
                NKM = max(NKT, NKTS)
                cT = SB(st, "a_cT", [128, 2, NKM * 128], BF16)
                kpT = SB(st, "a_kpT", [64, NKM * 128], BF16)
                ctok = SB(st, "a_ctok", [128, NKM, 256], BF16)
                kl = [SB(st, "a_kl%d" % i, [128, 320], F32) for i in range(2)]
                klb = [SB(st, "a_klb%d" % i, [128, 320], BF16) for i in range(2)]
                stg = [SB(st, "a_stg%d" % i, [128, 512], F32) for i in range(2)]
                wdq = SB(st, "a_wdq", [128, 8, Q_LORA], BF16)
                qg = SB(st, "a_qg", [128, Q_LORA], F32)
                wun = SB(st, "a_wun", [128, 3, 1024], BF16)
                wup = SB(st, "a_wup", [128, 3, 512], BF16)
                wur = SB(st, "a_wur", [128, 3, 512], BF16)
                wukv = SB(st, "a_wukv", [128, 2, 2048], BF16)
                wukT = SB(st, "a_wukT", [128, 8, 256], BF16)
                wo = SB(st, "a_wo", [128, 8, D], BF16)
                l1g = SB(st, "a_l1g", [128, D], F32); l1b = SB(st, "a_l1b", [128, D], F32)
                mb = SB(st, "a_mb", [128, 256], F32)
                xs = [SB(st, "a_x%d" % i, [128, D], F32) for i in range(1)] * 2
                xbf = SB(st, "a_xbf", [128, D], BF16)
                xT = SB(st, "a_xT", [128, 8, 128], BF16)
                cq = SB(st, "a_cq", [128, Q_LORA], F32)
                cqb = SB(st, "a_cqb", [128, Q_LORA], BF16)
                cqT = SB(st, "a_cqT", [128, 3, 128], BF16)
                ss = SB(st, "a_ss", [128, 1], F32)
                junk = SB(st, "a_junk", [128, Q_LORA], F32)
                qnT = SB(st, "a_qnT", [128, 8, 128], BF16)
                qaT = SB(st, "a_qaT", [128, 2, 8, 128], BF16)
                qpA = SB(st, "a_qpA", [64, 8, 128], F32)
                qpB = SB(st, "a_qpB", [64, 8, 128], F32)
                qpT = SB(st, "a_qpT", [64, 8, 128], BF16)
                rcs = SB(st, "a_rcs", [64, 128], F32); rsn = SB(st, "a_rsn", [64, 128], F32)
                S = SB(st, "a_S", [128, 512], F32)
                Pb = [SB(st, "a_Pb%d" % i, [128, 512], BF16) for i in range(3)]
                PT = [SB(st, "a_PT%d" % i, [128, 4, 128], BF16) for i in range(3)]
                den = SB(st, "a_den", [128, 1], F32)
                ms = [SB(st, "a_m%d" % i, [128, 1], F32) for i in range(2)]
                bms = [SB(st, "a_bm%d" % i, [128, 32], F32) for i in range(2)]
                denbs = [SB(st, "a_denb%d" % i, [128, 32], F32) for i in range(2)]
                oc = SB(st, "a_oc", [128, 8, 256], BF16)
                ocT = SB(st, "a_ocT", [128, 16, 128], BF16)
                oT = SB(st, "a_oT", [128, 8, 128], BF16)
                st6 = SB(st, "a_st6", [128, 24], F32); mv = SB(st, "a_mv", [128, 2], F32); rstd = SB(st, "a_rstd", [128, 1], F32)
                tp = [PS(st, "a_tp%d" % i, [128, 8, 128], BF16) for i in range(2)]
                gp = [PS(st, "a_gp%d" % i, [128, 512], F32) for i in range(3)]
                gp.append(PS(st, "a_gp3", [128, 512], F32))
                mxp = [PS(st, "a_mxp%d" % i, [128, 512], F32) for i in range(2)]

                load_w(stg, w_dq[jl], D, Q_LORA, wdq, 512)
                load_w(stg, w_uq_n[jl], Q_LORA, 1024, wun, 512)
                load_w(stg, w_uq_p[jl], Q_LORA, 512, wup, 512)
                load_w(stg, w_uq_r[jl], Q_LORA, 512, wur, 512)
                load_w(stg, w_ukv, KV_LORA, 2048, wukv, 512)
                load_w(stg, w_o[jl], 1024, D, wo, 512)
                bcast_row(qg, q_g[jl, :]); bcast_row(l1g, ln1_g[l, :]); bcast_row(l1b, ln1_b[l, :])
                DMA(P, "sp", mb[:], maskb, [], [mb])
                for h in range(8):
                    t_ = tp[h % 2]
                    for cc in range(2):
                        TP(P, t_[:, cc, :], wukv[:, cc, h * 256:h * 256 + 128], idb[:], [wukv, idb], [t_], inc=(cc == 1))
                    V(P, "tensor_copy", [t_], [wukT], out=wukT[:, h, :].rearrange("p (c k) -> p c k", c=2), in_=t_[:, 0:2, :])

                def key_tile(srcap, srcbuf, nk, kt, cT_, kpT_, ctok_, i):
                    a = kl[i % 2]; b = klb[i % 2]; t_ = tp[i % 2]
                    for (sa, c0, c1) in srcap:
                        DMA(P, "sp" if i % 2 else "act", a[0:nk, c0:c1], sa, [srcbuf] if srcbuf else [], [a])
                    cast(b[0:nk, :], a[0:nk, :], [a], [b])
                    V(P, "tensor_copy", [b], [ctok_], eng="pool", out=ctok_[0:nk, kt, :], in_=b[0:nk, 0:256])
                    TP(P, t_[:, 0, 0:nk], b[0:nk, 0:128], idb[0:nk, 0:nk], [b, idb], [t_], inc=False)
                    TP(P, t_[:, 1, 0:nk], b[0:nk, 128:256], idb[0:nk, 0:nk], [b, idb], [t_], inc=False)
                    TP(P, t_[0:64, 2, 0:nk], b[0:nk, 256:320], idb[0:nk, 0:nk], [b, idb], [t_], inc=True)
                    V(P, "tensor_copy", [t_], [cT_], out=cT_[:, :, kt * 128:kt * 128 + nk], in_=t_[:, 0:2, 0:nk])
                    V(P, "tensor_copy", [t_], [kpT_], out=kpT_[0:64, kt * 128:kt * 128 + nk], in_=t_[0:64, 2, 0:nk])
                ki = 0
                for kt in range(NKT):
                    j_ = kt // 2
                    ch = j_ // CHT
                    r_ = (kt % 2) * chn[ch] * 128 + (j_ % CHT) * 128
                    key_tile([(exout[ch][r_:r_ + 128, :], 0, 320)], exout_b[ch], 128, kt, cT, kpT, ctok, ki); ki += 1

                def sample_keys():
                    ki2 = ki
                    for kt in range(NKTS):
                        k0 = kt * 128
                        if k0 + 128 <= PAST:
                            key_tile([(cache_ckv[k0:k0 + 128, :], 0, 256), (cache_kpe[k0:k0 + 128, :], 256, 320)], None, 128, kt, cT, kpT, ctok, ki2)
                        else:
                            key_tile([(samp_ck[:, :], 0, 320)], samp_b, NS, kt, cT, kpT, ctok, ki2)
                        ki2 += 1

                gpi = [0]

                def nxt():
                    gpi[0] += 1
                    return gp[gpi[0] % 4]
                for ti, (r0, n) in enumerate(tiles):
                    prompt = (n == 128)
                    nkeys = 256 * (ti + 1) if prompt else NKS
                    cT_, kpT_, ctok_ = (cT, kpT, ctok)
                    if not prompt:
                        sample_keys()
                    x = xs[ti % 2]
                    sap, sbuf_ = src_tile(src, ti)
                    DMA(P, "sp", x[0:n, :], sap, [sbuf_] if sbuf_ else [], [x])
                    DMA(P, "act", rcs[:, 0:n], ropeT_cs[:, r0:r0 + n], [], [rcs])
                    DMA(P, "act", rsn[:, 0:n], ropeT_sn[:, r0:r0 + n], [], [rsn])
                    make_xT(x, n, xbf, tp[ti % 2], xT)
                    g_ = nxt()
                    for kc in range(8):
                        MM(P, g_[0:n, 0:Q_LORA], xT[:, kc, 0:n], wdq[:, kc, :], kc == 0, kc == 7, [xT, wdq], [g_])
                    V(P, "tensor_copy", [g_], [cq], out=cq[0:n, :], in_=g_[0:n, 0:Q_LORA])
                    V(P, "scalar_tensor_tensor", [cq], [junk, ss], out=junk[0:n, :], in0=cq[0:n, :], scalar=1.0, in1=cq[0:n, :],
                      op0=ALU.mult, op1=ALU.mult, accum_out=ss[0:n, :])
                    V(P, "tensor_scalar", [ss], [ss], out=ss[0:n, :], in0=ss[0:n, :], scalar1=1.0 / Q_LORA, scalar2=RMS_EPS, op0=ALU.mult, op1=ALU.add)
                    V(P, "activation", [ss], [ss], eng="act", out=ss[0:n, :], in_=ss[0:n, :], func=AF.Sqrt)
                    V(P, "reciprocal", [ss], [ss], out=ss[0:n, :], in_=ss[0:n, :])
                    V(P, "scalar_tensor_tensor", [cq, ss, qg], [cqb], out=cqb[0:n, :], in0=cq[0:n, :], scalar=ss[0:n, 0:1], in1=qg[0:n, :],
                      op0=ALU.mult, op1=ALU.mult)
                    t_ = tp[(ti + 1) % 2]
                    for kc in range(3):
                        TP(P, t_[:, kc, 0:n], cqb[0:n, kc * 128:(kc + 1) * 128], idb[0:n, 0:n], [cqb, idb], [t_], inc=(kc == 2))
                    V(P, "tensor_copy", [t_], [cqT], out=cqT[:, :, 0:n], in_=t_[:, 0:3, 0:n])
                    for h4 in range(2):
                        g_ = nxt()
                        for hh in range(4):
                            h = h4 * 4 + hh
                            for kc in range(3):
                                MM(P, g_[:, hh * 128:hh * 128 + n], wun[:, kc, h * 128:(h + 1) * 128], cqT[:, kc, 0:n], kc == 0, kc == 2, [wun, cqT], [g_])
                        cast(qnT[:, h4 * 4:h4 * 4 + 4, 0:n], g_[:, :].rearrange("p (h q) -> p h q", h=4)[:, :, 0:n], [g_], [qnT])
                    for cc in range(2):
                        for h4 in range(2):
                            g_ = nxt()
                            for hh in range(4):
                                h = h4 * 4 + hh
                                MM(P, g_[:, hh * 128:hh * 128 + n], wukT[:, h, cc * 128:(cc + 1) * 128], qnT[:, h, 0:n], True, True, [wukT, qnT], [g_])
                            V(P, "activation", [g_], [qaT], eng="act", out=qaT[:, cc, h4 * 4:h4 * 4 + 4, 0:n],
                              in_=g_[:, :].rearrange("p (h q) -> p h q", h=4)[:, :, 0:n], func=AF.Copy, scale=ATTN_SCALE)
                    for (wsrc, dstb) in ((wup, qpA), (wur, qpB)):
                        for h4 in range(2):
                            g_ = nxt()
                            for hh in range(4):
                                h = h4 * 4 + hh
                                for kc in range(3):
                                    MM(P, g_[0:64, hh * 128:hh * 128 + n], wsrc[:, kc, h * 64:(h + 1) * 64], cqT[:, kc, 0:n], kc == 0, kc == 2, [wsrc, cqT], [g_])
                            V(P, "tensor_copy", [g_], [dstb], out=dstb[0:64, h4 * 4:h4 * 4 + 4, 0:n],
                              in_=g_[0:64, :].rearrange("p (h q) -> p h q", h=4)[:, :, 0:n])
                    V(P, "tensor_tensor", [qpA, rcs], [qpA], out=qpA[:, :, 0:n], in0=qpA[:, :, 0:n], in1=rcs[:, 0:n].unsqueeze(1).to_broadcast([64, 8, n]), op=ALU.mult)
                    V(P, "tensor_tensor", [qpB, rsn], [qpB], out=qpB[:, :, 0:n], in0=qpB[:, :, 0:n], in1=rsn[:, 0:n].unsqueeze(1).to_broadcast([64, 8, n]), op=ALU.mult)
                    V(P, "tensor_tensor", [qpA, qpB], [qpA], out=qpA[:, :, 0:n], in0=qpA[:, :, 0:n], in1=qpB[:, :, 0:n], op=ALU.add)
                    V(P, "activation", [qpA], [qpT], eng="act", out=qpT[:, :, 0:n], in_=qpA[:, :, 0:n], func=AF.Copy, scale=ATTN_SCALE)
                    nkt = (nkeys + 127) // 128
                    blocks = [(bi_, k0, min(512, nkeys - k0)) for bi_, k0 in enumerate(range(0, nkeys, 512))]
                    nblk = len(blocks)

                    def score_mm(g_, h, k0, kw):
                        MM(P, g_[0:n, 0:kw], qaT[:, 0, h, 0:n], cT_[:, 0, k0:k0 + kw], True, False, [qaT, cT_], [g_])
                        MM(P, g_[0:n, 0:kw], qaT[:, 1, h, 0:n], cT_[:, 1, k0:k0 + kw], False, False, [qaT, cT_], [g_])
                        MM(P, g_[0:n, 0:kw], qpT[0:64, h, 0:n], kpT_[0:64, k0:k0 + kw], False, True, [qpT, kpT_], [g_])

                    def pass1_thunks(h):
                        bm_ = bms[h % 2]; mh = ms[h % 2]
                        th = []
                        for (bi_, k0, kw) in blocks:
                            def f(bi_=bi_, k0=k0, kw=kw):
                                g_ = nxt()
                                MM(P, g_[0:n, 0:kw], qaT[:, 0, h, 0:n], cT_[:, 0, k0:k0 + kw], True, True, [qaT, cT_], [g_])
                                V(P, "tensor_reduce", [g_], [bm_], out=bm_[0:n, bi_:bi_ + 1], in_=g_[0:n, 0:kw], axis=AX.X, op=ALU.max)
                            th.append(f)

                        def fin():
                            V(P, "tensor_reduce", [bm_], [mh], out=mh[0:n, :], in_=bm_[0:n, 0:nblk], axis=AX.X, op=ALU.max)
                            V(P, "tensor_scalar", [mh], [mh], out=mh[0:n, :], in0=mh[0:n, :], scalar1=-1.0, scalar2=None, op0=ALU.mult)
                        th.append(fin)
                        return th

                    def pass2(h, extra):
                        mh = ms[h % 2]; dn = denbs[h % 2]; ocp = mxp[h % 2]

                        def A(b):
                            bi_, k0, kw = blocks[b]
                            g_ = nxt()
                            pb_ = Pb[b % 3]
                            score_mm(g_, h, k0, kw)
                            if prompt and k0 + kw == nkeys:
                                V(P, "tensor_copy", [g_], [S], out=S[0:n, 0:kw], in_=g_[0:n, 0:kw])
                                V(P, "tensor_tensor", [S, mb], [S], out=S[0:n, kw - 256:kw], in0=S[0:n, kw - 256:kw], in1=mb[0:n, :], op=ALU.add)
                                V(P, "activation", [S, mh], [pb_, dn], eng="act", out=pb_[0:n, 0:kw], in_=S[0:n, 0:kw], func=AF.Exp,
                                  bias=mh[0:n, 0:1], scale=1.0, accum_out=dn[0:n, bi_:bi_ + 1])
                            else:
                                V(P, "activation", [g_, mh], [pb_, dn], eng="act", out=pb_[0:n, 0:kw], in_=g_[0:n, 0:kw], func=AF.Exp,
                                  bias=mh[0:n, 0:1], scale=1.0, accum_out=dn[0:n, bi_:bi_ + 1])

                        def B(b):
                            bi_, k0, kw = blocks[b]
                            pb_ = Pb[b % 3]; t_ = tp[b % 2]; pt_ = PT[b % 3]
                            ne = (kw + 127) // 128
                            for kk in range(ne):
                                nk = min(128, kw - kk * 128)
                                TP(P, t_[0:nk, kk, 0:n], pb_[0:n, kk * 128:kk * 128 + nk], idb[0:n, 0:n], [pb_, idb], [t_], inc=(kk == ne - 1))
                            nlast = kw - (ne - 1) * 128
                            nfull = ne if nlast == 128 else ne - 1
                            if nfull > 0:
                                V(P, "tensor_copy", [t_], [pt_], out=pt_[:, 0:nfull, 0:n], in_=t_[:, 0:nfull, 0:n])
                            if nlast < 128:
                                V(P, "tensor_copy", [t_], [pt_], out=pt_[0:nlast, ne - 1, 0:n], in_=t_[0:nlast, ne - 1, 0:n])

                        def C(b):
                            bi_, k0, kw = blocks[b]
                            pt_ = PT[b % 3]
                            ne = (kw + 127) // 128
                            for kk in range(ne):
                                kt = k0 // 128 + kk
                                nk = min(128, kw - kk * 128)
                                MM(P, ocp[0:n, 0:256], pt_[0:nk, kk, 0:n], ctok_[0:nk, kt, :], kt == 0, kt == nkt - 1, [pt_, ctok_], [ocp])
                        for step in range(nblk + 2):
                            if step < nblk:
                                A(step)
                            if 0 <= step - 1 < nblk:
                                B(step - 1)
                            if 0 <= step - 2 < nblk:
                                C(step - 2)
                            if extra:
                                extra.pop(0)()
                        while extra:
                            extra.pop(0)()
                        V(P, "tensor_reduce", [dn], [den], out=den[0:n, :], in_=dn[0:n, 0:nblk], axis=AX.X, op=ALU.add)
                        V(P, "reciprocal", [den], [den], out=den[0:n, :], in_=den[0:n, :])
                        V(P, "tensor_scalar", [ocp, den], [oc], out=oc[0:n, h, :], in0=ocp[0:n, 0:256], scalar1=den[0:n, 0:1], scalar2=None, op0=ALU.mult)
                    for f in pass1_thunks(0):
                        f()
                    for h in range(8):
                        pass2(h, pass1_thunks(h + 1) if h < 7 else [])
                    for half in range(2):
                        t_ = tp[half]
                        for kk in range(8):
                            c = half * 8 + kk
                            TP(P, t_[:, kk, 0:n], oc[0:n, c // 2, (c % 2) * 128:(c % 2) * 128 + 128], idb[0:n, 0:n], [oc, idb], [t_], inc=(kk == 7))
                        V(P, "tensor_copy", [t_], [ocT], out=ocT[:, half * 8:half * 8 + 8, 0:n], in_=t_[:, :, 0:n])
                    for h4 in range(2):
                        g_ = nxt()
                        for hh in range(4):
                            h = h4 * 4 + hh
                            for cc in range(2):
                                MM(P, g_[:, hh * 128:hh * 128 + n], wukv[:, cc, h * 256 + 128:h * 256 + 256], ocT[:, h * 2 + cc, 0:n], cc == 0, cc == 1, [wukv, ocT], [g_])
                        cast(oT[:, h4 * 4:h4 * 4 + 4, 0:n], g_[:, :].rearrange("p (h q) -> p h q", h=4)[:, :, 0:n], [g_], [oT])
                    for nb in range(2):
                        for h in range(8):
                            MM(P, mxp[nb][0:n, :], oT[:, h, 0:n], wo[:, h, nb * 512:(nb + 1) * 512], h == 0, h == 7, [oT, wo], [mxp[nb]])
                        V(P, "scalar_tensor_tensor", [x, mxp[nb]], [x], out=x[0:n, nb * 512:(nb + 1) * 512], in0=x[0:n, nb * 512:(nb + 1) * 512],
                          scalar=ALPHA, in1=mxp[nb][0:n, :], op0=ALU.mult, op1=ALU.add)
                    layer_norm(x, n, D, l1g, l1b, st6, mv, rstd)
                    dap, dbuf = src_tile(dst, ti)
                    DMA(P, "sp", dap, x[0:n, :], [x], [dbuf], owner=x)
                P.barrier()
                pass
                P.flush()

        phase_G(0, "in", "b")
        phase_P(0, "b", "a")
        phase_G(1, "a", "b")
        phase_P(1, "b", "a")
        phase_K("a")
        phase_A(0, 2, "a", "b")
        phase_P(2, "b", "a")
        phase_A(1, 3, "a", "b")
        phase_P(3, "b", "y")
        print("n_inst", P.n_inst)
    return nc


def _host_inputs(SEQ, PAST, inp):
    NT = SEQ // 256
    NPT = NT * 128
    f32 = np.float32
    xp = np.asarray(inp["x_prompt"], f32)
    xs = np.asarray(inp["x_sample"], f32)
    B = xp.shape[0]
    inv = (1.0 / (10000.0 ** (np.arange(0, 64, 2, dtype=np.float32) / 64.0))).astype(np.float32)
    shared = {}
    for k in ("ln1_g", "ln1_b", "ln2_g", "ln2_b", "gm_w_in", "gm_b_in", "gm_ln_g", "gm_ln_b", "gm_w_s", "gm_w_out",
              "mla_w_dkv", "mla_w_ukv", "mla_w_dq", "mla_q_norm_g", "mla_w_o", "peer_w_q", "peer_subkeys", "peer_u", "peer_v"):
        shared[k] = np.ascontiguousarray(np.asarray(inp[k], f32))
    shared["peer_u"] = shared["peer_u"].reshape(DEPTH * NEXP, D)
    shared["peer_v"] = shared["peer_v"].reshape(DEPTH * NEXP, D)
    shared["gm_b_sT"] = np.ascontiguousarray(np.transpose(np.asarray(inp["gm_b_s"], f32), (0, 2, 1)))
    shared["mla_kv_norm_g"] = np.asarray(inp["mla_kv_norm_g"], f32).reshape(1, -1)
    wuq = np.asarray(inp["mla_w_uq"], f32).reshape(2, Q_LORA, NH, 192)
    shared["w_uq_n"] = np.ascontiguousarray(wuq[..., :128].reshape(2, Q_LORA, 1024))
    shared["w_uq_p"] = np.ascontiguousarray(wuq[..., 128:].reshape(2, Q_LORA, 512))
    shared["w_uq_r"] = np.ascontiguousarray(np.concatenate([wuq[..., 160:192], wuq[..., 128:160]], -1).reshape(2, Q_LORA, 512))
    shared["ident"] = np.eye(128, dtype=f32)
    shared["iota16"] = np.tile(np.arange(16, dtype=f32)[None], (128, 1))
    maps = []
    for c in range(2 * B):
        b, r = c // 2, c % 2
        gt = [2 * j + r for j in range(NT)]
        xin = np.concatenate([xp[b].reshape(-1, 128, D)[gt].reshape(NPT, D), xs[c]], 0)
        pos = np.concatenate([(np.array(gt)[:, None] * 128 + np.arange(128)[None]).reshape(-1), PAST + np.arange(NS)]).astype(np.float32)
        ang = pos[:, None] * inv[None, :]
        cs, sn = np.cos(ang).astype(f32), np.sin(ang).astype(f32)
        qi = np.arange(128)[:, None]; kk = np.arange(256)[None, :]
        kchunk = kk // 64
        qchunk = (r * 128 + qi) // 64
        mask = np.where(kchunk <= qchunk, 0.0, -1e30).astype(f32)
        m = dict(shared)
        m.update(xin=np.ascontiguousarray(xin), cache_ckv=np.ascontiguousarray(np.asarray(inp["cache_ckv"], f32)[c]),
                 cache_kpe=np.ascontiguousarray(np.asarray(inp["cache_kpe"], f32)[c]),
                 rope_tok=np.ascontiguousarray(np.concatenate([cs, sn], 1)),
                 ropeT_cs=np.ascontiguousarray(np.concatenate([cs.T, cs.T], 0)),
                 ropeT_sn=np.ascontiguousarray(np.concatenate([-sn.T, sn.T], 0)),
                 maskb=mask)
        maps.append(m)
    return maps


def _assemble(SEQ, PAST, res, B):
    NT = SEQ // 256
    NPT = NT * 128
    f32 = np.float32
    y_p = np.zeros((B, SEQ, D), f32); ckv_p = np.zeros((B, SEQ, KV_LORA), f32); kpe_p = np.zeros((B, SEQ, QK_ROPE), f32)
    y_s = np.zeros((2 * B, NS, D), f32); gv = np.zeros((N_A, 2 * B, NS, GH), f32)
    ckv_s = np.zeros((2 * B, NS, KV_LORA), f32); kpe_s = np.zeros((2 * B, NS, QK_ROPE), f32)
    for c in range(2 * B):
        b, r = c // 2, c % 2
        o = res[c]
        for arr, key, w in ((y_p, "y_out", D), (ckv_p, "ckv_out", KV_LORA), (kpe_p, "kpe_out", QK_ROPE)):
            arr[b].reshape(SEQ // 128, 128, w)[r::2] = o[key][:NPT].reshape(NT, 128, w)
        y_s[c] = o["y_out"][NPT:]; ckv_s[c] = o["ckv_out"][NPT:]; kpe_s[c] = o["kpe_out"][NPT:]
        gv[:, c] = o["gv_out"]
    return (y_p, y_s, gv, ckv_p, kpe_p, ckv_s, kpe_s)


def run(SEQ, PAST, inp):
    nc = build(SEQ, PAST)
    maps = _host_inputs(SEQ, PAST, inp)
    res = run_bass_kernel_spmd(nc, maps, core_ids=list(range(8)))
    return _assemble(SEQ, PAST, res.results, np.asarray(inp["x_prompt"]).shape[0])


def kernel(**inputs):
    return run(8192, 2048, inputs)
```
